# Optimizing a Trainium2 kernel written in Bass

```python
import math
import jax, jax.numpy as jnp
from jax import lax
import numpy as np

D_MODEL = 2048
BATCH = 4
SEQ = 4096
DEPTH = 1
DEC_BATCH = 16
DEC_SEQ = 64
PAST_LEN = 1024

CHUNK = 64
N_HEADS_A = 8
D_QK = 64
D_V = 2 * D_QK
WIDTH_A = N_HEADS_A * D_V
WIDTH_B = D_MODEL // 2
CONV_W = 3
D_FF = 5632
FFN_CONV_W = 3
N_BUCKETS = 32
MAX_DIST = 128
Q_BLOCK = 128
EPS = 1e-6
NEG_INF = -1e30
SPLIT_SIZES = (N_HEADS_A * 2 * D_QK, N_HEADS_A * 2 * D_QK, WIDTH_A, WIDTH_B, WIDTH_B, WIDTH_B, D_MODEL, D_MODEL)
IN_WIDTH = sum(SPLIT_SIZES)

kernel_name = "streaming_diffattn_shortconv_convffn"


def rms_norm(x, g):
    xf = x.astype(jnp.float32)
    y = xf * lax.rsqrt(jnp.mean(xf * xf, axis=-1, keepdims=True) + EPS)
    return (y * g.astype(jnp.float32)).astype(x.dtype)


def lambda_init_for(layer_idx):
    return 0.8 - 0.6 * math.exp(-0.3 * layer_idx)


def rel_bucket(rel):
    nb = N_BUCKETS // 2
    max_exact = nb // 2
    ret = jnp.where(rel > 0, nb, 0)
    n = jnp.abs(rel)
    large = max_exact + (jnp.log(jnp.maximum(n, 1).astype(jnp.float32) / max_exact)
                         / math.log(MAX_DIST / max_exact) * (nb - max_exact)).astype(jnp.int32)
    large = jnp.minimum(large, nb - 1)
    return ret + jnp.where(n < max_exact, n, large)


def position_bias(rel_bias, q_pos, k_pos):
    b = rel_bucket(k_pos[None, :] - q_pos[:, None])
    return jnp.transpose(rel_bias[b], (2, 0, 1)).astype(jnp.float32)


def diff_attn_core(q, k, v, q_pos, k_pos, rel_bias, lam):
    scale = D_QK ** -0.5
    q1, q2 = jnp.split(q, 2, axis=-1)
    k1, k2 = jnp.split(k, 2, axis=-1)
    bias = position_bias(rel_bias, q_pos, k_pos)
    mask = (k_pos[None, :] // CHUNK) <= (q_pos[:, None] // CHUNK)

    def probs(qa, ka):
        s = jnp.einsum('bqhd,bkhd->bhqk', qa, ka, preferred_element_type=jnp.float32) * scale + bias
        s = jnp.where(mask, s, NEG_INF)
        return jax.nn.softmax(s, axis=-1)

    a = probs(q1, k1) - lam * probs(q2, k2)
    return jnp.einsum('bhqk,bkhd->bqhd', a.astype(v.dtype), v)


def diff_attn_prompt(q, k, v, rel_bias, lam):
    B, S, H, _ = q.shape
    nblk = S // Q_BLOCK
    k_pos = jnp.arange(S)
    qb = q.reshape(B, nblk, Q_BLOCK, H, q.shape[-1]).swapaxes(0, 1)
    starts = jnp.arange(nblk) * Q_BLOCK

    def one(args):
        qi, s0 = args
        return diff_attn_core(qi, k, v, s0 + jnp.arange(Q_BLOCK), k_pos, rel_bias, lam)

    o = lax.map(one, (qb, starts))
    return o.swapaxes(0, 1).reshape(B, S, H, v.shape[-1])


def causal_dwconv(x, prev, w):
    T = x.shape[1]
    W = w.shape[0]
    xp = jnp.concatenate([prev.astype(x.dtype), x], axis=1)
    y = xp[:, 0:T] * w[0]
    for j in range(1, W):
        y = y + xp[:, j:j + T] * w[j]
    return y, xp[:, T:]


def split_cols(proj):
    out, off = [], 0
    for s in SPLIT_SIZES:
        out.append(proj[..., off:off + s])
        off += s
    return out


def trunk_layer(x, k_past, v_past, conv_prev, ffn_prev, rel_bias, lambda_init,
                norm1_g, w_in, lambda_q1, lambda_k1, lambda_q2, lambda_k2, subln_g,
                conv_w, w_proj_a, w_proj_b, w_o, norm2_g, w_up, ffn_conv_w, w_down):
    B, T, _ = x.shape
    xn = rms_norm(x, norm1_g)
    q, k, v, b_gate, c_gate, x_in, g_a, g_b = split_cols(xn @ w_in)
    q = q.reshape(B, T, N_HEADS_A, 2 * D_QK)
    k = k.reshape(B, T, N_HEADS_A, 2 * D_QK)
    v = v.reshape(B, T, N_HEADS_A, D_V)
    lam = (jnp.exp(jnp.sum(lambda_q1.astype(jnp.float32) * lambda_k1.astype(jnp.float32)))
           - jnp.exp(jnp.sum(lambda_q2.astype(jnp.float32) * lambda_k2.astype(jnp.float32)))
           + lambda_init)
    if k_past is None:
        o = diff_attn_prompt(q, k, v, rel_bias, lam)
    else:
        P = k_past.shape[1]
        keys = jnp.concatenate([k_past.astype(k.dtype), k], axis=1)
        vals = jnp.concatenate([v_past.astype(v.dtype), v], axis=1)
        o = diff_attn_core(q, keys, vals, P + jnp.arange(T), jnp.arange(P + T), rel_bias, lam)
    o = rms_norm(o, subln_g) * (1.0 - lambda_init)
    o = o.reshape(B, T, WIDTH_A)
    u = c_gate * x_in
    if conv_prev is None:
        conv_prev = jnp.zeros((B, CONV_W - 1, WIDTH_B), u.dtype)
    z, conv_state = causal_dwconv(u, conv_prev, conv_w)
    ob = b_gate * z
    merged = jax.nn.sigmoid(g_a) * (o @ w_proj_a) + jax.nn.sigmoid(g_b) * (ob @ w_proj_b)
    h = x + merged @ w_o
    hn = rms_norm(h, norm2_g)
    up = hn @ w_up
    if ffn_prev is None:
        ffn_prev = jnp.zeros((B, FFN_CONV_W - 1, 2 * D_FF), up.dtype)
    upc, ffn_state = causal_dwconv(up, ffn_prev, ffn_conv_w)
    gate, val = jnp.split(upc, 2, axis=-1)
    h = h + (jax.nn.silu(gate) * val) @ w_down
    return h, k, v, conv_state, ffn_state


def setup_inputs(seed: int = 0) -> dict:
    key = jax.random.key(seed)
    ks = jax.random.split(key, 24)
    f32 = jnp.float32
    nrm = lambda k, shp, s: jax.random.normal(k, shp, f32) * s
    return {
        "x_prompt": nrm(ks[0], (BATCH, SEQ, D_MODEL), 1.0),
        "x_sample": nrm(ks[1], (DEC_BATCH, DEC_SEQ, D_MODEL), 1.0),
        "cache_k": nrm(ks[2], (DEPTH, DEC_BATCH, PAST_LEN, N_HEADS_A, 2 * D_QK), 1.0),
        "cache_v": nrm(ks[3], (DEPTH, DEC_BATCH, PAST_LEN, N_HEADS_A, D_V), 1.0),
        "state_conv_mix": nrm(ks[4], (DEPTH, DEC_BATCH, CONV_W - 1, WIDTH_B), 1.0),
        "state_conv_ffn": nrm(ks[5], (DEPTH, DEC_BATCH, FFN_CONV_W - 1, 2 * D_FF), 1.0),
        "rel_bias": nrm(ks[6], (N_BUCKETS, N_HEADS_A), 0.5),
        "norm1_g": 1.0 + nrm(ks[7], (DEPTH, D_MODEL), 0.02),
        "w_in": nrm(ks[8], (DEPTH, D_MODEL, IN_WIDTH), D_MODEL ** -0.5),
        "lambda_q1": nrm(ks[9], (DEPTH, D_QK), 0.1),
        "lambda_k1": nrm(ks[10], (DEPTH, D_QK), 0.1),
        "lambda_q2": nrm(ks[11], (DEPTH, D_QK), 0.1),
        "lambda_k2": nrm(ks[12], (DEPTH, D_QK), 0.1),
        "subln_g": 1.0 + nrm(ks[13], (DEPTH, D_V), 0.02),
        "conv_w": nrm(ks[14], (DEPTH, CONV_W, WIDTH_B), CONV_W ** -0.5),
        "w_proj_a": nrm(ks[15], (DEPTH, WIDTH_A, D_MODEL), WIDTH_A ** -0.5),
        "w_proj_b": nrm(ks[16], (DEPTH, WIDTH_B, D_MODEL), WIDTH_B ** -0.5),
        "w_o": nrm(ks[17], (DEPTH, D_MODEL, D_MODEL), D_MODEL ** -0.5),
        "norm2_g": 1.0 + nrm(ks[18], (DEPTH, D_MODEL), 0.02),
        "w_up": nrm(ks[19], (DEPTH, D_MODEL, 2 * D_FF), D_MODEL ** -0.5),
        "ffn_conv_w": nrm(ks[20], (DEPTH, FFN_CONV_W, 2 * D_FF), FFN_CONV_W ** -0.5),
        "w_down": nrm(ks[21], (DEPTH, D_FF, D_MODEL), D_FF ** -0.5),
        "final_g": 1.0 + nrm(ks[22], (D_MODEL,), 0.02),
    }


def reference(x_prompt, x_sample, cache_k, cache_v, state_conv_mix, state_conv_ffn, rel_bias,
              norm1_g, w_in, lambda_q1, lambda_k1, lambda_q2, lambda_k2, subln_g, conv_w,
              w_proj_a, w_proj_b, w_o, norm2_g, w_up, ffn_conv_w, w_down, final_g):
    hp, hs = x_prompt, x_sample
    kp_l, vp_l, cmp_l, cfp_l = [], [], [], []
    ks_l, vs_l, cms_l, cfs_l = [], [], [], []
    for d in range(DEPTH):
        lp = (norm1_g[d], w_in[d], lambda_q1[d], lambda_k1[d], lambda_q2[d], lambda_k2[d], subln_g[d],
              conv_w[d], w_proj_a[d], w_proj_b[d], w_o[d], norm2_g[d], w_up[d], ffn_conv_w[d], w_down[d])
        li = lambda_init_for(d)
        hp, kp, vp, cmp_, cfp = trunk_layer(hp, None, None, None, None, rel_bias, li, *lp)
        hs, ks_, vs_, cms, cfs = trunk_layer(hs, cache_k[d], cache_v[d], state_conv_mix[d],
                                             state_conv_ffn[d], rel_bias, li, *lp)
        kp_l.append(kp); vp_l.append(vp); cmp_l.append(cmp_); cfp_l.append(cfp)
        ks_l.append(ks_); vs_l.append(vs_); cms_l.append(cms); cfs_l.append(cfs)
    y_prompt = rms_norm(hp, final_g)
    y_sample = rms_norm(hs, final_g)
    return (y_prompt, y_sample,
            jnp.stack(kp_l), jnp.stack(vp_l), jnp.stack(cmp_l), jnp.stack(cfp_l),
            jnp.stack(ks_l), jnp.stack(vs_l), jnp.stack(cms_l), jnp.stack(cfs_l))
```

```python
import numpy as np
from contextlib import ExitStack
import concourse.bass as bass
import concourse.mybir as mybir
from concourse.bass_utils import run_bass_kernel_spmd

F32 = mybir.dt.float32
BF16 = mybir.dt.bfloat16
AF = mybir.ActivationFunctionType
ALU = mybir.AluOpType
AX = mybir.AxisListType

D = 2048
NH = 8
DFF = 5632
EPS = 1e-6
NCORES = 8
COMPUTE = ("pe", "act", "dve", "pool")
SAME_ENGINE_SYNC = True
SEM_MAX = 30000
NEGM = -30000.0
import os
KSTOP = os.environ.get("KSTOP", "")
SKIP = os.environ.get("SKIP", "")
CONVQ = os.environ.get("CONVQ", "pool")
NOCONV = os.environ.get("NOCONV", "") == "1"


class Res:
    __slots__ = ("name", "w", "r", "al", "chan", "excl")

    def __init__(self, name=""):
        self.name = name
        self.w = []
        self.r = {}
        self.al = []
        self.excl = False
        self.chan = None


class Chan:
    def __init__(self, prog, name):
        self.name = name
        self.idx = len(prog.chans)
        self.count = 0
        prog.chans.append(self)


def alias(ga, gb):
    for a in ga:
        for b in gb:
            if a is not b:
                a.al.append(b)
                b.al.append(a)


class Prog:
    def __init__(self):
        self.recs = {e: [] for e in ("pe", "act", "dve", "pool", "sp")}
        self.chans = []
        self.waited = {e: {} for e in self.recs}
        self.dry = False

    def chan_of(self, res):
        if res.chan is None:
            res.chan = Chan(self, res.name)
        return res.chan

    def emit(self, eng, fn, R=(), W=(), WM=(), chan=None):
        if self.dry:
            return None
        deps = []
        for r in R:
            deps.extend(r.w)
            if r.excl:
                deps.extend(ev for k_, ev in r.r.items() if k_ != eng)
        for w in W:
            deps.extend(w.w)
            deps.extend(w.r.values())
            for a in w.al:
                deps.extend(a.w)
                deps.extend(a.r.values())
        for w in WM:
            deps.extend(w.r.values())
        if chan is not None and chan.count > 0:
            deps.append(("d", chan.idx, chan.count))
        idx = len(self.recs[eng])
        best = {}
        for ev in deps:
            if ev[0] == "c":
                if ev[1] == eng and (eng == "pe" or not SAME_ENGINE_SYNC):
                    continue
                k = ("c", ev[1])
            else:
                k = ("d", ev[1])
            if ev[2] > best.get(k, -1):
                best[k] = ev[2]
        waits = []
        wd = self.waited[eng]
        for k, v in best.items():
            if wd.get(k, -1) >= v:
                continue
            wd[k] = v
            waits.append((k, v))
        if chan is not None:
            chan.count += 16
            ev = ("d", chan.idx, chan.count)
        else:
            ev = ("c", eng, idx)
        self.recs[eng].append((fn, waits, chan))
        key = ev[1] if ev[0] == "c" else ("d", ev[1])
        for r in R:
            old = r.r.get(key)
            if old is None or old[2] < ev[2]:
                r.r[key] = ev
        for w in W:
            w.w = [ev]
            w.r = {}
        for w in WM:
            w.w = w.w + [ev]
        return ev

    def finalize(self, nc, block, mksems):
        miles = {e: set() for e in COMPUTE}
        for e, recs in self.recs.items():
            for fn, waits, chan in recs:
                for k, v in waits:
                    if k[0] == "c":
                        miles[k[1]].add(v)
        rank, nsem = {}, {}
        for e in COMPUTE:
            s = sorted(miles[e])
            rank[e] = {idx: i for i, idx in enumerate(s)}
            nsem[e] = max(1, (len(s) + SEM_MAX - 1) // SEM_MAX)
        total = sum(nsem.values()) + len(self.chans)
        sems = mksems(total)
        pos = 0
        esem = {}
        for e in COMPUTE:
            esem[e] = sems[pos:pos + nsem[e]]
            pos += nsem[e]
        csem = sems[pos:pos + len(self.chans)]

        def run(e):
            def body(eng):
                myrank = rank.get(e, {})
                for idx, (fn, waits, chan) in enumerate(self.recs[e]):
                    for k, v in waits:
                        if k[0] == "c":
                            r = rank[k[1]][v]
                            eng.wait_ge(esem[k[1]][r // SEM_MAX], (r % SEM_MAX) + 1)
                        else:
                            eng.wait_ge(csem[k[1]], v)
                    ins = fn(eng)
                    if chan is not None:
                        ins.then_inc(csem[chan.idx], 16)
                    elif idx in myrank:
                        r = myrank[idx]
                        ins.then_inc(esem[e][r // SEM_MAX], 1)
                if e == "sp":
                    for c in self.chans:
                        if c.count > 0:
                            eng.wait_ge(csem[c.idx], c.count)
            return body

        block.sync(run("sp"))
        block.tensor(run("pe"))
        block.scalar(run("act"))
        block.vector(run("dve"))
        block.gpsimd(run("pool"))
        return ({e: len(r) for e, r in self.recs.items()}, {e: len(miles[e]) for e in COMPUTE},
                total)


class Buf:
    def __init__(self, ap, name):
        self.ap = ap
        self.res = Res(name)

    def __getitem__(self, k):
        return self.ap[k]


WSPECS = {
    "w_in": (D, 10240), "w_proj_a": (1024, D), "w_proj_b": (1024, D),
    "w_o": (D, D), "w_up": (D, 2 * DFF), "w_down": (DFF, D),
}
SL_Q, SL_K, SL_V, SL_B, SL_C, SL_X, SL_GA, SL_GB = (0, 1), (2, 3), (4, 5), (6, 7), (8, 9), (10, 11), (12, 13, 14, 15), (16, 17, 18, 19)


class Group:
    def __init__(self, name, N, pts, nseg, L, c0=0, u0=0, scol=0):
        self.name, self.N, self.pts, self.nseg, self.L = name, N, pts, nseg, L
        self.nt = len(pts)
        self.c0, self.u0, self.scol = c0, u0, scol
        self.xs = self.xn = None


def build_program():
    nc = bass.Bass("TRN2", target_bir_lowering=False)
    es = ExitStack()
    P = Prog()

    def din(name, shape, dt=F32):
        return nc.dram_tensor(name, list(shape), dt, kind="ExternalInput")

    def dout(name, shape, dt=F32):
        return nc.dram_tensor(name, list(shape), dt, kind="ExternalOutput")

    def dscr(name, shape, dt=BF16):
        return nc.dram_tensor(name, list(shape), dt, kind="Internal")

    x_own = din("x_own", [2048, D])
    x_pre = din("x_pre", [2048, D])
    x_smp = din("x_smp", [128, D])
    x_halo = din("x_halo", [4, D])
    premask_d = din("premask", [128, 1])
    cache_k = din("cache_k", [2, 1024, 1024])
    cache_v = din("cache_v", [2, 1024, 1024])
    st_mix = din("st_mix", [2, 2, 1024])
    st_ffn = din("st_ffn", [2, 2, 2 * DFF])
    bd_d = din("bias_diag", [NH, 128, 128])
    bp_d = din("bias_prev", [NH, 128, 128])
    mk_d = din("mask_diag", [128, 128])
    relb_d = din("rel_bias", [32, NH])
    wd = {n: din(n, s) for n, s in WSPECS.items()}
    g1_d, g2_d, gf_d = din("norm1_g", [D]), din("norm2_g", [D]), din("final_g", [D])
    lam_d = [din(n, [64]) for n in ("lambda_q1", "lambda_k1", "lambda_q2", "lambda_k2")]
    subg_d = din("subln_g", [128])
    cw_d = din("conv_w", [3, 1024])
    fw_d = din("ffn_conv_w", [3, 2 * DFF])

    y_own = dout("y_own", [2048, D])
    y_smp = dout("y_smp", [128, D])
    k_own = dout("k_own", [2048, 1024])
    v_own = dout("v_own", [2048, 1024])
    cm_own = dout("cm_own", [1, 2, 1024])
    cf_own = dout("cf_own", [1, 2, 2 * DFF])
    k_smp = dout("k_smp", [128, 1024])
    v_smp = dout("v_smp", [128, 1024])
    cm_smp = dout("cm_smp", [2, 2, 1024])
    cf_smp = dout("cf_smp", [2, 2, 2 * DFF])

    wb = {n: dscr(n + "_bf", [N // 512, 128, K // 128, 512]) for n, (K, N) in WSPECS.items()}
    wb_res = {n: [[Res(f"{n}_s{s}_k{kc}") for kc in range(K // 128)] for s in range(N // 512)]
              for n, (K, N) in WSPECS.items()}
    kTs = dscr("kTs", [NH, 128, 4096])
    Vs = dscr("Vs", [4096, 1024])
    kTs_res = [Res(f"kTs_b{i}") for i in range(8)]
    Vs_res = [Res(f"Vs_b{i}") for i in range(8)]
    kTq = [dscr(f"kTq{s}", [NH, 128, 1152]) for s in range(2)]
    Vq = [dscr(f"Vq{s}", [1152, 1024]) for s in range(2)]
    kTq_res = [[Res(f"kTq{s}_{i}") for i in range(3)] for s in range(2)]
    Vq_res = [[Res(f"Vq{s}_{i}") for i in range(3)] for s in range(2)]

    def region(name, kb):
        return es.enter_context(nc.sbuf_tensor(name, [128, int(kb * 512)], BF16))

    class Carver:
        def __init__(self, reg, name):
            self.reg, self.name, self.off = reg, name, 0
            self.bufs = []

        def reset(self):
            self.off = 0

        def take(self, name, free_shape, dt):
            esz = 4 if dt == F32 else 2
            n = int(np.prod(free_shape))
            self.off = (self.off + 7) // 8 * 8
            if dt == F32:
                v = self.reg.bitcast(F32)[:, self.off // 4: self.off // 4 + n]
            else:
                v = self.reg[:, self.off // 2: self.off // 2 + n]
            self.off += n * esz
            assert self.off <= self.reg.shape[1] * 2, (self.name, name, self.off)
            if len(free_shape) == 2:
                v = v.rearrange("p (a b) -> p a b", a=free_shape[0])
            elif len(free_shape) == 3:
                v = v.rearrange("p (a b c) -> p a b c", a=free_shape[0], b=free_shape[1])
            b = Buf(v, name)
            self.bufs.append(b)
            return b

    RC = Carver(region("RC", 21), "RC")
    RX = Carver(region("RX", 32.5), "RX")
    RW = Carver(region("RW", 48), "RW")
    RN = Carver(region("RN", 16.25), "RN")
    RA = Carver(region("RA", 44.5), "RA")
    RM = Carver(region("RM", 20), "RM")

    pp = [es.enter_context(nc.psum_tensor(f"pp{i}", [128, 1024], F32)) for i in range(4)]
    psum = [Buf(pp[i // 2][:, (i % 2) * 512:(i % 2 + 1) * 512], f"ps{i}") for i in range(8)]
    psum_b = [pp[i // 2].bitcast(BF16)[:, (i % 2) * 1024:(i % 2 + 1) * 1024] for i in range(8)]
    for p in psum:
        p.res.excl = True
    bank_ctr = [0]

    def next_bank():
        b = bank_ctr[0] % 8
        bank_ctr[0] += 1
        return b

    def rs(xs):
        out = []
        for x in xs:
            if x is None:
                continue
            out.append(x.res if isinstance(x, Buf) else x)
        return out

    def E(eng, fn, R=(), W=(), WM=()):
        return P.emit(eng, fn, rs(R), rs(W), rs(WM))

    def dma_load(dst, dst_ap, src_ap, R=(), q="sp", **kw):
        if kw.get("allow_slow_non_contiguous") and "slow" in SKIP:
            return E("pool", lambda e: e.memset(dst_ap, 0.5), W=[dst])
        ch = P.chan_of(dst.res)
        return P.emit(q, lambda e: e.dma_start(out=dst_ap, in_=src_ap, **kw), rs(R), [dst.res], (), ch)

    def dma_store(src, dst_ap, src_ap, W=(), WM=(), R=(), q="sp", **kw):
        if kw.get("allow_slow_non_contiguous") and "slow" in SKIP:
            return None
        ch = P.chan_of(src.res)
        return P.emit(q, lambda e: e.dma_start(out=dst_ap, in_=src_ap, **kw), rs([src] + list(R)), rs(W), rs(WM), ch)

    cast_rr = [0]

    def cast_copy(out_ap, in_ap, R, W, engs=("act", "dve")):
        eng = engs[cast_rr[0] % len(engs)]
        cast_rr[0] += 1
        if eng == "act":
            E("act", lambda e: e.copy(out=out_ap, in_=in_ap), R, W)
        elif eng == "dve":
            E("dve", lambda e: e.tensor_copy(out=out_ap, in_=in_ap), R, W)
        else:
            E("pool", lambda e: e.tensor_copy(out=out_ap, in_=in_ap), R, W)

    ident = RC.take("ident", [128], BF16)
    ones = RC.take("ones", [128], BF16)
    iot = RC.take("iota", [128], F32)
    g1c = RC.take("g1c", [16], F32)
    g2c = RC.take("g2c", [16], F32)
    gfb = RC.take("gfb", [D], F32)
    cwc = RC.take("cwc", [3, 8], F32)
    fwc = RC.take("fwc", [3, 88], F32)
    identf = RC.take("identf", [128], F32)
    cst = RC.take("cst", [128], F32)
    cst2 = RC.take("cst2", [128], F32)
    chv = RC.take("chv", [NH], F32)
    nchv = RC.take("nchv", [NH], F32)
    cpre = RC.take("cpre", [NH], F32)
    pmk = RC.take("pmk", [1], F32)
    onespre = RC.take("onespre", [128], BF16)
    selA = RC.take("selA", [128], F32)
    selB = RC.take("selB", [128], F32)
    lamv = RC.take("lamv", [4, 64], F32)
    lamt = RC.take("lamt", [8], F32)
    nlam = RC.take("nlam", [1], F32)
    gsub = RC.take("gsub", [1], F32)
    EBd = RC.take("EBd", [NH, 128], BF16)
    EBp = RC.take("EBp", [NH, 128], BF16)
    uph = RC.take("uph", [88, 2], F32)
    uphq = RC.take("uphq", [88, 2, 2], F32)
    uhalo = RC.take("uhalo", [8, 2], F32)
    uhaloq = RC.take("uhaloq", [8, 2, 2], F32)
    ssq = RC.take("ssq", [10], F32)
    rstd = RC.take("rstd", [10], F32)

    def load_cols(dst, dst_ap, src_rows, n):
        dma_load(cst, cst.ap[0:n, :], src_rows)
        bk = next_bank()
        E("pe", lambda e: e.transpose(psum[bk].ap[:, 0:n], cst.ap[0:n, :], identf.ap[0:n, 0:n]), R=[cst, identf], W=[psum[bk]])
        E("dve", lambda e: e.tensor_copy(out=dst_ap, in_=psum[bk].ap[:, 0:n]), R=[psum[bk]], W=[dst])

    def store_cols(src, src_ap, dst_rows, n):
        E("dve", lambda e: e.tensor_copy(out=cst2.ap[:, 0:n], in_=src_ap), R=[src], W=[cst2])
        bk = next_bank()
        E("pe", lambda e: e.transpose(psum[bk].ap[0:n, 0:128], cst2.ap[:, 0:n], identf.ap), R=[cst2, identf], W=[psum[bk]])
        E("act", lambda e: e.copy(out=cst.ap[0:n, :], in_=psum[bk].ap[0:n, 0:128]), R=[psum[bk]], W=[cst])
        dma_store(cst, dst_rows, cst.ap[0:n, :])

    def setup_consts():
        E("pool", lambda e: e.iota(iot.ap, pattern=[[1, 128]], base=0, channel_multiplier=-1,
                                   allow_small_or_imprecise_dtypes=True), W=[iot])
        E("dve", lambda e: e.tensor_scalar(out=ident.ap, in0=iot.ap, scalar1=0.0, scalar2=None, op0=ALU.is_equal),
          R=[iot], W=[ident])
        E("dve", lambda e: e.tensor_scalar(out=identf.ap, in0=iot.ap, scalar1=0.0, scalar2=None, op0=ALU.is_equal),
          R=[iot], W=[identf])
        E("pool", lambda e: e.memset(ones.ap, 1.0), W=[ones])
        E("pool", lambda e: e.memset(selA.ap, 0.0), W=[selA])
        E("pool", lambda e: e.memset(selA.ap[0:1, :], 1.0), W=[selA])
        E("pool", lambda e: e.memset(selB.ap, 0.0), W=[selB])
        E("pool", lambda e: e.memset(selB.ap[32:33, :], 1.0), W=[selB])
        E("pool", lambda e: e.memset(uph.ap, 0.0), W=[uph])
        E("pool", lambda e: e.memset(uhalo.ap, 0.0), W=[uhalo])
        load_cols(g1c, g1c.ap, g1_d.ap().rearrange("(kc p) -> kc p", p=128), 16)
        load_cols(g2c, g2c.ap, g2_d.ap().rearrange("(kc p) -> kc p", p=128), 16)
        dma_load(gfb, gfb.ap, gf_d.ap().partition_broadcast(128))
        for jj in range(3):
            load_cols(cwc, cwc.ap[:, jj, :], cw_d.ap()[jj, :].rearrange("(c p) -> c p", p=128), 8)
            load_cols(fwc, fwc.ap[:, jj, :], fw_d.ap()[jj, :].rearrange("(c p) -> c p", p=128), 88)
        dma_load(chv, chv.ap, relb_d.ap()[15, :].partition_broadcast(128))
        dma_load(pmk, pmk.ap, premask_d.ap())
        for i in range(4):
            dma_load(lamv, lamv.ap[:, i, :], lam_d[i].ap().partition_broadcast(128))
        dma_load(gsub, gsub.ap, subg_d.ap().rearrange("(p o) -> p o", o=1))
        for s in range(2):
            for jj in range(2):
                load_cols(uhaloq, uhaloq.ap[:, :, s, jj], st_mix.ap()[s, jj, :].rearrange("(c p) -> c p", p=128), 8)
                load_cols(uphq, uphq.ap[:, :, s, jj], st_ffn.ap()[s, jj, :].rearrange("(c p) -> c p", p=128), 88)
        E("dve", lambda e: e.tensor_scalar(out=nchv.ap, in0=chv.ap, scalar1=-1.0, scalar2=None, op0=ALU.mult),
          R=[chv], W=[nchv])
        E("dve", lambda e: e.tensor_scalar(out=onespre.ap, in0=ones.ap, scalar1=pmk.ap[:, 0:1], scalar2=None, op0=ALU.mult),
          R=[ones, pmk], W=[onespre])
        E("dve", lambda e: e.tensor_scalar(out=gsub.ap, in0=gsub.ap, scalar1=0.8, scalar2=None, op0=ALU.mult),
          R=[gsub], W=[gsub])
        E("dve", lambda e: e.tensor_tensor(out=lamv.ap[:, 0, :], in0=lamv.ap[:, 0, :], in1=lamv.ap[:, 1, :], op=ALU.mult),
          R=[lamv], W=[lamv])
        E("dve", lambda e: e.tensor_tensor(out=lamv.ap[:, 2, :], in0=lamv.ap[:, 2, :], in1=lamv.ap[:, 3, :], op=ALU.mult),
          R=[lamv], W=[lamv])
        E("dve", lambda e: e.tensor_reduce(out=lamt.ap[:, 0:1], in_=lamv.ap[:, 0, :], axis=AX.X, op=ALU.add), R=[lamv], W=[lamt])
        E("dve", lambda e: e.tensor_reduce(out=lamt.ap[:, 1:2], in_=lamv.ap[:, 2, :], axis=AX.X, op=ALU.add), R=[lamv], W=[lamt])
        E("act", lambda e: e.activation(out=lamt.ap[:, 2:4], in_=lamt.ap[:, 0:2], func=AF.Exp), R=[lamt], W=[lamt])
        E("dve", lambda e: e.tensor_tensor(out=nlam.ap, in0=lamt.ap[:, 3:4], in1=lamt.ap[:, 2:3], op=ALU.subtract), R=[lamt], W=[nlam])
        E("dve", lambda e: e.tensor_scalar(out=nlam.ap, in0=nlam.ap, scalar1=-0.2, scalar2=None, op0=ALU.add), R=[nlam], W=[nlam])
        RA.reset()
        bst = RA.take("bst", [NH, 128], F32)
        bst2 = RA.take("bst2", [NH, 128], F32)
        mst = RA.take("mst", [128], F32)
        dma_load(bst, bst.ap, bd_d.ap().rearrange("h k q -> k h q"))
        dma_load(bst2, bst2.ap, bp_d.ap().rearrange("h k q -> k h q"))
        dma_load(mst, mst.ap, mk_d.ap())
        for h in range(NH):
            E("act", lambda e, h=h: e.activation(out=bst.ap[:, h, :], in_=bst.ap[:, h, :], func=AF.Exp,
                                                 bias=nchv.ap[:, h:h + 1], scale=1.0), R=[bst, nchv], W=[bst])
            E("dve", lambda e, h=h: e.tensor_tensor(out=EBd.ap[:, h, :], in0=bst.ap[:, h, :], in1=mst.ap, op=ALU.mult),
              R=[bst, mst], W=[EBd])
            E("act", lambda e, h=h: e.activation(out=EBp.ap[:, h, :], in_=bst2.ap[:, h, :], func=AF.Exp,
                                                 bias=nchv.ap[:, h:h + 1], scale=1.0), R=[bst2, nchv], W=[EBp])
        return [bst, bst2, mst]

    RV = Carver(region("RV", 24), "RV")

    class Conv:
        CW = 1024

        def __init__(self):
            self.a = [RV.take(f"cv32_{i}", [self.CW], F32) for i in range(4)]
            self.b = [RV.take(f"cv16_{i}", [self.CW], BF16) for i in range(4)]
            units = [("w_in", 1024), ("w_in", 2048), ("w_in", 3072), ("w_in", 4096), ("w_in", 5120), ("w_in", 0),
                     ("w_in", 6144), ("w_proj_a", 0), ("w_in", 7168), ("w_proj_a", 1024),
                     ("w_in", 8192), ("w_proj_b", 0), ("w_in", 9216), ("w_proj_b", 1024),
                     ("w_o", 0), ("w_o", 1024)] + [("w_up", c) for c in range(0, 2 * DFF, 1024)] + [("w_down", 0), ("w_down", 1024)]
            self.steps = []
            self.need = {}
            for name, c0 in units:
                K, N = WSPECS[name]
                for kc in range(K // 128):
                    self.steps.append((name, kc, c0))
                for s_ in (c0 // 512, c0 // 512 + 1):
                    self.need[(name, s_)] = len(self.steps)
            self.nl = 0
            self.nd = 0

        def _load(self, i):
            name, kc, c0 = self.steps[i]
            a_ = self.a[i % 4]
            dma_load(a_, a_.ap, wd[name].ap()[kc * 128:(kc + 1) * 128, c0:c0 + self.CW], q=CONVQ)

        def ensure(self, n):
            n = min(n, len(self.steps))
            while self.nd < n:
                while self.nl < min(len(self.steps), self.nd + 3):
                    self._load(self.nl)
                    self.nl += 1
                i = self.nd
                name, kc, c0 = self.steps[i]
                a_, b_ = self.a[i % 4], self.b[i % 4]
                cast_copy(b_.ap, a_.ap, [a_], [b_], engs=("act", "dve"))
                s0 = c0 // 512
                dma_store(b_, wb[name].ap()[s0:s0 + 2, :, kc, :].rearrange("s p c -> p s c"),
                          b_.ap.rearrange("p (s c) -> p s c", c=512), W=[wb_res[name][s_][kc] for s_ in (s0, s0 + 1)], q=CONVQ)
                self.nd += 1

        def advance(self, k):
            if not P.dry:
                self.ensure(self.nd + k)

    class Slabs:
        def __init__(self, sched):
            self.sched = sched
            self.rec = []
            self.i = 0
            self.loaded = 0
            RW.reset()
            self.slots = [RW.take(f"slab{i}", [16, 512], BF16) for i in range(3)]

        def _load(self, j):
            name, s, k0, nk = self.sched[j]
            CONV.ensure(CONV.need[(name, s)])
            CONV.advance(3)
            slot = self.slots[j % 3]
            dma_load(slot, slot.ap[:, 0:nk, :], wb[name].ap()[s, :, k0:k0 + nk, :],
                     R=[wb_res[name][s][k] for k in range(k0, k0 + nk)])

        def get(self, name, s, k0, nk):
            d = (name, s, k0, nk)
            if self.sched is None:
                self.rec.append(d)
                return self.slots[0]
            assert self.sched[self.i] == d, (self.i, self.sched[self.i], d)
            while self.loaded < min(len(self.sched), self.i + 3):
                self._load(self.loaded)
                self.loaded += 1
            slot = self.slots[self.i % 3]
            self.i += 1
            return slot

    FW = 516

    def alloc_block_bufs():
        B = {}
        RX.reset()
        B["xs"] = [RX.take(f"xs{t}", [D], F32) for t in range(4)]
        RX.reset()
        B["bT"] = [RX.take(f"bT{j}", [FW], BF16) for j in range(8)]
        B["cT"] = [RX.take(f"cT{j}", [FW], BF16) for j in range(8)]
        B["uT"] = [RX.take(f"uT{j}", [520], BF16) for j in range(8)]
        B["obT"] = [RX.take(f"obT{j}", [FW], BF16) for j in range(8)]
        alias([b.res for b in B["xs"]], [b.res for k in ("bT", "cT", "uT", "obT") for b in B[k]])
        RN.reset()
        B["nT"] = [RN.take(f"nT{kc}", [FW], BF16) for kc in range(16)]
        RA.reset()
        B["xn"] = [RA.take(f"xn{t}", [D], BF16) for t in range(4)]
        B["junk"] = RA.take("junk", [D], BF16)
        B["xnh"] = RA.take("xnh", [D], BF16)
        roleA1 = [b.res for b in B["xn"]] + [B["junk"].res, B["xnh"].res]
        RA.reset()
        B["qT"] = [RA.take(f"qT{h}", [FW], BF16) for h in range(NH)]
        B["mergedT"] = [RA.take(f"mg{m}", [FW], BF16) for m in range(16)]
        off_m = RA.off
        RA.off = 8 * FW * 2
        B["kTc"] = [RA.take(f"kTc{i}", [1024], BF16) for i in range(3)]
        B["Vc"] = [RA.take(f"Vc{i}", [8, 128], BF16) for i in range(3)]
        B["PP"] = [RA.take(f"PP{i}", [2, 512], BF16) for i in range(2)]
        B["r"] = [RA.take(f"r{m}", [512], F32) for m in range(2)]
        B["o"] = [RA.take(f"o{m}", [512], F32) for m in range(2)]
        B["sq"] = RA.take("sq", [512], BF16)
        B["rz"] = RA.take("rz", [512], F32)
        att = [B["rz"].res] + [b.res for k in ("kTc", "Vc", "r", "o") for b in B[k]] + [b.res for b in B["PP"]] + [B["sq"].res]
        roleA2 = [b.res for b in B["qT"]] + [b.res for b in B["mergedT"]] + att
        alias(att, [b.res for b in B["mergedT"]])
        RA.reset()
        B["actT"] = [RA.take(f"act{c}", [FW], BF16) for c in range(44)]
        roleA3 = [b.res for b in B["actT"]]
        alias(roleA1, roleA2)
        alias(roleA1, roleA3)
        alias(roleA2, roleA3)
        RM.reset()
        B["kf"] = [RM.take(f"kf{i}", [512], F32) for i in range(2)]
        B["kb"] = [RM.take(f"kb{i}", [512], BF16) for i in range(2)]
        B["kTst"] = [RM.take(f"kTst{i}", [4, 128], BF16) for i in range(2)]
        B["ztmp"] = [RM.take(f"z{i}", [512], F32) for i in range(2)]
        B["ust"] = RM.take("ust", [8, 2, 2], F32)
        roleM1 = [b.res for k in ("kf", "kb", "kTst", "ztmp") for b in B[k]] + [B["ust"].res]
        RM.reset()
        B["oT"] = [RM.take(f"oT{h}", [512], BF16) for h in range(NH)]
        B["sg"] = [RM.take(f"sg{i}", [512], BF16) for i in range(2)]
        B["mtmp"] = [RM.take(f"mtmp{i}", [512], BF16) for i in range(2)]
        roleM2 = [b.res for k in ("oT", "sg", "mtmp") for b in B[k]]
        RM.reset()
        B["ctmp"] = [RM.take(f"ct{i}", [512], F32) for i in range(4)]
        roleM3 = [b.res for b in B["ctmp"]]
        alias(roleM1, roleM2)
        alias(roleM1, roleM3)
        alias(roleM2, roleM3)
        return B

    def rms_stats(g, src, t, col):
        pt = g.pts[t]
        E("act", lambda e: e.activation(out=Bk["junk"].ap[0:pt, :], in_=src.ap[0:pt, :], func=AF.Square,
                                        accum_out=ssq.ap[0:pt, col:col + 1]), R=[src], W=[Bk["junk"], ssq])
        E("dve", lambda e: e.tensor_scalar(out=rstd.ap[0:pt, col:col + 1], in0=ssq.ap[0:pt, col:col + 1],
                                           scalar1=1.0 / D, scalar2=EPS, op0=ALU.mult, op1=ALU.add), R=[ssq], W=[rstd])
        E("act", lambda e: e.activation(out=rstd.ap[0:pt, col:col + 1], in_=rstd.ap[0:pt, col:col + 1], func=AF.Sqrt),
          R=[rstd], W=[rstd])
        E("dve", lambda e: e.reciprocal(out=rstd.ap[0:pt, col:col + 1], in_=rstd.ap[0:pt, col:col + 1]), R=[rstd], W=[rstd])

    def norm_transpose(g, gcol):
        c0, N = g.c0, g.N
        for t in range(g.nt):
            pt = g.pts[t]
            xs, xn = g.xs[t], g.xn[t]
            rms_stats(g, xs, t, g.scol + t)
            E("dve", lambda e, xs=xs, xn=xn, t=t, pt=pt: e.tensor_scalar(out=xn.ap[0:pt, :], in0=xs.ap[0:pt, :],
                                                                        scalar1=rstd.ap[0:pt, g.scol + t:g.scol + t + 1], scalar2=None, op0=ALU.mult),
              R=[xs, rstd], W=[xn])
        for kc in range(16):
            bk = next_bank()
            for t in range(g.nt):
                pt = g.pts[t]
                xn = g.xn[t]
                E("pe", lambda e, bk=bk, t=t, pt=pt, kc=kc, xn=xn: e.transpose(psum_b[bk][:, t * 128:t * 128 + pt],
                                                                               xn.ap[0:pt, kc * 128:(kc + 1) * 128],
                                                                               ident.ap[0:pt, 0:pt]),
                  R=[xn, ident], W=[psum[bk]])
            if kc % 2 == 0:
                E("dve", lambda e, bk=bk, kc=kc: e.tensor_scalar(out=Bk["nT"][kc].ap[:, c0:c0 + N], in0=psum_b[bk][:, 0:N],
                                                                 scalar1=gcol.ap[:, kc:kc + 1], scalar2=None, op0=ALU.mult),
                  R=[psum[bk], gcol], W=[Bk["nT"][kc]])
            else:
                E("act", lambda e, bk=bk, kc=kc: e.activation(out=Bk["nT"][kc].ap[:, c0:c0 + N], in_=psum_b[bk][:, 0:N],
                                                              func=AF.Copy, scale=gcol.ap[:, kc:kc + 1]),
                  R=[psum[bk], gcol], W=[Bk["nT"][kc]])

    def formA(g, slot, nk, j, rhs):
        bk = next_bank()
        c0, N = g.c0, g.N
        for kc in range(nk):
            E("pe", lambda e, bk=bk, kc=kc: e.matmul(psum[bk].ap[:, 0:N], lhsT=slot.ap[:, kc, j * 128:(j + 1) * 128],
                                                     rhs=rhs[kc].ap[:, c0:c0 + N], start=(kc == 0), stop=(kc == nk - 1)),
              R=[slot, rhs[kc]], W=[psum[bk]])
        return bk

    def formB(g, slot, k0, nk, lhs, t, bk, first, last):
        pt = g.pts[t]
        c0 = g.c0
        for kc in range(nk):
            E("pe", lambda e, kc=kc: e.matmul(psum[bk].ap[0:pt, :], lhsT=lhs[k0 + kc].ap[:, c0 + t * 128:c0 + t * 128 + pt],
                                              rhs=slot.ap[:, kc, :], start=(first and kc == 0), stop=(last and kc == nk - 1)),
              R=[slot, lhs[k0 + kc]], W=[psum[bk]])

    def inproj_conv(gs, halos):
        for si, s in enumerate(SL_B):
            slot = SLB.get("w_in", s, 0, 16)
            for j in range(4):
                c = si * 4 + j
                for g in gs:
                    bk = formA(g, slot, 16, j, Bk["nT"])
                    E("act", lambda e, bk=bk, c=c, g=g: e.copy(out=Bk["bT"][c].ap[:, g.c0:g.c0 + g.N], in_=psum[bk].ap[:, 0:g.N]),
                      R=[psum[bk]], W=[Bk["bT"][c]])
        for si, s in enumerate(SL_C):
            slot = SLB.get("w_in", s, 0, 16)
            for j in range(4):
                c = si * 4 + j
                for g in gs:
                    bk = formA(g, slot, 16, j, Bk["nT"])
                    E("act", lambda e, bk=bk, c=c, g=g: e.copy(out=Bk["cT"][c].ap[:, g.c0:g.c0 + g.N], in_=psum[bk].ap[:, 0:g.N]),
                      R=[psum[bk]], W=[Bk["cT"][c]])
        for si, s in enumerate(SL_X):
            slot = SLB.get("w_in", s, 0, 16)
            for j in range(4):
                c = si * 4 + j
                for g, u_halo_src in zip(gs, halos):
                    nseg, L, N, c0 = g.nseg, g.L, g.N, g.c0
                    W2 = L + 2
                    bk = formA(g, slot, 16, j, Bk["nT"])
                    u3 = Bk["uT"][c].ap[:, g.u0:g.u0 + nseg * W2].rearrange("p (s w) -> p s w", w=W2)
                    ps3 = psum[bk].ap[:, 0:N].rearrange("p (s w) -> p s w", w=L)
                    c3 = Bk["cT"][c].ap[:, c0:c0 + N].rearrange("p (s w) -> p s w", w=L)
                    hsrc = u_halo_src.ap[:, c] if nseg > 1 else u_halo_src.ap[:, c, :].rearrange("p (s w) -> p s w", s=1)
                    E("act", lambda e, u3=u3, hsrc=hsrc: e.copy(out=u3[:, :, 0:2], in_=hsrc), R=[u_halo_src], W=[Bk["uT"][c]])
                    E("dve", lambda e, u3=u3, ps3=ps3, c3=c3, W2=W2: e.tensor_tensor(out=u3[:, :, 2:W2], in0=ps3, in1=c3, op=ALU.mult),
                      R=[psum[bk], Bk["cT"][c]], W=[Bk["uT"][c]])
                    E("act", lambda e, u3=u3, hsrc=hsrc, L=L, W2=W2: e.copy(out=hsrc, in_=u3[:, :, L:W2]), R=[Bk["uT"][c]], W=[u_halo_src])
                    z = Bk["ztmp"][c % 2]
                    z3 = z.ap[:, 0:N].rearrange("p (s w) -> p s w", w=L)
                    E("dve", lambda e, z3=z3, u3=u3, c=c, L=L: e.tensor_scalar(out=z3, in0=u3[:, :, 0:L], scalar1=cwc.ap[:, 0, c:c + 1],
                                                                             scalar2=None, op0=ALU.mult), R=[Bk["uT"][c], cwc], W=[z])
                    E("dve", lambda e, z3=z3, u3=u3, c=c, L=L: e.scalar_tensor_tensor(out=z3, in0=u3[:, :, 1:L + 1], scalar=cwc.ap[:, 1, c:c + 1],
                                                                                    in1=z3, op0=ALU.mult, op1=ALU.add),
                      R=[Bk["uT"][c], cwc, z], W=[z])
                    E("dve", lambda e, z3=z3, u3=u3, c=c, L=L: e.scalar_tensor_tensor(out=z3, in0=u3[:, :, 2:L + 2], scalar=cwc.ap[:, 2, c:c + 1],
                                                                                    in1=z3, op0=ALU.mult, op1=ALU.add),
                      R=[Bk["uT"][c], cwc, z], W=[z])
                    E("pool", lambda e, z=z, c=c, N=N, c0=c0: e.tensor_tensor(out=Bk["obT"][c].ap[:, c0:c0 + N], in0=z.ap[:, 0:N],
                                                                            in1=Bk["bT"][c].ap[:, c0:c0 + N], op=ALU.mult),
                      R=[z, Bk["bT"][c]], W=[Bk["obT"][c]])

    def inproj_q(gs):
        for si, s in enumerate(SL_Q):
            slot = SLB.get("w_in", s, 0, 16)
            for j in range(4):
                h = si * 4 + j
                for g in gs:
                    bk = formA(g, slot, 16, j, Bk["nT"])
                    E("act", lambda e, bk=bk, h=h, g=g: e.copy(out=Bk["qT"][h].ap[:, g.c0:g.c0 + g.N], in_=psum[bk].ap[:, 0:g.N]),
                      R=[psum[bk]], W=[Bk["qT"][h]])

    def inproj_kv(g, kout, vout, row0, kdst, vdst):
        for si, s in enumerate(SL_K):
            slot = SLB.get("w_in", s, 0, 16)
            for t in range(g.nt):
                pt = g.pts[t]
                bk = next_bank()
                formB(g, slot, 0, 16, Bk["nT"], t, bk, True, True)
                i = (si * g.nt + t) % 2
                kf, kb, kst = Bk["kf"][i], Bk["kb"][i], Bk["kTst"][i]
                if kout is not None:
                    E("act", lambda e, bk=bk, kf=kf, pt=pt: e.copy(out=kf.ap[0:pt, :], in_=psum[bk].ap[0:pt, :]), R=[psum[bk]], W=[kf])
                    dma_store(kf, kout[row0 + t * 128: row0 + t * 128 + pt, si * 512:(si + 1) * 512], kf.ap[0:pt, :])
                E("dve", lambda e, bk=bk, kb=kb, pt=pt: e.tensor_copy(out=kb.ap[0:pt, :], in_=psum[bk].ap[0:pt, :]), R=[psum[bk]], W=[kb])
                bk2 = next_bank()
                for hh in range(4):
                    E("pe", lambda e, bk2=bk2, hh=hh, kb=kb, pt=pt: e.transpose(psum_b[bk2][:, hh * 128:hh * 128 + pt],
                                                                             kb.ap[0:pt, hh * 128:(hh + 1) * 128], ident.ap[0:pt, 0:pt]),
                      R=[kb, ident], W=[psum[bk2]])
                E("act", lambda e, bk2=bk2, kst=kst, pt=pt: e.copy(out=kst.ap[:, :, 0:pt],
                                                                  in_=psum_b[bk2][:, 0:512].rearrange("p (h k) -> p h k", k=128)[:, :, 0:pt]),
                  R=[psum[bk2]], W=[kst])
                for (dap, a_, b_, res) in kdst(t, pt, si):
                    dma_store(kst, dap, kst.ap[:, :, a_:b_], WM=[res])
        for si, s in enumerate(SL_V):
            slot = SLB.get("w_in", s, 0, 16)
            for t in range(g.nt):
                pt = g.pts[t]
                bk = next_bank()
                formB(g, slot, 0, 16, Bk["nT"], t, bk, True, True)
                i = (si * g.nt + t) % 2
                vf, vb = Bk["kf"][i], Bk["kb"][i]
                if vout is not None:
                    E("act", lambda e, bk=bk, vf=vf, pt=pt: e.copy(out=vf.ap[0:pt, :], in_=psum[bk].ap[0:pt, :]), R=[psum[bk]], W=[vf])
                    dma_store(vf, vout[row0 + t * 128: row0 + t * 128 + pt, si * 512:(si + 1) * 512], vf.ap[0:pt, :])
                E("dve", lambda e, bk=bk, vb=vb, pt=pt: e.tensor_copy(out=vb.ap[0:pt, :], in_=psum[bk].ap[0:pt, :]), R=[psum[bk]], W=[vb])
                for (dap, a_, b_, res) in vdst(t, pt, si):
                    dma_store(vb, dap, vb.ap[a_:b_, :], WM=[res])

    def attention(g, q0, q1, kT_d, V_d, kres, vres, nkeys, tile_info):
        nq = q1 - q0
        nkt = (nkeys + 127) // 128
        nch = (nkeys + 1023) // 1024
        O = [psum[4], psum[5]]
        ZB = psum[6]
        EP = psum[7]
        r, o, sq, rz = Bk["r"], Bk["o"], Bk["sq"], Bk["rz"]
        pending = []

        def make_epilogue(h):
            def st0():
                E("dve", lambda e: e.tensor_copy(out=o[0].ap[:, 0:nq], in_=O[0].ap[:, 0:nq]), R=[O[0]], W=[o[0]])
                E("dve", lambda e: e.tensor_copy(out=o[1].ap[:, 0:nq], in_=O[1].ap[:, 0:nq]), R=[O[1]], W=[o[1]])
                E("dve", lambda e: e.tensor_scalar(out=rz.ap[0:64, 0:nq], in0=ZB.ap[0:64, 0:nq], scalar1=1e-30, scalar2=None, op0=ALU.max),
                  R=[ZB], W=[rz])
                E("dve", lambda e: e.reciprocal(out=rz.ap[0:64, 0:nq], in_=rz.ap[0:64, 0:nq]), R=[rz], W=[rz])

            def st1():
                E("pe", lambda e: e.matmul(EP.ap[:, 0:nq], lhsT=selA.ap[0:64, :], rhs=rz.ap[0:64, 0:nq], start=True, stop=True),
                  R=[selA, rz], W=[EP])
                E("dve", lambda e: e.tensor_tensor(out=o[0].ap[:, 0:nq], in0=o[0].ap[:, 0:nq], in1=EP.ap[:, 0:nq], op=ALU.mult),
                  R=[o[0], EP], W=[o[0]])

            def st2():
                E("pe", lambda e: e.matmul(EP.ap[:, 0:nq], lhsT=selB.ap[0:64, :], rhs=rz.ap[0:64, 0:nq], start=True, stop=True),
                  R=[selB, rz], W=[EP])
                E("dve", lambda e: e.tensor_tensor(out=o[1].ap[:, 0:nq], in0=o[1].ap[:, 0:nq], in1=EP.ap[:, 0:nq], op=ALU.mult),
                  R=[o[1], EP], W=[o[1]])
                E("dve", lambda e: e.scalar_tensor_tensor(out=o[0].ap[:, 0:nq], in0=o[1].ap[:, 0:nq], scalar=nlam.ap[:, 0:1], in1=o[0].ap[:, 0:nq],
                                                          op0=ALU.mult, op1=ALU.add), R=[o[0], o[1], nlam], W=[o[0]])
                E("dve", lambda e: e.tensor_tensor(out=sq.ap[:, 0:nq], in0=o[0].ap[:, 0:nq], in1=o[0].ap[:, 0:nq], op=ALU.mult), R=[o[0]], W=[sq])

            def st3():
                E("pe", lambda e: e.matmul(EP.ap[:, 0:nq], lhsT=ones.ap, rhs=sq.ap[:, 0:nq], start=True, stop=True), R=[ones, sq], W=[EP])
                E("dve", lambda e: e.tensor_scalar(out=r[0].ap[:, 0:nq], in0=EP.ap[:, 0:nq], scalar1=1.0 / 128, scalar2=EPS, op0=ALU.mult, op1=ALU.add),
                  R=[EP], W=[r[0]])
                E("act", lambda e: e.activation(out=r[0].ap[:, 0:nq], in_=r[0].ap[:, 0:nq], func=AF.Ln), R=[r[0]], W=[r[0]])
                E("act", lambda e: e.activation(out=r[0].ap[:, 0:nq], in_=r[0].ap[:, 0:nq], func=AF.Exp, scale=-0.5), R=[r[0]], W=[r[0]])
                E("dve", lambda e: e.scalar_tensor_tensor(out=Bk["oT"][h].ap[:, q0:q1], in0=o[0].ap[:, 0:nq], scalar=gsub.ap[:, 0:1],
                                                          in1=r[0].ap[:, 0:nq], op0=ALU.mult, op1=ALU.mult),
                  R=[o[0], gsub, r[0]], W=[Bk["oT"][h]])
            return [st0, st1, st2, st3]

        def load_chunk(ci, h):
            k0 = ci * 1024
            n = min(1024, nkeys - k0)
            kc_, vc_ = Bk["kTc"][ld_ctr[0] % 3], Bk["Vc"][ld_ctr[0] % 3]
            ld_ctr[0] += 1
            dma_load(kc_, kc_.ap[:, 0:n], kT_d.ap()[h, :, k0:k0 + n], R=kres(ci))
            nfull = n // 128
            if nfull > 0:
                dma_load(vc_, vc_.ap[:, 0:nfull, :],
                         V_d.ap()[k0:k0 + nfull * 128, h * 128:(h + 1) * 128].rearrange("(kt p) d -> p kt d", p=128),
                         R=vres(ci))
            rem = n - nfull * 128
            if rem > 0:
                dma_load(vc_, vc_.ap[0:rem, nfull, :], V_d.ap()[k0 + nfull * 128:k0 + n, h * 128:(h + 1) * 128], R=vres(ci))
            return kc_, vc_

        first_chunk = load_chunk(0, 0)
        for h in range(NH):
            chunks = {0: first_chunk}
            first_chunk = None
            pend = None
            for kt in range(nkt + 1):
                cur = None
                if kt < nkt:
                    ci = kt // 8
                    if kt % 8 == 0:
                        if ci + 1 < nch:
                            chunks[ci + 1] = load_chunk(ci + 1, h)
                        elif h + 1 < NH:
                            first_chunk = load_chunk(0, h + 1)
                    kc_, vc_ = chunks[ci]
                    kl = kt % 8
                    kp = min(128, nkeys - kt * 128)
                    qa, is_pre, specials = tile_info(kt, h)
                    pi = kt % 2
                    sb0, sb1 = psum[2 * pi], psum[2 * pi + 1]
                    for m, sb in enumerate((sb0, sb1)):
                        E("pe", lambda e, sb=sb, m=m, kc_=kc_, kl=kl, kp=kp, qa=qa, h=h: e.matmul(
                            sb.ap[0:kp, qa:nq], lhsT=kc_.ap[64 * m:64 * m + 64, kl * 128:kl * 128 + kp],
                            rhs=Bk["qT"][h].ap[64 * m:64 * m + 64, q0 + qa:q1], start=True, stop=True),
                          R=[kc_, Bk["qT"][h]], W=[sb])
                    PPt = Bk["PP"][pi]
                    sv = pp[pi][0:kp, :].rearrange("p (m w) -> p m w", m=2)[:, :, qa:nq]
                    E("act", lambda e, sv=sv, PPt=PPt, kp=kp, qa=qa: e.activation(
                        out=PPt.ap[0:kp, :, qa:nq], in_=sv, func=AF.Exp, scale=0.125), R=[sb0, sb1], W=[PPt])
                    for (ca, cb, EB, ea, eb) in specials:
                        for m in range(2):
                            E("dve", lambda e, PPt=PPt, m=m, kp=kp, ca=ca, cb=cb, EB=EB, ea=ea, eb=eb, h=h: e.tensor_tensor(
                                out=PPt.ap[0:kp, m, ca:cb], in0=PPt.ap[0:kp, m, ca:cb], in1=EB.ap[0:kp, h, ea:eb], op=ALU.mult),
                              R=[PPt, EB], W=[PPt])
                    cur = (kt, kp, qa, vc_, kl, PPt, onespre if is_pre else ones)
                if pend is not None:
                    pkt, pkp, pqa, pvc, pkl, pP, pones = pend
                    for m in range(2):
                        E("pe", lambda e, m=m, pkp=pkp, pqa=pqa, pvc=pvc, pkl=pkl, pP=pP, pkt=pkt: e.matmul(
                            O[m].ap[:, pqa:nq], lhsT=pvc.ap[0:pkp, pkl, :], rhs=pP.ap[0:pkp, m, pqa:nq],
                            start=(pkt == 0), stop=(pkt == nkt - 1)), R=[pvc, pP], W=[O[m]])
                    for m in range(2):
                        E("pe", lambda e, m=m, pkp=pkp, pqa=pqa, pP=pP, pkt=pkt, pones=pones: e.matmul(
                            ZB.ap[32 * m:32 * m + 32, pqa:nq], lhsT=pones.ap[0:pkp, 0:32], rhs=pP.ap[0:pkp, m, pqa:nq],
                            start=(pkt == 0), stop=(pkt == nkt - 1)), R=[pones, pP], W=[ZB])
                    if pending and pkt >= 1:
                        pending.pop(0)()
                pend = cur
            while pending:
                pending.pop(0)()
            steps = make_epilogue(h)
            steps[0]()
            pending = steps[1:]
        while pending:
            pending.pop(0)()

    def proj_merge(gs):
        for (wname, gsl, src, first) in (("w_proj_a", SL_GA, "oT", True), ("w_proj_b", SL_GB, "obT", False)):
            for si in range(4):
                gslot = SLB.get("w_in", gsl[si], 0, 16)
                for j in range(4):
                    sg = Bk["sg4"][j]
                    for g in gs:
                        bk = formA(g, gslot, 16, j, Bk["nT"])
                        E("act", lambda e, bk=bk, sg=sg, g=g: e.activation(out=sg.ap[:, g.c0:g.c0 + g.N], in_=psum[bk].ap[:, 0:g.N], func=AF.Sigmoid),
                          R=[psum[bk]], W=[sg])
                pslot = SLB.get(wname, si, 0, 8)
                for j in range(4):
                    m = si * 4 + j
                    sg = Bk["sg4"][j]
                    mg = Bk["mergedT"][m]
                    for g in gs:
                        c0, N = g.c0, g.N
                        bk = formA(g, pslot, 8, j, Bk[src])
                        if first:
                            E("dve", lambda e, bk=bk, mg=mg, sg=sg, c0=c0, N=N: e.tensor_tensor(out=mg.ap[:, c0:c0 + N], in0=psum[bk].ap[:, 0:N],
                                                                                              in1=sg.ap[:, c0:c0 + N], op=ALU.mult),
                              R=[psum[bk], sg], W=[mg])
                        else:
                            tmp = Bk["mtmp"][m % 2]
                            E("dve", lambda e, bk=bk, tmp=tmp, sg=sg, c0=c0, N=N: e.tensor_tensor(out=tmp.ap[:, 0:N], in0=psum[bk].ap[:, 0:N],
                                                                                                in1=sg.ap[:, c0:c0 + N], op=ALU.mult),
                              R=[psum[bk], sg], W=[tmp])
                            E("pool", lambda e, mg=mg, tmp=tmp, c0=c0, N=N: e.tensor_tensor(out=mg.ap[:, c0:c0 + N], in0=mg.ap[:, c0:c0 + N],
                                                                                          in1=tmp.ap[:, 0:N], op=ALU.add),
                              R=[mg, tmp], W=[mg])

    def out_proj(gs, xsrcs, row0s):
        for g, xsrc, row0 in zip(gs, xsrcs, row0s):
            for t in range(g.nt):
                pt = g.pts[t]
                xs = g.xs[t]
                dma_load(xs, xs.ap[0:pt, :], xsrc[row0 + t * 128: row0 + t * 128 + pt, :])
        for n in range(4):
            slot = SLB.get("w_o", n, 0, 16)
            for g in gs:
                for t in range(g.nt):
                    pt = g.pts[t]
                    bk = next_bank()
                    formB(g, slot, 0, 16, Bk["mergedT"], t, bk, True, True)
                    xs = g.xs[t]
                    E("dve", lambda e, bk=bk, xs=xs, pt=pt, n=n: e.tensor_tensor(out=xs.ap[0:pt, n * 512:(n + 1) * 512], in0=psum[bk].ap[0:pt, :],
                                                                             in1=xs.ap[0:pt, n * 512:(n + 1) * 512], op=ALU.add),
                      R=[psum[bk], xs], W=[xs])

    def ffn_up(gs, halos, act_for):
        for s in range(22):
            slot = SLB.get("w_up", s, 0, 16)
            for j in range(4):
                c = s * 4 + j
                for gi, (g, halo) in enumerate(zip(gs, halos)):
                    nseg, L, N, c0 = g.nseg, g.L, g.N, g.c0
                    bk = formA(g, slot, 16, j, Bk["nT"])
                    p3 = psum[bk].ap[:, 0:N].rearrange("p (s w) -> p s w", w=L)
                    h3 = halo.ap[:, c] if nseg > 1 else halo.ap[:, c, :].rearrange("p (s w) -> p s w", s=1)
                    if g not in act_for:
                        E("act", lambda e, h3=h3, p3=p3, L=L: e.copy(out=h3, in_=p3[:, :, L - 2:L]), R=[psum[bk]], W=[halo])
                        continue
                    tmp = Bk["ctmp"][c % 4]
                    t3 = tmp.ap[:, 0:N].rearrange("p (s w) -> p s w", w=L)
                    E("act", lambda e, t3=t3, p3=p3, c=c: e.activation(out=t3, in_=p3, func=AF.Copy, scale=fwc.ap[:, 2, c:c + 1]),
                      R=[psum[bk], fwc], W=[tmp])
                    E("dve", lambda e, t3=t3, h3=h3, c=c: e.scalar_tensor_tensor(out=t3[:, :, 0:2], in0=h3, scalar=fwc.ap[:, 0, c:c + 1],
                                                                              in1=t3[:, :, 0:2], op0=ALU.mult, op1=ALU.add),
                      R=[halo, fwc, tmp], W=[tmp])
                    E("dve", lambda e, t3=t3, h3=h3, c=c: e.scalar_tensor_tensor(out=t3[:, :, 0:1], in0=h3[:, :, 1:2], scalar=fwc.ap[:, 1, c:c + 1],
                                                                              in1=t3[:, :, 0:1], op0=ALU.mult, op1=ALU.add),
                      R=[halo, fwc, tmp], W=[tmp])
                    E("act", lambda e, h3=h3, p3=p3, L=L: e.copy(out=h3, in_=p3[:, :, L - 2:L]), R=[psum[bk]], W=[halo])
                    E("dve", lambda e, t3=t3, p3=p3, c=c, L=L: e.scalar_tensor_tensor(out=t3[:, :, 1:L], in0=p3[:, :, 0:L - 1], scalar=fwc.ap[:, 1, c:c + 1],
                                                                                   in1=t3[:, :, 1:L], op0=ALU.mult, op1=ALU.add),
                      R=[psum[bk], fwc, tmp], W=[tmp])
                    E("dve", lambda e, t3=t3, p3=p3, c=c, L=L: e.scalar_tensor_tensor(out=t3[:, :, 2:L], in0=p3[:, :, 0:L - 2], scalar=fwc.ap[:, 0, c:c + 1],
                                                                                   in1=t3[:, :, 2:L], op0=ALU.mult, op1=ALU.add),
                      R=[psum[bk], fwc, tmp], W=[tmp])
                    if c < 44:
                        E("act", lambda e, tmp=tmp, c=c, N=N, c0=c0: e.activation(out=Bk["actT"][c].ap[:, c0:c0 + N], in_=tmp.ap[:, 0:N], func=AF.Silu),
                          R=[tmp], W=[Bk["actT"][c]])
                    else:
                        a_ = Bk["actT"][c - 44]
                        E("pool", lambda e, tmp=tmp, a_=a_, N=N, c0=c0: e.tensor_tensor(out=a_.ap[:, c0:c0 + N], in0=a_.ap[:, c0:c0 + N],
                                                                                      in1=tmp.ap[:, 0:N], op=ALU.mult),
                          R=[tmp, a_], W=[a_])

    def ffn_down(g):
        for n in range(4):
            banks = [(n % 2) * 4 + t for t in range(g.nt)]
            for sub in range(4):
                slot = SLB.get("w_down", n, sub * 11, 11)
                for t in range(g.nt):
                    formB(g, slot, sub * 11, 11, Bk["actT"], t, banks[t], sub == 0, sub == 3)
            for t in range(g.nt):
                pt = g.pts[t]
                xs, bk = g.xs[t], banks[t]
                E("dve", lambda e, bk=bk, xs=xs, pt=pt, n=n: e.tensor_tensor(out=xs.ap[0:pt, n * 512:(n + 1) * 512], in0=psum[bk].ap[0:pt, :],
                                                                         in1=xs.ap[0:pt, n * 512:(n + 1) * 512], op=ALU.add),
                  R=[psum[bk], xs], W=[xs])

    def final_norm(g, yout, row0):
        for t in range(g.nt):
            pt = g.pts[t]
            xs = g.xs[t]
            rms_stats(g, xs, t, 4 + t)
            E("dve", lambda e, xs=xs, pt=pt, t=t: e.scalar_tensor_tensor(out=xs.ap[0:pt, :], in0=xs.ap[0:pt, :], scalar=rstd.ap[0:pt, 4 + t:5 + t],
                                                                        in1=gfb.ap[0:pt, :], op0=ALU.mult, op1=ALU.mult),
              R=[xs, rstd, gfb], W=[xs])
            dma_store(xs, yout[row0 + t * 128: row0 + t * 128 + pt, :], xs.ap[0:pt, :])

    def store_states(g, uh, fh, cm_out, cf_out):
        for s in range(g.nseg):
            for jj in range(2):
                src = uh.ap[:, :, s, jj] if g.nseg > 1 else uh.ap[:, :, jj]
                store_cols(uh, src, cm_out.ap()[s, jj, :].rearrange("(c p) -> c p", p=128), 8)
                src = fh.ap[:, :, s, jj] if g.nseg > 1 else fh.ap[:, :, jj]
                store_cols(fh, src, cf_out.ap()[s, jj, :].rearrange("(c p) -> c p", p=128), 88)

    GM = Group("main", 512, [128] * 4, 1, 512)
    GS = Group("smp", 128, [128], 2, 64)
    GH = Group("halo", 4, [4], 1, 4, c0=512, u0=514, scol=8)

    def load_x(g, src, row0):
        for t in range(g.nt):
            pt = g.pts[t]
            xs = g.xs[t]
            dma_load(xs, xs.ap[0:pt, :], src[row0 + t * 128: row0 + t * 128 + pt, :])

    class CacheImport:
        def __init__(self):
            RV.reset()
            self.k32 = [RV.take(f"ik32_{i}", [1024], F32) for i in range(2)]
            self.v32 = [RV.take(f"iv32_{i}", [1024], F32) for i in range(2)]
            self.k16 = RV.take("ik16", [1024], BF16)
            self.v16 = RV.take("iv16", [1024], BF16)
            self.kst = RV.take("ikst", [NH, 128], BF16)
            self.bufs = self.k32 + self.v32 + [self.k16, self.v16, self.kst]
            self.i = 0

        def left(self):
            return self.i < 16

        def step(self):
            if self.i >= 16:
                return
            s_, kt = self.i // 8, self.i % 8
            k32, v32 = self.k32[self.i % 2], self.v32[self.i % 2]
            k16, v16, kst = self.k16, self.v16, self.kst
            self.i += 1
            dma_load(k32, k32.ap, cache_k.ap()[s_, kt * 128:(kt + 1) * 128, :])
            dma_load(v32, v32.ap, cache_v.ap()[s_, kt * 128:(kt + 1) * 128, :])
            E("pool", lambda e: e.tensor_copy(out=k16.ap, in_=k32.ap), R=[k32], W=[k16])
            E("pool", lambda e: e.tensor_copy(out=v16.ap, in_=v32.ap), R=[v32], W=[v16])
            bk = next_bank()
            for h in range(NH):
                E("pe", lambda e, h=h: e.transpose(psum_b[bk][:, h * 128:(h + 1) * 128], k16.ap[:, h * 128:(h + 1) * 128], ident.ap),
                  R=[k16, ident], W=[psum[bk]])
            E("act", lambda e: e.copy(out=kst.ap, in_=psum_b[bk][:, 0:1024].rearrange("p (h k) -> p h k", k=128)), R=[psum[bk]], W=[kst])
            dma_store(kst, kTq[s_].ap()[:, :, kt * 128:(kt + 1) * 128].rearrange("h p k -> p h k"), kst.ap, WM=[kTq_res[s_][0]])
            dma_store(v16, Vq[s_].ap()[kt * 128:(kt + 1) * 128, :], v16.ap, WM=[Vq_res[s_][0]])

    class Stop(Exception):
        pass

    def chk(name):
        if KSTOP == name:
            raise Stop()

    def program():
        try:
            program_()
        except Stop:
            pass

    def program_():
        chk("consts")
        pre_res_k = lambda ci: [kTs_res[2 * ci], kTs_res[2 * ci + 1]]
        pre_res_v = lambda ci: [Vs_res[2 * ci], Vs_res[2 * ci + 1]]
        for pb in range(4):
            load_x(GM, x_pre.ap(), pb * 512)
            norm_transpose(GM, g1c)

            def kdst(t, pt, si, pb=pb):
                k0 = pb * 512 + t * 128
                return [(kTs.ap()[si * 4:(si + 1) * 4, :, k0:k0 + pt].rearrange("h p k -> p h k"), 0, pt, kTs_res[pb])]

            def vdst(t, pt, si, pb=pb):
                k0 = pb * 512 + t * 128
                return [(Vs.ap()[k0:k0 + pt, si * 512:(si + 1) * 512], 0, pt, Vs_res[pb])]
            inproj_kv(GM, None, None, 0, kdst, vdst)
            CONV.advance(40)
            chk("prefix0")
        chk("prefix")
        def tile_info_h(kt, h):
            if kt == 14:
                return 0, True, [(0, 4, EBp, 124, 128)]
            if kt == 15:
                return 0, True, [(0, 4, EBd, 124, 128)]
            return 0, True, []
        for j in range(4):
            gs = [GH, GM] if j == 0 else [GM]
            if j == 0:
                load_x(GH, x_halo.ap(), 0)
            load_x(GM, x_own.ap(), j * 512)
            imp = (lambda n=2: [IMP.step() for _ in range(n)]) if j == 1 else (lambda n=2: None)
            for g in gs:
                norm_transpose(g, g1c)
            imp()
            inproj_conv(gs, [uhalo] * len(gs))
            imp()
            inproj_q(gs)
            imp()

            def kdst(t, pt, si, j=j):
                k0 = 2048 + j * 512 + t * 128
                return [(kTs.ap()[si * 4:(si + 1) * 4, :, k0:k0 + pt].rearrange("h p k -> p h k"), 0, pt, kTs_res[4 + j])]

            def vdst(t, pt, si, j=j):
                k0 = 2048 + j * 512 + t * 128
                return [(Vs.ap()[k0:k0 + pt, si * 512:(si + 1) * 512], 0, pt, Vs_res[4 + j])]
            inproj_kv(GM, k_own.ap(), v_own.ap(), j * 512, kdst, vdst)
            if j == 0:
                attention(GH, GH.c0, GH.c0 + 4, kTs, Vs, pre_res_k, pre_res_v, 2048, tile_info_h)

            def tile_info(kt, h, j=j):
                own0 = 16 + 4 * j
                if kt < 16:
                    sp = [(0, 128, EBp, 0, 128)] if (kt == 15 and j == 0) else []
                    return 0, True, sp
                if kt < own0:
                    sp = [(0, 128, EBp, 0, 128)] if kt == own0 - 1 else []
                    return 0, False, sp
                i = kt - own0
                sp = [(i * 128, (i + 1) * 128, EBd, 0, 128)]
                if i + 1 < 4:
                    sp.append(((i + 1) * 128, (i + 2) * 128, EBp, 0, 128))
                return i * 128, False, sp
            nkeys = 2048 + (j + 1) * 512
            attention(GM, 0, 512, kTs, Vs, pre_res_k, pre_res_v, nkeys, tile_info)
            chk("attn")
            imp()
            proj_merge(gs)
            imp()
            out_proj(gs, [x_halo.ap(), x_own.ap()] if j == 0 else [x_own.ap()], [0, 0] if j == 0 else [j * 512])
            imp()
            for g in gs:
                norm_transpose(g, g2c)
            imp()
            if j == 0:
                dma_load(gfb, gfb.ap, gf_d.ap().partition_broadcast(128))
            ffn_up(gs, [uph] * len(gs), [GM])
            imp()
            ffn_down(GM)
            final_norm(GM, y_own.ap(), j * 512)
            chk("block0")
        store_states(GM, uhalo, uph, cm_own, cf_own)
        while IMP.left():
            IMP.step()
        load_x(GS, x_smp.ap(), 0)
        norm_transpose(GS, g1c)
        inproj_conv([GS], [uhaloq])
        inproj_q([GS])

        def kdst(t, pt, si):
            return [(kTq[s].ap()[si * 4:(si + 1) * 4, :, 1024:1088].rearrange("h p k -> p h k"), s * 64, (s + 1) * 64, kTq_res[s][1])
                    for s in range(2)]

        def vdst(t, pt, si):
            return [(Vq[s].ap()[1024:1088, si * 512:(si + 1) * 512], s * 64, (s + 1) * 64, Vq_res[s][1]) for s in range(2)]
        inproj_kv(GS, k_smp.ap(), v_smp.ap(), 0, kdst, vdst)
        for s in range(2):
            def tile_info(kt, h):
                if kt < 7:
                    return 0, False, []
                if kt == 7:
                    return 0, False, [(0, 64, EBp, 0, 64)]
                return 0, False, [(0, 64, EBd, 0, 64)]
            attention(GS, s * 64, (s + 1) * 64, kTq[s], Vq[s], lambda ci, s=s: [kTq_res[s][ci]], lambda ci, s=s: [Vq_res[s][ci]],
                      1088, tile_info)
        proj_merge([GS])
        out_proj([GS], [x_smp.ap()], [0])
        norm_transpose(GS, g2c)
        ffn_up([GS], [uphq], [GS])
        ffn_down(GS)
        final_norm(GS, y_smp.ap(), 0)
        store_states(GS, uhaloq, uphq, cm_smp, cf_smp)

    Bk = alloc_block_bufs()
    Bk["sg4"] = Bk["sg"] + Bk["mtmp"]
    RM.reset()
    oT = [RM.take(f"oT{h}", [FW], BF16) for h in range(NH)]
    sg4 = [RM.take(f"sg4_{i}", [FW], BF16) for i in range(4)]
    mtmp = [RM.take(f"mtmp{i}", [FW], BF16) for i in range(2)]
    role_old = [b.res for k in ("kf", "kb", "kTst", "ztmp") for b in Bk[k]] + [Bk["ust"].res] + [b.res for b in Bk["ctmp"]]
    alias([b.res for b in oT + sg4 + mtmp], role_old)
    Bk["oT"], Bk["sg4"], Bk["mtmp"] = oT, sg4, mtmp
    GM.xs, GM.xn = Bk["xs"], Bk["xn"]
    GS.xs, GS.xn = Bk["xs"][0:1], Bk["xn"][0:1]
    GH.xs, GH.xn = [gfb], [Bk["xnh"]]
    ld_ctr = [0]

    CONV = Conv()
    IMP = CacheImport()
    alias([b.res for b in CONV.a + CONV.b], [b.res for b in IMP.bufs])
    P.dry = True
    SLB = Slabs(None)
    program()
    IMP.i = 0
    sched = SLB.rec
    P.dry = False
    bank_ctr[0] = 0
    ld_ctr[0] = 0
    cast_rr[0] = 0
    cbufs = setup_consts()
    SLB = Slabs(sched)
    alias([b.res for b in cbufs], [b.res for k in ("xn", "qT", "mergedT", "kTc", "Vc", "r", "o", "actT") for b in Bk[k]]
          + [Bk["junk"].res, Bk["sq"].res] + [b.res for b in Bk["PP"]])
    program()

    def mksems(n):
        return [es.enter_context(nc.semaphore(f"s{i}")) for i in range(n)]
    block = es.enter_context(nc.Block())
    stats = P.finalize(nc, block, mksems)
    es.close()
    return nc, stats


def _rel_bucket(rel):
    nb, max_exact = 16, 8
    ret = np.where(rel > 0, nb, 0)
    n = np.abs(rel)
    large = max_exact + (np.log(np.maximum(n, 1).astype(np.float32) / max_exact)
                         / np.float32(np.log(128 / max_exact)) * (nb - max_exact)).astype(np.int32)
    large = np.minimum(large, nb - 1)
    return ret + np.where(n < max_exact, n, large)


_CACHE = {}


def kernel(**inputs):
    f32 = lambda a: np.ascontiguousarray(np.asarray(a, dtype=np.float32))
    inp = {k: f32(v) for k, v in inputs.items()}
    if "nc" not in _CACHE:
        _CACHE["nc"] = build_program()
    nc, stats = _CACHE["nc"]
    kk = np.arange(128)[:, None]
    qq = np.arange(128)[None, :]
    bkt_d = _rel_bucket(kk - qq)
    bkt_p = _rel_bucket(kk - 128 - qq)
    rb = inp["rel_bias"]
    bias_diag = np.ascontiguousarray(np.transpose(rb[bkt_d], (2, 0, 1)))
    bias_prev = np.ascontiguousarray(np.transpose(rb[bkt_p], (2, 0, 1)))
    mask_diag = ((kk // 64) <= (qq // 64)).astype(np.float32)
    shared = {
        "bias_diag": bias_diag, "bias_prev": bias_prev, "mask_diag": mask_diag, "rel_bias": rb,
        "w_in": inp["w_in"][0], "w_proj_a": inp["w_proj_a"][0], "w_proj_b": inp["w_proj_b"][0], "w_o": inp["w_o"][0],
        "w_up": inp["w_up"][0], "w_down": inp["w_down"][0],
        "norm1_g": inp["norm1_g"][0], "norm2_g": inp["norm2_g"][0], "final_g": inp["final_g"],
        "lambda_q1": inp["lambda_q1"][0], "lambda_k1": inp["lambda_k1"][0], "lambda_q2": inp["lambda_q2"][0],
        "lambda_k2": inp["lambda_k2"][0], "subln_g": inp["subln_g"][0], "conv_w": inp["conv_w"][0],
        "ffn_conv_w": inp["ffn_conv_w"][0],
    }
    in_maps = []
    for c in range(NCORES):
        b, half = c // 2, c % 2
        m = dict(shared)
        m["x_own"] = np.ascontiguousarray(inp["x_prompt"][b, half * 2048:(half + 1) * 2048])
        m["x_pre"] = np.ascontiguousarray(inp["x_prompt"][b, 0:2048]) if half == 1 else np.zeros((2048, D), np.float32)
        m["x_smp"] = np.ascontiguousarray(inp["x_sample"][2 * c:2 * c + 2].reshape(128, D))
        m["x_halo"] = (np.ascontiguousarray(inp["x_prompt"][b, 2044:2048]) if half == 1 else np.zeros((4, D), np.float32))
        m["premask"] = np.full((128, 1), 1.0 if half == 1 else 0.0, np.float32)
        m["cache_k"] = np.ascontiguousarray(inp["cache_k"][0, 2 * c:2 * c + 2].reshape(2, 1024, 1024))
        m["cache_v"] = np.ascontiguousarray(inp["cache_v"][0, 2 * c:2 * c + 2].reshape(2, 1024, 1024))
        m["st_mix"] = np.ascontiguousarray(inp["state_conv_mix"][0, 2 * c:2 * c + 2])
        m["st_ffn"] = np.ascontiguousarray(inp["state_conv_ffn"][0, 2 * c:2 * c + 2])
        in_maps.append(m)
    ncr = int(os.environ.get("NC_DEBUG", NCORES))
    res = run_bass_kernel_spmd(nc, in_maps[:ncr], core_ids=list(range(ncr)))
    R = list(res.results) + [res.results[0]] * (NCORES - ncr)
    y_prompt = np.zeros((4, 4096, D), np.float32)
    k_prompt = np.zeros((1, 4, 4096, NH, 128), np.float32)
    v_prompt = np.zeros((1, 4, 4096, NH, 128), np.float32)
    cm_prompt = np.zeros((1, 4, 2, 1024), np.float32)
    cf_prompt = np.zeros((1, 4, 2, 2 * DFF), np.float32)
    y_sample = np.zeros((16, 64, D), np.float32)
    k_sample = np.zeros((1, 16, 64, NH, 128), np.float32)
    v_sample = np.zeros((1, 16, 64, NH, 128), np.float32)
    cm_sample = np.zeros((1, 16, 2, 1024), np.float32)
    cf_sample = np.zeros((1, 16, 2, 2 * DFF), np.float32)
    for c in range(NCORES):
        b, half = c // 2, c % 2
        r = R[c]
        sl = slice(half * 2048, (half + 1) * 2048)
        y_prompt[b, sl] = r["y_own"]
        k_prompt[0, b, sl] = r["k_own"].reshape(2048, NH, 128)
        v_prompt[0, b, sl] = r["v_own"].reshape(2048, NH, 128)
        if half == 1:
            cm_prompt[0, b] = r["cm_own"][0]
            cf_prompt[0, b] = r["cf_own"][0]
        y_sample[2 * c:2 * c + 2] = r["y_smp"].reshape(2, 64, D)
        k_sample[0, 2 * c:2 * c + 2] = r["k_smp"].reshape(2, 64, NH, 128)
        v_sample[0, 2 * c:2 * c + 2] = r["v_smp"].reshape(2, 64, NH, 128)
        cm_sample[0, 2 * c:2 * c + 2] = r["cm_smp"]
        cf_sample[0, 2 * c:2 * c + 2] = r["cf_smp"]
    return (y_prompt, y_sample, k_prompt, v_prompt, cm_prompt, cf_prompt,
            k_sample, v_sample, cm_sample, cf_sample)
```

```python
import numpy as np
from contextlib import ExitStack
import concourse.bass as bass
import concourse.mybir as mybir
from concourse.bass_utils import run_bass_kernel_spmd

F32 = mybir.dt.float32
BF16 = mybir.dt.bfloat16
AF = mybir.ActivationFunctionType
ALU = mybir.AluOpType
AX = mybir.AxisListType

D = 2048
NH = 8
DFF = 5632
EPS = 1e-6
NCORES = 8
COMPUTE = ("pe", "act", "dve", "pool")
SAME_ENGINE_SYNC = True
SEM_MAX = 30000
NEGM = -30000.0
import os
KSTOP = os.environ.get("KSTOP", "")
SKIP = os.environ.get("SKIP", "")
NOCONV = os.environ.get("NOCONV", "") == "1"


class Res:
    __slots__ = ("name", "w", "r", "al", "chan", "excl")

    def __init__(self, name=""):
        self.name = name
        self.w = []
        self.r = {}
        self.al = []
        self.excl = False
        self.chan = None


class Chan:
    def __init__(self, prog, name):
        self.name = name
        self.idx = len(prog.chans)
        self.count = 0
        prog.chans.append(self)


def alias(ga, gb):
    for a in ga:
        for b in gb:
            if a is not b:
                a.al.append(b)
                b.al.append(a)


class Prog:
    def __init__(self):
        self.recs = {e: [] for e in ("pe", "act", "dve", "pool", "sp")}
        self.chans = []
        self.waited = {e: {} for e in self.recs}
        self.dry = False

    def chan_of(self, res):
        if res.chan is None:
            res.chan = Chan(self, res.name)
        return res.chan

    def emit(self, eng, fn, R=(), W=(), WM=(), chan=None):
        if self.dry:
            return None
        deps = []
        for r in R:
            deps.extend(r.w)
            if r.excl:
                deps.extend(ev for k_, ev in r.r.items() if k_ != eng)
        for w in W:
            deps.extend(w.w)
            deps.extend(w.r.values())
            for a in w.al:
                deps.extend(a.w)
                deps.extend(a.r.values())
        for w in WM:
            deps.extend(w.r.values())
        if chan is not None and chan.count > 0:
            deps.append(("d", chan.idx, chan.count))
        idx = len(self.recs[eng])
        best = {}
        for ev in deps:
            if ev[0] == "c":
                if ev[1] == eng and (eng == "pe" or not SAME_ENGINE_SYNC):
                    continue
                k = ("c", ev[1])
            else:
                k = ("d", ev[1])
            if ev[2] > best.get(k, -1):
                best[k] = ev[2]
        waits = []
        wd = self.waited[eng]
        for k, v in best.items():
            if wd.get(k, -1) >= v:
                continue
            wd[k] = v
            waits.append((k, v))
        if chan is not None:
            chan.count += 16
            ev = ("d", chan.idx, chan.count)
        else:
            ev = ("c", eng, idx)
        self.recs[eng].append((fn, waits, chan))
        key = ev[1] if ev[0] == "c" else ("d", ev[1])
        for r in R:
            old = r.r.get(key)
            if old is None or old[2] < ev[2]:
                r.r[key] = ev
        for w in W:
            w.w = [ev]
            w.r = {}
        for w in WM:
            w.w = w.w + [ev]
        return ev

    def finalize(self, nc, block, mksems):
        miles = {e: set() for e in COMPUTE}
        for e, recs in self.recs.items():
            for fn, waits, chan in recs:
                for k, v in waits:
                    if k[0] == "c":
                        miles[k[1]].add(v)
        rank, nsem = {}, {}
        for e in COMPUTE:
            s = sorted(miles[e])
            rank[e] = {idx: i for i, idx in enumerate(s)}
            nsem[e] = max(1, (len(s) + SEM_MAX - 1) // SEM_MAX)
        total = sum(nsem.values()) + len(self.chans)
        sems = mksems(total)
        pos = 0
        esem = {}
        for e in COMPUTE:
            esem[e] = sems[pos:pos + nsem[e]]
            pos += nsem[e]
        csem = sems[pos:pos + len(self.chans)]

        def run(e):
            def body(eng):
                myrank = rank.get(e, {})
                for idx, (fn, waits, chan) in enumerate(self.recs[e]):
                    for k, v in waits:
                        if k[0] == "c":
                            r = rank[k[1]][v]
                            eng.wait_ge(esem[k[1]][r // SEM_MAX], (r % SEM_MAX) + 1)
                        else:
                            eng.wait_ge(csem[k[1]], v)
                    ins = fn(eng)
                    if chan is not None:
                        ins.then_inc(csem[chan.idx], 16)
                    elif idx in myrank:
                        r = myrank[idx]
                        ins.then_inc(esem[e][r // SEM_MAX], 1)
                if e == "sp":
                    for c in self.chans:
                        if c.count > 0:
                            eng.wait_ge(csem[c.idx], c.count)
            return body

        block.sync(run("sp"))
        block.tensor(run("pe"))
        block.scalar(run("act"))
        block.vector(run("dve"))
        block.gpsimd(run("pool"))
        return ({e: len(r) for e, r in self.recs.items()}, {e: len(miles[e]) for e in COMPUTE},
                total)


class Buf:
    def __init__(self, ap, name):
        self.ap = ap
        self.res = Res(name)

    def __getitem__(self, k):
        return self.ap[k]


WSPECS = {
    "w_in": (D, 10240), "w_proj_a": (1024, D), "w_proj_b": (1024, D),
    "w_o": (D, D), "w_up": (D, 2 * DFF), "w_down": (DFF, D),
}
SL_Q, SL_K, SL_V, SL_B, SL_C, SL_X, SL_GA, SL_GB = (0, 1), (2, 3), (4, 5), (6, 7), (8, 9), (10, 11), (12, 13, 14, 15), (16, 17, 18, 19)


class Group:
    def __init__(self, name, N, pts, nseg, L, c0=0, u0=0, scol=0):
        self.name, self.N, self.pts, self.nseg, self.L = name, N, pts, nseg, L
        self.nt = len(pts)
        self.c0, self.u0, self.scol = c0, u0, scol
        self.xs = self.xn = None


def build_program():
    nc = bass.Bass("TRN2", target_bir_lowering=False)
    es = ExitStack()
    P = Prog()

    def din(name, shape, dt=F32):
        return nc.dram_tensor(name, list(shape), dt, kind="ExternalInput")

    def dout(name, shape, dt=F32):
        return nc.dram_tensor(name, list(shape), dt, kind="ExternalOutput")

    def dscr(name, shape, dt=BF16):
        return nc.dram_tensor(name, list(shape), dt, kind="Internal")

    x_own = din("x_own", [2048, D])
    x_pre = din("x_pre", [2048, D])
    x_smp = din("x_smp", [128, D])
    x_halo = din("x_halo", [4, D])
    premask_d = din("premask", [128, 1])
    cache_k = din("cache_k", [2, 1024, 1024])
    cache_v = din("cache_v", [2, 1024, 1024])
    st_mix = din("st_mix", [2, 2, 1024])
    st_ffn = din("st_ffn", [2, 2, 2 * DFF])
    bd_d = din("bias_diag", [NH, 128, 128])
    bp_d = din("bias_prev", [NH, 128, 128])
    mk_d = din("mask_diag", [128, 128])
    relb_d = din("rel_bias", [32, NH])
    wd = {n: din(n, s) for n, s in WSPECS.items()}
    g1_d, g2_d, gf_d = din("norm1_g", [D]), din("norm2_g", [D]), din("final_g", [D])
    lam_d = [din(n, [64]) for n in ("lambda_q1", "lambda_k1", "lambda_q2", "lambda_k2")]
    subg_d = din("subln_g", [128])
    cw_d = din("conv_w", [3, 1024])
    fw_d = din("ffn_conv_w", [3, 2 * DFF])

    y_own = dout("y_own", [2048, D])
    y_smp = dout("y_smp", [128, D])
    k_own = dout("k_own", [2048, 1024])
    v_own = dout("v_own", [2048, 1024])
    cm_own = dout("cm_own", [1, 2, 1024])
    cf_own = dout("cf_own", [1, 2, 2 * DFF])
    k_smp = dout("k_smp", [128, 1024])
    v_smp = dout("v_smp", [128, 1024])
    cm_smp = dout("cm_smp", [2, 2, 1024])
    cf_smp = dout("cf_smp", [2, 2, 2 * DFF])

    wb = {n: dscr(n + "_bf", [N // 512, 128, K // 128, 512]) for n, (K, N) in WSPECS.items()}
    wb_res = {n: [[Res(f"{n}_s{s}_k{kc}") for kc in range(K // 128)] for s in range(N // 512)]
              for n, (K, N) in WSPECS.items()}
    kTs = dscr("kTs", [NH, 128, 4096])
    Vs = dscr("Vs", [4096, 1024])
    kTs_res = [Res(f"kTs_b{i}") for i in range(8)]
    Vs_res = [Res(f"Vs_b{i}") for i in range(8)]
    kTq = [dscr(f"kTq{s}", [NH, 128, 1152]) for s in range(2)]
    Vq = [dscr(f"Vq{s}", [1152, 1024]) for s in range(2)]
    kTq_res = [[Res(f"kTq{s}_{i}") for i in range(3)] for s in range(2)]
    Vq_res = [[Res(f"Vq{s}_{i}") for i in range(3)] for s in range(2)]

    def region(name, kb):
        return es.enter_context(nc.sbuf_tensor(name, [128, int(kb * 512)], BF16))

    class Carver:
        def __init__(self, reg, name):
            self.reg, self.name, self.off = reg, name, 0
            self.bufs = []

        def reset(self):
            self.off = 0

        def take(self, name, free_shape, dt):
            esz = 4 if dt == F32 else 2
            n = int(np.prod(free_shape))
            self.off = (self.off + 7) // 8 * 8
            if dt == F32:
                v = self.reg.bitcast(F32)[:, self.off // 4: self.off // 4 + n]
            else:
                v = self.reg[:, self.off // 2: self.off // 2 + n]
            self.off += n * esz
            assert self.off <= self.reg.shape[1] * 2, (self.name, name, self.off)
            if len(free_shape) == 2:
                v = v.rearrange("p (a b) -> p a b", a=free_shape[0])
            elif len(free_shape) == 3:
                v = v.rearrange("p (a b c) -> p a b c", a=free_shape[0], b=free_shape[1])
            b = Buf(v, name)
            self.bufs.append(b)
            return b

    RC = Carver(region("RC", 21), "RC")
    RX = Carver(region("RX", 32.5), "RX")
    RW = Carver(region("RW", 48), "RW")
    RN = Carver(region("RN", 16.25), "RN")
    RA = Carver(region("RA", 44.5), "RA")
    RM = Carver(region("RM", 20), "RM")

    pp = [es.enter_context(nc.psum_tensor(f"pp{i}", [128, 1024], F32)) for i in range(4)]
    psum = [Buf(pp[i // 2][:, (i % 2) * 512:(i % 2 + 1) * 512], f"ps{i}") for i in range(8)]
    psum_b = [pp[i // 2].bitcast(BF16)[:, (i % 2) * 1024:(i % 2 + 1) * 1024] for i in range(8)]
    for p in psum:
        p.res.excl = True
    bank_ctr = [0]

    def next_bank():
        b = bank_ctr[0] % 8
        bank_ctr[0] += 1
        return b

    def rs(xs):
        out = []
        for x in xs:
            if x is None:
                continue
            out.append(x.res if isinstance(x, Buf) else x)
        return out

    def E(eng, fn, R=(), W=(), WM=()):
        return P.emit(eng, fn, rs(R), rs(W), rs(WM))

    def dma_load(dst, dst_ap, src_ap, R=(), **kw):
        if kw.get("allow_slow_non_contiguous") and "slow" in SKIP:
            return E("pool", lambda e: e.memset(dst_ap, 0.5), W=[dst])
        ch = P.chan_of(dst.res)
        return P.emit("sp", lambda e: e.dma_start(out=dst_ap, in_=src_ap, **kw), rs(R), [dst.res], (), ch)

    def dma_store(src, dst_ap, src_ap, W=(), WM=(), R=(), **kw):
        if kw.get("allow_slow_non_contiguous") and "slow" in SKIP:
            return None
        ch = P.chan_of(src.res)
        return P.emit("sp", lambda e: e.dma_start(out=dst_ap, in_=src_ap, **kw), rs([src] + list(R)), rs(W), rs(WM), ch)

    cast_rr = [0]

    def cast_copy(out_ap, in_ap, R, W, engs=("act", "dve")):
        eng = engs[cast_rr[0] % len(engs)]
        cast_rr[0] += 1
        if eng == "act":
            E("act", lambda e: e.copy(out=out_ap, in_=in_ap), R, W)
        elif eng == "dve":
            E("dve", lambda e: e.tensor_copy(out=out_ap, in_=in_ap), R, W)
        else:
            E("pool", lambda e: e.tensor_copy(out=out_ap, in_=in_ap), R, W)

    ident = RC.take("ident", [128], BF16)
    ones = RC.take("ones", [128], BF16)
    iot = RC.take("iota", [128], F32)
    g1c = RC.take("g1c", [16], F32)
    g2c = RC.take("g2c", [16], F32)
    gfb = RC.take("gfb", [D], F32)
    cwc = RC.take("cwc", [3, 8], F32)
    fwc = RC.take("fwc", [3, 88], F32)
    identf = RC.take("identf", [128], F32)
    cst = RC.take("cst", [128], F32)
    cst2 = RC.take("cst2", [128], F32)
    chv = RC.take("chv", [NH], F32)
    nchv = RC.take("nchv", [NH], F32)
    cpre = RC.take("cpre", [NH], F32)
    pmk = RC.take("pmk", [1], F32)
    onespre = RC.take("onespre", [128], BF16)
    selA = RC.take("selA", [128], F32)
    selB = RC.take("selB", [128], F32)
    lamv = RC.take("lamv", [4, 64], F32)
    lamt = RC.take("lamt", [8], F32)
    nlam = RC.take("nlam", [1], F32)
    gsub = RC.take("gsub", [1], F32)
    EBd = RC.take("EBd", [NH, 128], BF16)
    EBp = RC.take("EBp", [NH, 128], BF16)
    uph = RC.take("uph", [88, 2], F32)
    uphq = RC.take("uphq", [88, 2, 2], F32)
    uhalo = RC.take("uhalo", [8, 2], F32)
    uhaloq = RC.take("uhaloq", [8, 2, 2], F32)
    ssq = RC.take("ssq", [10], F32)
    rstd = RC.take("rstd", [10], F32)

    def load_cols(dst, dst_ap, src_rows, n):
        dma_load(cst, cst.ap[0:n, :], src_rows)
        bk = next_bank()
        E("pe", lambda e: e.transpose(psum[bk].ap[:, 0:n], cst.ap[0:n, :], identf.ap[0:n, 0:n]), R=[cst, identf], W=[psum[bk]])
        E("dve", lambda e: e.tensor_copy(out=dst_ap, in_=psum[bk].ap[:, 0:n]), R=[psum[bk]], W=[dst])

    def store_cols(src, src_ap, dst_rows, n):
        E("dve", lambda e: e.tensor_copy(out=cst2.ap[:, 0:n], in_=src_ap), R=[src], W=[cst2])
        bk = next_bank()
        E("pe", lambda e: e.transpose(psum[bk].ap[0:n, 0:128], cst2.ap[:, 0:n], identf.ap), R=[cst2, identf], W=[psum[bk]])
        E("act", lambda e: e.copy(out=cst.ap[0:n, :], in_=psum[bk].ap[0:n, 0:128]), R=[psum[bk]], W=[cst])
        dma_store(cst, dst_rows, cst.ap[0:n, :])

    def setup_consts():
        E("pool", lambda e: e.iota(iot.ap, pattern=[[1, 128]], base=0, channel_multiplier=-1,
                                   allow_small_or_imprecise_dtypes=True), W=[iot])
        E("dve", lambda e: e.tensor_scalar(out=ident.ap, in0=iot.ap, scalar1=0.0, scalar2=None, op0=ALU.is_equal),
          R=[iot], W=[ident])
        E("dve", lambda e: e.tensor_scalar(out=identf.ap, in0=iot.ap, scalar1=0.0, scalar2=None, op0=ALU.is_equal),
          R=[iot], W=[identf])
        E("pool", lambda e: e.memset(ones.ap, 1.0), W=[ones])
        E("pool", lambda e: e.memset(selA.ap, 0.0), W=[selA])
        E("pool", lambda e: e.memset(selA.ap[0:1, :], 1.0), W=[selA])
        E("pool", lambda e: e.memset(selB.ap, 0.0), W=[selB])
        E("pool", lambda e: e.memset(selB.ap[32:33, :], 1.0), W=[selB])
        E("pool", lambda e: e.memset(uph.ap, 0.0), W=[uph])
        E("pool", lambda e: e.memset(uhalo.ap, 0.0), W=[uhalo])
        load_cols(g1c, g1c.ap, g1_d.ap().rearrange("(kc p) -> kc p", p=128), 16)
        load_cols(g2c, g2c.ap, g2_d.ap().rearrange("(kc p) -> kc p", p=128), 16)
        dma_load(gfb, gfb.ap, gf_d.ap().partition_broadcast(128))
        for jj in range(3):
            load_cols(cwc, cwc.ap[:, jj, :], cw_d.ap()[jj, :].rearrange("(c p) -> c p", p=128), 8)
            load_cols(fwc, fwc.ap[:, jj, :], fw_d.ap()[jj, :].rearrange("(c p) -> c p", p=128), 88)
        dma_load(chv, chv.ap, relb_d.ap()[15, :].partition_broadcast(128))
        dma_load(pmk, pmk.ap, premask_d.ap())
        for i in range(4):
            dma_load(lamv, lamv.ap[:, i, :], lam_d[i].ap().partition_broadcast(128))
        dma_load(gsub, gsub.ap, subg_d.ap().rearrange("(p o) -> p o", o=1))
        for s in range(2):
            for jj in range(2):
                load_cols(uhaloq, uhaloq.ap[:, :, s, jj], st_mix.ap()[s, jj, :].rearrange("(c p) -> c p", p=128), 8)
                load_cols(uphq, uphq.ap[:, :, s, jj], st_ffn.ap()[s, jj, :].rearrange("(c p) -> c p", p=128), 88)
        E("dve", lambda e: e.tensor_scalar(out=nchv.ap, in0=chv.ap, scalar1=-1.0, scalar2=None, op0=ALU.mult),
          R=[chv], W=[nchv])
        E("dve", lambda e: e.tensor_scalar(out=onespre.ap, in0=ones.ap, scalar1=pmk.ap[:, 0:1], scalar2=None, op0=ALU.mult),
          R=[ones, pmk], W=[onespre])
        E("dve", lambda e: e.tensor_scalar(out=gsub.ap, in0=gsub.ap, scalar1=0.8, scalar2=None, op0=ALU.mult),
          R=[gsub], W=[gsub])
        E("dve", lambda e: e.tensor_tensor(out=lamv.ap[:, 0, :], in0=lamv.ap[:, 0, :], in1=lamv.ap[:, 1, :], op=ALU.mult),
          R=[lamv], W=[lamv])
        E("dve", lambda e: e.tensor_tensor(out=lamv.ap[:, 2, :], in0=lamv.ap[:, 2, :], in1=lamv.ap[:, 3, :], op=ALU.mult),
          R=[lamv], W=[lamv])
        E("dve", lambda e: e.tensor_reduce(out=lamt.ap[:, 0:1], in_=lamv.ap[:, 0, :], axis=AX.X, op=ALU.add), R=[lamv], W=[lamt])
        E("dve", lambda e: e.tensor_reduce(out=lamt.ap[:, 1:2], in_=lamv.ap[:, 2, :], axis=AX.X, op=ALU.add), R=[lamv], W=[lamt])
        E("act", lambda e: e.activation(out=lamt.ap[:, 2:4], in_=lamt.ap[:, 0:2], func=AF.Exp), R=[lamt], W=[lamt])
        E("dve", lambda e: e.tensor_tensor(out=nlam.ap, in0=lamt.ap[:, 3:4], in1=lamt.ap[:, 2:3], op=ALU.subtract), R=[lamt], W=[nlam])
        E("dve", lambda e: e.tensor_scalar(out=nlam.ap, in0=nlam.ap, scalar1=-0.2, scalar2=None, op0=ALU.add), R=[nlam], W=[nlam])
        RA.reset()
        bst = RA.take("bst", [NH, 128], F32)
        bst2 = RA.take("bst2", [NH, 128], F32)
        mst = RA.take("mst", [128], F32)
        dma_load(bst, bst.ap, bd_d.ap().rearrange("h k q -> k h q"))
        dma_load(bst2, bst2.ap, bp_d.ap().rearrange("h k q -> k h q"))
        dma_load(mst, mst.ap, mk_d.ap())
        for h in range(NH):
            E("act", lambda e, h=h: e.activation(out=bst.ap[:, h, :], in_=bst.ap[:, h, :], func=AF.Exp,
                                                 bias=nchv.ap[:, h:h + 1], scale=1.0), R=[bst, nchv], W=[bst])
            E("dve", lambda e, h=h: e.tensor_tensor(out=EBd.ap[:, h, :], in0=bst.ap[:, h, :], in1=mst.ap, op=ALU.mult),
              R=[bst, mst], W=[EBd])
            E("act", lambda e, h=h: e.activation(out=EBp.ap[:, h, :], in_=bst2.ap[:, h, :], func=AF.Exp,
                                                 bias=nchv.ap[:, h:h + 1], scale=1.0), R=[bst2, nchv], W=[EBp])
        return [bst, bst2, mst]

    RV = Carver(region("RV", 24), "RV")

    class Conv:
        def __init__(self):
            self.a = [RV.take(f"cv32_{i}", [4, 512], F32) for i in range(3)]
            self.b = []
            self.i = 0

        def advance(self, k):
            pass

    class Slabs:
        def __init__(self, sched):
            self.sched = sched
            self.rec = []
            self.i = 0
            self.loaded = 0
            self.converted = set()
            RW.reset()
            self.slots = [RW.take(f"slab{i}", [16, 512], BF16) for i in range(3)]

        def _load(self, j):
            name, s, k0, nk = self.sched[j]
            slot = self.slots[j % 3]
            key = (name, s, k0)
            if key in self.converted:
                dma_load(slot, slot.ap[:, 0:nk, :], wb[name].ap()[s, :, k0:k0 + nk, :],
                         R=[wb_res[name][s][k] for k in range(k0, k0 + nk)])
                return
            self.converted.add(key)
            kk = 0
            while kk < nk:
                n = min(4, nk - kk)
                a_ = CONV.a[CONV.i % 3]
                CONV.i += 1
                dma_load(a_, a_.ap[:, 0:n, :],
                         wd[name].ap()[(k0 + kk) * 128:(k0 + kk + n) * 128, s * 512:(s + 1) * 512].rearrange("(k p) c -> p k c", p=128))
                eng = ("act", "dve")[cast_rr[0] % 2]
                cast_rr[0] += 1
                oap, iap = slot.ap[:, kk:kk + n, :], a_.ap[:, 0:n, :]
                first = (kk == 0)
                if eng == "act":
                    P.emit("act", lambda e, oap=oap, iap=iap: e.copy(out=oap, in_=iap), [a_.res], [slot.res] if first else [], [] if first else [slot.res])
                else:
                    P.emit("dve", lambda e, oap=oap, iap=iap: e.tensor_copy(out=oap, in_=iap), [a_.res], [slot.res] if first else [], [] if first else [slot.res])
                kk += n
            dma_store(slot, wb[name].ap()[s, :, k0:k0 + nk, :], slot.ap[:, 0:nk, :], W=[wb_res[name][s][k] for k in range(k0, k0 + nk)])

        def get(self, name, s, k0, nk):
            d = (name, s, k0, nk)
            if self.sched is None:
                self.rec.append(d)
                return self.slots[0]
            assert self.sched[self.i] == d, (self.i, self.sched[self.i], d)
            while self.loaded < min(len(self.sched), self.i + 3):
                self._load(self.loaded)
                self.loaded += 1
            slot = self.slots[self.i % 3]
            self.i += 1
            return slot

    FW = 516

    def alloc_block_bufs():
        B = {}
        RX.reset()
        B["xs"] = [RX.take(f"xs{t}", [D], F32) for t in range(4)]
        RX.reset()
        B["bT"] = [RX.take(f"bT{j}", [FW], BF16) for j in range(8)]
        B["cT"] = [RX.take(f"cT{j}", [FW], BF16) for j in range(8)]
        B["uT"] = [RX.take(f"uT{j}", [520], BF16) for j in range(8)]
        B["obT"] = [RX.take(f"obT{j}", [FW], BF16) for j in range(8)]
        alias([b.res for b in B["xs"]], [b.res for k in ("bT", "cT", "uT", "obT") for b in B[k]])
        RN.reset()
        B["nT"] = [RN.take(f"nT{kc}", [FW], BF16) for kc in range(16)]
        RA.reset()
        B["xn"] = [RA.take(f"xn{t}", [D], BF16) for t in range(4)]
        B["junk"] = RA.take("junk", [D], BF16)
        B["xnh"] = RA.take("xnh", [D], BF16)
        roleA1 = [b.res for b in B["xn"]] + [B["junk"].res, B["xnh"].res]
        RA.reset()
        B["qT"] = [RA.take(f"qT{h}", [FW], BF16) for h in range(NH)]
        B["mergedT"] = [RA.take(f"mg{m}", [FW], BF16) for m in range(16)]
        off_m = RA.off
        RA.off = 8 * FW * 2
        B["kTc"] = [RA.take(f"kTc{i}", [1024], BF16) for i in range(3)]
        B["Vc"] = [RA.take(f"Vc{i}", [8, 128], BF16) for i in range(3)]
        B["PP"] = [RA.take(f"PP{i}", [2, 512], BF16) for i in range(2)]
        B["r"] = [RA.take(f"r{m}", [512], F32) for m in range(2)]
        B["o"] = [RA.take(f"o{m}", [512], F32) for m in range(2)]
        B["sq"] = RA.take("sq", [512], BF16)
        B["rz"] = RA.take("rz", [512], F32)
        att = [B["rz"].res] + [b.res for k in ("kTc", "Vc", "r", "o") for b in B[k]] + [b.res for b in B["PP"]] + [B["sq"].res]
        roleA2 = [b.res for b in B["qT"]] + [b.res for b in B["mergedT"]] + att
        alias(att, [b.res for b in B["mergedT"]])
        RA.reset()
        B["actT"] = [RA.take(f"act{c}", [FW], BF16) for c in range(44)]
        roleA3 = [b.res for b in B["actT"]]
        alias(roleA1, roleA2)
        alias(roleA1, roleA3)
        alias(roleA2, roleA3)
        RM.reset()
        B["kf"] = [RM.take(f"kf{i}", [512], F32) for i in range(2)]
        B["kb"] = [RM.take(f"kb{i}", [512], BF16) for i in range(2)]
        B["kTst"] = [RM.take(f"kTst{i}", [4, 128], BF16) for i in range(2)]
        B["ztmp"] = [RM.take(f"z{i}", [512], F32) for i in range(2)]
        B["ust"] = RM.take("ust", [8, 2, 2], F32)
        roleM1 = [b.res for k in ("kf", "kb", "kTst", "ztmp") for b in B[k]] + [B["ust"].res]
        RM.reset()
        B["oT"] = [RM.take(f"oT{h}", [512], BF16) for h in range(NH)]
        B["sg"] = [RM.take(f"sg{i}", [512], BF16) for i in range(2)]
        B["mtmp"] = [RM.take(f"mtmp{i}", [512], BF16) for i in range(2)]
        roleM2 = [b.res for k in ("oT", "sg", "mtmp") for b in B[k]]
        RM.reset()
        B["ctmp"] = [RM.take(f"ct{i}", [512], F32) for i in range(4)]
        roleM3 = [b.res for b in B["ctmp"]]
        alias(roleM1, roleM2)
        alias(roleM1, roleM3)
        alias(roleM2, roleM3)
        return B

    def rms_stats(g, src, t, col):
        pt = g.pts[t]
        E("act", lambda e: e.activation(out=Bk["junk"].ap[0:pt, :], in_=src.ap[0:pt, :], func=AF.Square,
                                        accum_out=ssq.ap[0:pt, col:col + 1]), R=[src], W=[Bk["junk"], ssq])
        E("dve", lambda e: e.tensor_scalar(out=rstd.ap[0:pt, col:col + 1], in0=ssq.ap[0:pt, col:col + 1],
                                           scalar1=1.0 / D, scalar2=EPS, op0=ALU.mult, op1=ALU.add), R=[ssq], W=[rstd])
        E("act", lambda e: e.activation(out=rstd.ap[0:pt, col:col + 1], in_=rstd.ap[0:pt, col:col + 1], func=AF.Sqrt),
          R=[rstd], W=[rstd])
        E("dve", lambda e: e.reciprocal(out=rstd.ap[0:pt, col:col + 1], in_=rstd.ap[0:pt, col:col + 1]), R=[rstd], W=[rstd])

    def norm_transpose(g, gcol):
        c0, N = g.c0, g.N
        for t in range(g.nt):
            pt = g.pts[t]
            xs, xn = g.xs[t], g.xn[t]
            rms_stats(g, xs, t, g.scol + t)
            E("dve", lambda e, xs=xs, xn=xn, t=t, pt=pt: e.tensor_scalar(out=xn.ap[0:pt, :], in0=xs.ap[0:pt, :],
                                                                        scalar1=rstd.ap[0:pt, g.scol + t:g.scol + t + 1], scalar2=None, op0=ALU.mult),
              R=[xs, rstd], W=[xn])
        for kc in range(16):
            bk = next_bank()
            for t in range(g.nt):
                pt = g.pts[t]
                xn = g.xn[t]
                E("pe", lambda e, bk=bk, t=t, pt=pt, kc=kc, xn=xn: e.transpose(psum_b[bk][:, t * 128:t * 128 + pt],
                                                                               xn.ap[0:pt, kc * 128:(kc + 1) * 128],
                                                                               ident.ap[0:pt, 0:pt]),
                  R=[xn, ident], W=[psum[bk]])
            if kc % 2 == 0:
                E("dve", lambda e, bk=bk, kc=kc: e.tensor_scalar(out=Bk["nT"][kc].ap[:, c0:c0 + N], in0=psum_b[bk][:, 0:N],
                                                                 scalar1=gcol.ap[:, kc:kc + 1], scalar2=None, op0=ALU.mult),
                  R=[psum[bk], gcol], W=[Bk["nT"][kc]])
            else:
                E("act", lambda e, bk=bk, kc=kc: e.activation(out=Bk["nT"][kc].ap[:, c0:c0 + N], in_=psum_b[bk][:, 0:N],
                                                              func=AF.Copy, scale=gcol.ap[:, kc:kc + 1]),
                  R=[psum[bk], gcol], W=[Bk["nT"][kc]])

    def formA(g, slot, nk, j, rhs):
        bk = next_bank()
        c0, N = g.c0, g.N
        for kc in range(nk):
            E("pe", lambda e, bk=bk, kc=kc: e.matmul(psum[bk].ap[:, 0:N], lhsT=slot.ap[:, kc, j * 128:(j + 1) * 128],
                                                     rhs=rhs[kc].ap[:, c0:c0 + N], start=(kc == 0), stop=(kc == nk - 1)),
              R=[slot, rhs[kc]], W=[psum[bk]])
        return bk

    def formB(g, slot, k0, nk, lhs, t, bk, first, last):
        pt = g.pts[t]
        c0 = g.c0
        for kc in range(nk):
            E("pe", lambda e, kc=kc: e.matmul(psum[bk].ap[0:pt, :], lhsT=lhs[k0 + kc].ap[:, c0 + t * 128:c0 + t * 128 + pt],
                                              rhs=slot.ap[:, kc, :], start=(first and kc == 0), stop=(last and kc == nk - 1)),
              R=[slot, lhs[k0 + kc]], W=[psum[bk]])

    def inproj_conv(gs, halos):
        for si, s in enumerate(SL_B):
            slot = SLB.get("w_in", s, 0, 16)
            for j in range(4):
                c = si * 4 + j
                for g in gs:
                    bk = formA(g, slot, 16, j, Bk["nT"])
                    E("act", lambda e, bk=bk, c=c, g=g: e.copy(out=Bk["bT"][c].ap[:, g.c0:g.c0 + g.N], in_=psum[bk].ap[:, 0:g.N]),
                      R=[psum[bk]], W=[Bk["bT"][c]])
        for si, s in enumerate(SL_C):
            slot = SLB.get("w_in", s, 0, 16)
            for j in range(4):
                c = si * 4 + j
                for g in gs:
                    bk = formA(g, slot, 16, j, Bk["nT"])
                    E("act", lambda e, bk=bk, c=c, g=g: e.copy(out=Bk["cT"][c].ap[:, g.c0:g.c0 + g.N], in_=psum[bk].ap[:, 0:g.N]),
                      R=[psum[bk]], W=[Bk["cT"][c]])
        for si, s in enumerate(SL_X):
            slot = SLB.get("w_in", s, 0, 16)
            for j in range(4):
                c = si * 4 + j
                for g, u_halo_src in zip(gs, halos):
                    nseg, L, N, c0 = g.nseg, g.L, g.N, g.c0
                    W2 = L + 2
                    bk = formA(g, slot, 16, j, Bk["nT"])
                    u3 = Bk["uT"][c].ap[:, g.u0:g.u0 + nseg * W2].rearrange("p (s w) -> p s w", w=W2)
                    ps3 = psum[bk].ap[:, 0:N].rearrange("p (s w) -> p s w", w=L)
                    c3 = Bk["cT"][c].ap[:, c0:c0 + N].rearrange("p (s w) -> p s w", w=L)
                    hsrc = u_halo_src.ap[:, c] if nseg > 1 else u_halo_src.ap[:, c, :].rearrange("p (s w) -> p s w", s=1)
                    E("act", lambda e, u3=u3, hsrc=hsrc: e.copy(out=u3[:, :, 0:2], in_=hsrc), R=[u_halo_src], W=[Bk["uT"][c]])
                    E("dve", lambda e, u3=u3, ps3=ps3, c3=c3, W2=W2: e.tensor_tensor(out=u3[:, :, 2:W2], in0=ps3, in1=c3, op=ALU.mult),
                      R=[psum[bk], Bk["cT"][c]], W=[Bk["uT"][c]])
                    E("act", lambda e, u3=u3, hsrc=hsrc, L=L, W2=W2: e.copy(out=hsrc, in_=u3[:, :, L:W2]), R=[Bk["uT"][c]], W=[u_halo_src])
                    z = Bk["ztmp"][c % 2]
                    z3 = z.ap[:, 0:N].rearrange("p (s w) -> p s w", w=L)
                    E("dve", lambda e, z3=z3, u3=u3, c=c, L=L: e.tensor_scalar(out=z3, in0=u3[:, :, 0:L], scalar1=cwc.ap[:, 0, c:c + 1],
                                                                             scalar2=None, op0=ALU.mult), R=[Bk["uT"][c], cwc], W=[z])
                    E("dve", lambda e, z3=z3, u3=u3, c=c, L=L: e.scalar_tensor_tensor(out=z3, in0=u3[:, :, 1:L + 1], scalar=cwc.ap[:, 1, c:c + 1],
                                                                                    in1=z3, op0=ALU.mult, op1=ALU.add),
                      R=[Bk["uT"][c], cwc, z], W=[z])
                    E("dve", lambda e, z3=z3, u3=u3, c=c, L=L: e.scalar_tensor_tensor(out=z3, in0=u3[:, :, 2:L + 2], scalar=cwc.ap[:, 2, c:c + 1],
                                                                                    in1=z3, op0=ALU.mult, op1=ALU.add),
                      R=[Bk["uT"][c], cwc, z], W=[z])
                    E("pool", lambda e, z=z, c=c, N=N, c0=c0: e.tensor_tensor(out=Bk["obT"][c].ap[:, c0:c0 + N], in0=z.ap[:, 0:N],
                                                                            in1=Bk["bT"][c].ap[:, c0:c0 + N], op=ALU.mult),
                      R=[z, Bk["bT"][c]], W=[Bk["obT"][c]])

    def inproj_q(gs):
        for si, s in enumerate(SL_Q):
            slot = SLB.get("w_in", s, 0, 16)
            for j in range(4):
                h = si * 4 + j
                for g in gs:
                    bk = formA(g, slot, 16, j, Bk["nT"])
                    E("act", lambda e, bk=bk, h=h, g=g: e.copy(out=Bk["qT"][h].ap[:, g.c0:g.c0 + g.N], in_=psum[bk].ap[:, 0:g.N]),
                      R=[psum[bk]], W=[Bk["qT"][h]])

    def inproj_kv(g, kout, vout, row0, kdst, vdst):
        for si, s in enumerate(SL_K):
            slot = SLB.get("w_in", s, 0, 16)
            for t in range(g.nt):
                pt = g.pts[t]
                bk = next_bank()
                formB(g, slot, 0, 16, Bk["nT"], t, bk, True, True)
                i = (si * g.nt + t) % 2
                kf, kb, kst = Bk["kf"][i], Bk["kb"][i], Bk["kTst"][i]
                if kout is not None:
                    E("act", lambda e, bk=bk, kf=kf, pt=pt: e.copy(out=kf.ap[0:pt, :], in_=psum[bk].ap[0:pt, :]), R=[psum[bk]], W=[kf])
                    dma_store(kf, kout[row0 + t * 128: row0 + t * 128 + pt, si * 512:(si + 1) * 512], kf.ap[0:pt, :])
                E("dve", lambda e, bk=bk, kb=kb, pt=pt: e.tensor_copy(out=kb.ap[0:pt, :], in_=psum[bk].ap[0:pt, :]), R=[psum[bk]], W=[kb])
                bk2 = next_bank()
                for hh in range(4):
                    E("pe", lambda e, bk2=bk2, hh=hh, kb=kb, pt=pt: e.transpose(psum_b[bk2][:, hh * 128:hh * 128 + pt],
                                                                             kb.ap[0:pt, hh * 128:(hh + 1) * 128], ident.ap[0:pt, 0:pt]),
                      R=[kb, ident], W=[psum[bk2]])
                E("act", lambda e, bk2=bk2, kst=kst, pt=pt: e.copy(out=kst.ap[:, :, 0:pt],
                                                                  in_=psum_b[bk2][:, 0:512].rearrange("p (h k) -> p h k", k=128)[:, :, 0:pt]),
                  R=[psum[bk2]], W=[kst])
                for (dap, a_, b_, res) in kdst(t, pt, si):
                    dma_store(kst, dap, kst.ap[:, :, a_:b_], WM=[res])
        for si, s in enumerate(SL_V):
            slot = SLB.get("w_in", s, 0, 16)
            for t in range(g.nt):
                pt = g.pts[t]
                bk = next_bank()
                formB(g, slot, 0, 16, Bk["nT"], t, bk, True, True)
                i = (si * g.nt + t) % 2
                vf, vb = Bk["kf"][i], Bk["kb"][i]
                if vout is not None:
                    E("act", lambda e, bk=bk, vf=vf, pt=pt: e.copy(out=vf.ap[0:pt, :], in_=psum[bk].ap[0:pt, :]), R=[psum[bk]], W=[vf])
                    dma_store(vf, vout[row0 + t * 128: row0 + t * 128 + pt, si * 512:(si + 1) * 512], vf.ap[0:pt, :])
                E("dve", lambda e, bk=bk, vb=vb, pt=pt: e.tensor_copy(out=vb.ap[0:pt, :], in_=psum[bk].ap[0:pt, :]), R=[psum[bk]], W=[vb])
                for (dap, a_, b_, res) in vdst(t, pt, si):
                    dma_store(vb, dap, vb.ap[a_:b_, :], WM=[res])

    def attention(g, q0, q1, kT_d, V_d, kres, vres, nkeys, tile_info):
        nq = q1 - q0
        nkt = (nkeys + 127) // 128
        nch = (nkeys + 1023) // 1024
        O = [psum[4], psum[5]]
        ZB = psum[6]
        EP = psum[7]
        r, o, sq, rz = Bk["r"], Bk["o"], Bk["sq"], Bk["rz"]
        pending = []

        def make_epilogue(h):
            def st0():
                E("dve", lambda e: e.tensor_copy(out=o[0].ap[:, 0:nq], in_=O[0].ap[:, 0:nq]), R=[O[0]], W=[o[0]])
                E("dve", lambda e: e.tensor_copy(out=o[1].ap[:, 0:nq], in_=O[1].ap[:, 0:nq]), R=[O[1]], W=[o[1]])
                E("dve", lambda e: e.tensor_scalar(out=rz.ap[0:64, 0:nq], in0=ZB.ap[0:64, 0:nq], scalar1=1e-30, scalar2=None, op0=ALU.max),
                  R=[ZB], W=[rz])
                E("dve", lambda e: e.reciprocal(out=rz.ap[0:64, 0:nq], in_=rz.ap[0:64, 0:nq]), R=[rz], W=[rz])

            def st1():
                E("pe", lambda e: e.matmul(EP.ap[:, 0:nq], lhsT=selA.ap[0:64, :], rhs=rz.ap[0:64, 0:nq], start=True, stop=True),
                  R=[selA, rz], W=[EP])
                E("dve", lambda e: e.tensor_tensor(out=o[0].ap[:, 0:nq], in0=o[0].ap[:, 0:nq], in1=EP.ap[:, 0:nq], op=ALU.mult),
                  R=[o[0], EP], W=[o[0]])

            def st2():
                E("pe", lambda e: e.matmul(EP.ap[:, 0:nq], lhsT=selB.ap[0:64, :], rhs=rz.ap[0:64, 0:nq], start=True, stop=True),
                  R=[selB, rz], W=[EP])
                E("dve", lambda e: e.tensor_tensor(out=o[1].ap[:, 0:nq], in0=o[1].ap[:, 0:nq], in1=EP.ap[:, 0:nq], op=ALU.mult),
                  R=[o[1], EP], W=[o[1]])
                E("dve", lambda e: e.scalar_tensor_tensor(out=o[0].ap[:, 0:nq], in0=o[1].ap[:, 0:nq], scalar=nlam.ap[:, 0:1], in1=o[0].ap[:, 0:nq],
                                                          op0=ALU.mult, op1=ALU.add), R=[o[0], o[1], nlam], W=[o[0]])
                E("dve", lambda e: e.tensor_tensor(out=sq.ap[:, 0:nq], in0=o[0].ap[:, 0:nq], in1=o[0].ap[:, 0:nq], op=ALU.mult), R=[o[0]], W=[sq])

            def st3():
                E("pe", lambda e: e.matmul(EP.ap[:, 0:nq], lhsT=ones.ap, rhs=sq.ap[:, 0:nq], start=True, stop=True), R=[ones, sq], W=[EP])
                E("dve", lambda e: e.tensor_scalar(out=r[0].ap[:, 0:nq], in0=EP.ap[:, 0:nq], scalar1=1.0 / 128, scalar2=EPS, op0=ALU.mult, op1=ALU.add),
                  R=[EP], W=[r[0]])
                E("act", lambda e: e.activation(out=r[0].ap[:, 0:nq], in_=r[0].ap[:, 0:nq], func=AF.Ln), R=[r[0]], W=[r[0]])
                E("act", lambda e: e.activation(out=r[0].ap[:, 0:nq], in_=r[0].ap[:, 0:nq], func=AF.Exp, scale=-0.5), R=[r[0]], W=[r[0]])
                E("dve", lambda e: e.scalar_tensor_tensor(out=Bk["oT"][h].ap[:, q0:q1], in0=o[0].ap[:, 0:nq], scalar=gsub.ap[:, 0:1],
                                                          in1=r[0].ap[:, 0:nq], op0=ALU.mult, op1=ALU.mult),
                  R=[o[0], gsub, r[0]], W=[Bk["oT"][h]])
            return [st0, st1, st2, st3]

        def load_chunk(ci, h):
            k0 = ci * 1024
            n = min(1024, nkeys - k0)
            kc_, vc_ = Bk["kTc"][ld_ctr[0] % 3], Bk["Vc"][ld_ctr[0] % 3]
            ld_ctr[0] += 1
            dma_load(kc_, kc_.ap[:, 0:n], kT_d.ap()[h, :, k0:k0 + n], R=kres(ci))
            nfull = n // 128
            if nfull > 0:
                dma_load(vc_, vc_.ap[:, 0:nfull, :],
                         V_d.ap()[k0:k0 + nfull * 128, h * 128:(h + 1) * 128].rearrange("(kt p) d -> p kt d", p=128),
                         R=vres(ci))
            rem = n - nfull * 128
            if rem > 0:
                dma_load(vc_, vc_.ap[0:rem, nfull, :], V_d.ap()[k0 + nfull * 128:k0 + n, h * 128:(h + 1) * 128], R=vres(ci))
            return kc_, vc_

        first_chunk = load_chunk(0, 0)
        for h in range(NH):
            chunks = {0: first_chunk}
            first_chunk = None
            pend = None
            for kt in range(nkt + 1):
                cur = None
                if kt < nkt:
                    ci = kt // 8
                    if kt % 8 == 0:
                        if ci + 1 < nch:
                            chunks[ci + 1] = load_chunk(ci + 1, h)
                        elif h + 1 < NH:
                            first_chunk = load_chunk(0, h + 1)
                    kc_, vc_ = chunks[ci]
                    kl = kt % 8
                    kp = min(128, nkeys - kt * 128)
                    qa, is_pre, specials = tile_info(kt, h)
                    pi = kt % 2
                    sb0, sb1 = psum[2 * pi], psum[2 * pi + 1]
                    for m, sb in enumerate((sb0, sb1)):
                        E("pe", lambda e, sb=sb, m=m, kc_=kc_, kl=kl, kp=kp, qa=qa, h=h: e.matmul(
                            sb.ap[0:kp, qa:nq], lhsT=kc_.ap[64 * m:64 * m + 64, kl * 128:kl * 128 + kp],
                            rhs=Bk["qT"][h].ap[64 * m:64 * m + 64, q0 + qa:q1], start=True, stop=True),
                          R=[kc_, Bk["qT"][h]], W=[sb])
                    PPt = Bk["PP"][pi]
                    sv = pp[pi][0:kp, :].rearrange("p (m w) -> p m w", m=2)[:, :, qa:nq]
                    E("act", lambda e, sv=sv, PPt=PPt, kp=kp, qa=qa: e.activation(
                        out=PPt.ap[0:kp, :, qa:nq], in_=sv, func=AF.Exp, scale=0.125), R=[sb0, sb1], W=[PPt])
                    for (ca, cb, EB, ea, eb) in specials:
                        for m in range(2):
                            E("dve", lambda e, PPt=PPt, m=m, kp=kp, ca=ca, cb=cb, EB=EB, ea=ea, eb=eb, h=h: e.tensor_tensor(
                                out=PPt.ap[0:kp, m, ca:cb], in0=PPt.ap[0:kp, m, ca:cb], in1=EB.ap[0:kp, h, ea:eb], op=ALU.mult),
                              R=[PPt, EB], W=[PPt])
                    cur = (kt, kp, qa, vc_, kl, PPt, onespre if is_pre else ones)
                if pend is not None:
                    pkt, pkp, pqa, pvc, pkl, pP, pones = pend
                    for m in range(2):
                        E("pe", lambda e, m=m, pkp=pkp, pqa=pqa, pvc=pvc, pkl=pkl, pP=pP, pkt=pkt: e.matmul(
                            O[m].ap[:, pqa:nq], lhsT=pvc.ap[0:pkp, pkl, :], rhs=pP.ap[0:pkp, m, pqa:nq],
                            start=(pkt == 0), stop=(pkt == nkt - 1)), R=[pvc, pP], W=[O[m]])
                    for m in range(2):
                        E("pe", lambda e, m=m, pkp=pkp, pqa=pqa, pP=pP, pkt=pkt, pones=pones: e.matmul(
                            ZB.ap[32 * m:32 * m + 32, pqa:nq], lhsT=pones.ap[0:pkp, 0:32], rhs=pP.ap[0:pkp, m, pqa:nq],
                            start=(pkt == 0), stop=(pkt == nkt - 1)), R=[pones, pP], W=[ZB])
                    if pending and pkt >= 1:
                        pending.pop(0)()
                pend = cur
            while pending:
                pending.pop(0)()
            steps = make_epilogue(h)
            steps[0]()
            pending = steps[1:]
        while pending:
            pending.pop(0)()

    def proj_merge(gs):
        for (wname, gsl, src, first) in (("w_proj_a", SL_GA, "oT", True), ("w_proj_b", SL_GB, "obT", False)):
            for si in range(4):
                gslot = SLB.get("w_in", gsl[si], 0, 16)
                for j in range(4):
                    sg = Bk["sg4"][j]
                    for g in gs:
                        bk = formA(g, gslot, 16, j, Bk["nT"])
                        E("act", lambda e, bk=bk, sg=sg, g=g: e.activation(out=sg.ap[:, g.c0:g.c0 + g.N], in_=psum[bk].ap[:, 0:g.N], func=AF.Sigmoid),
                          R=[psum[bk]], W=[sg])
                pslot = SLB.get(wname, si, 0, 8)
                for j in range(4):
                    m = si * 4 + j
                    sg = Bk["sg4"][j]
                    mg = Bk["mergedT"][m]
                    for g in gs:
                        c0, N = g.c0, g.N
                        bk = formA(g, pslot, 8, j, Bk[src])
                        if first:
                            E("dve", lambda e, bk=bk, mg=mg, sg=sg, c0=c0, N=N: e.tensor_tensor(out=mg.ap[:, c0:c0 + N], in0=psum[bk].ap[:, 0:N],
                                                                                              in1=sg.ap[:, c0:c0 + N], op=ALU.mult),
                              R=[psum[bk], sg], W=[mg])
                        else:
                            tmp = Bk["mtmp"][m % 2]
                            E("dve", lambda e, bk=bk, tmp=tmp, sg=sg, c0=c0, N=N: e.tensor_tensor(out=tmp.ap[:, 0:N], in0=psum[bk].ap[:, 0:N],
                                                                                                in1=sg.ap[:, c0:c0 + N], op=ALU.mult),
                              R=[psum[bk], sg], W=[tmp])
                            E("pool", lambda e, mg=mg, tmp=tmp, c0=c0, N=N: e.tensor_tensor(out=mg.ap[:, c0:c0 + N], in0=mg.ap[:, c0:c0 + N],
                                                                                          in1=tmp.ap[:, 0:N], op=ALU.add),
                              R=[mg, tmp], W=[mg])

    def out_proj(gs, xsrcs, row0s):
        for g, xsrc, row0 in zip(gs, xsrcs, row0s):
            for t in range(g.nt):
                pt = g.pts[t]
                xs = g.xs[t]
                dma_load(xs, xs.ap[0:pt, :], xsrc[row0 + t * 128: row0 + t * 128 + pt, :])
        for n in range(4):
            slot = SLB.get("w_o", n, 0, 16)
            for g in gs:
                for t in range(g.nt):
                    pt = g.pts[t]
                    bk = next_bank()
                    formB(g, slot, 0, 16, Bk["mergedT"], t, bk, True, True)
                    xs = g.xs[t]
                    E("dve", lambda e, bk=bk, xs=xs, pt=pt, n=n: e.tensor_tensor(out=xs.ap[0:pt, n * 512:(n + 1) * 512], in0=psum[bk].ap[0:pt, :],
                                                                             in1=xs.ap[0:pt, n * 512:(n + 1) * 512], op=ALU.add),
                      R=[psum[bk], xs], W=[xs])

    def ffn_up(gs, halos, act_for):
        for s in range(22):
            slot = SLB.get("w_up", s, 0, 16)
            for j in range(4):
                c = s * 4 + j
                for gi, (g, halo) in enumerate(zip(gs, halos)):
                    nseg, L, N, c0 = g.nseg, g.L, g.N, g.c0
                    bk = formA(g, slot, 16, j, Bk["nT"])
                    p3 = psum[bk].ap[:, 0:N].rearrange("p (s w) -> p s w", w=L)
                    h3 = halo.ap[:, c] if nseg > 1 else halo.ap[:, c, :].rearrange("p (s w) -> p s w", s=1)
                    if g not in act_for:
                        E("act", lambda e, h3=h3, p3=p3, L=L: e.copy(out=h3, in_=p3[:, :, L - 2:L]), R=[psum[bk]], W=[halo])
                        continue
                    tmp = Bk["ctmp"][c % 4]
                    t3 = tmp.ap[:, 0:N].rearrange("p (s w) -> p s w", w=L)
                    E("act", lambda e, t3=t3, p3=p3, c=c: e.activation(out=t3, in_=p3, func=AF.Copy, scale=fwc.ap[:, 2, c:c + 1]),
                      R=[psum[bk], fwc], W=[tmp])
                    E("dve", lambda e, t3=t3, h3=h3, c=c: e.scalar_tensor_tensor(out=t3[:, :, 0:2], in0=h3, scalar=fwc.ap[:, 0, c:c + 1],
                                                                              in1=t3[:, :, 0:2], op0=ALU.mult, op1=ALU.add),
                      R=[halo, fwc, tmp], W=[tmp])
                    E("dve", lambda e, t3=t3, h3=h3, c=c: e.scalar_tensor_tensor(out=t3[:, :, 0:1], in0=h3[:, :, 1:2], scalar=fwc.ap[:, 1, c:c + 1],
                                                                              in1=t3[:, :, 0:1], op0=ALU.mult, op1=ALU.add),
                      R=[halo, fwc, tmp], W=[tmp])
                    E("act", lambda e, h3=h3, p3=p3, L=L: e.copy(out=h3, in_=p3[:, :, L - 2:L]), R=[psum[bk]], W=[halo])
                    E("dve", lambda e, t3=t3, p3=p3, c=c, L=L: e.scalar_tensor_tensor(out=t3[:, :, 1:L], in0=p3[:, :, 0:L - 1], scalar=fwc.ap[:, 1, c:c + 1],
                                                                                   in1=t3[:, :, 1:L], op0=ALU.mult, op1=ALU.add),
                      R=[psum[bk], fwc, tmp], W=[tmp])
                    E("dve", lambda e, t3=t3, p3=p3, c=c, L=L: e.scalar_tensor_tensor(out=t3[:, :, 2:L], in0=p3[:, :, 0:L - 2], scalar=fwc.ap[:, 0, c:c + 1],
                                                                                   in1=t3[:, :, 2:L], op0=ALU.mult, op1=ALU.add),
                      R=[psum[bk], fwc, tmp], W=[tmp])
                    if c < 44:
                        E("act", lambda e, tmp=tmp, c=c, N=N, c0=c0: e.activation(out=Bk["actT"][c].ap[:, c0:c0 + N], in_=tmp.ap[:, 0:N], func=AF.Silu),
                          R=[tmp], W=[Bk["actT"][c]])
                    else:
                        a_ = Bk["actT"][c - 44]
                        E("pool", lambda e, tmp=tmp, a_=a_, N=N, c0=c0: e.tensor_tensor(out=a_.ap[:, c0:c0 + N], in0=a_.ap[:, c0:c0 + N],
                                                                                      in1=tmp.ap[:, 0:N], op=ALU.mult),
                          R=[tmp, a_], W=[a_])

    def ffn_down(g):
        for n in range(4):
            banks = [(n % 2) * 4 + t for t in range(g.nt)]
            for sub in range(4):
                slot = SLB.get("w_down", n, sub * 11, 11)
                for t in range(g.nt):
                    formB(g, slot, sub * 11, 11, Bk["actT"], t, banks[t], sub == 0, sub == 3)
            for t in range(g.nt):
                pt = g.pts[t]
                xs, bk = g.xs[t], banks[t]
                E("dve", lambda e, bk=bk, xs=xs, pt=pt, n=n: e.tensor_tensor(out=xs.ap[0:pt, n * 512:(n + 1) * 512], in0=psum[bk].ap[0:pt, :],
                                                                         in1=xs.ap[0:pt, n * 512:(n + 1) * 512], op=ALU.add),
                  R=[psum[bk], xs], W=[xs])

    def final_norm(g, yout, row0):
        for t in range(g.nt):
            pt = g.pts[t]
            xs = g.xs[t]
            rms_stats(g, xs, t, 4 + t)
            E("dve", lambda e, xs=xs, pt=pt, t=t: e.scalar_tensor_tensor(out=xs.ap[0:pt, :], in0=xs.ap[0:pt, :], scalar=rstd.ap[0:pt, 4 + t:5 + t],
                                                                        in1=gfb.ap[0:pt, :], op0=ALU.mult, op1=ALU.mult),
              R=[xs, rstd, gfb], W=[xs])
            dma_store(xs, yout[row0 + t * 128: row0 + t * 128 + pt, :], xs.ap[0:pt, :])

    def store_states(g, uh, fh, cm_out, cf_out):
        for s in range(g.nseg):
            for jj in range(2):
                src = uh.ap[:, :, s, jj] if g.nseg > 1 else uh.ap[:, :, jj]
                store_cols(uh, src, cm_out.ap()[s, jj, :].rearrange("(c p) -> c p", p=128), 8)
                src = fh.ap[:, :, s, jj] if g.nseg > 1 else fh.ap[:, :, jj]
                store_cols(fh, src, cf_out.ap()[s, jj, :].rearrange("(c p) -> c p", p=128), 88)

    GM = Group("main", 512, [128] * 4, 1, 512)
    GS = Group("smp", 128, [128], 2, 64)
    GH = Group("halo", 4, [4], 1, 4, c0=512, u0=514, scol=8)

    def load_x(g, src, row0):
        for t in range(g.nt):
            pt = g.pts[t]
            xs = g.xs[t]
            dma_load(xs, xs.ap[0:pt, :], src[row0 + t * 128: row0 + t * 128 + pt, :])

    class CacheImport:
        def __init__(self):
            RV.reset()
            self.k32 = [RV.take(f"ik32_{i}", [1024], F32) for i in range(2)]
            self.v32 = [RV.take(f"iv32_{i}", [1024], F32) for i in range(2)]
            self.k16 = RV.take("ik16", [1024], BF16)
            self.v16 = RV.take("iv16", [1024], BF16)
            self.kst = RV.take("ikst", [NH, 128], BF16)
            self.bufs = self.k32 + self.v32 + [self.k16, self.v16, self.kst]
            self.i = 0

        def left(self):
            return self.i < 16

        def step(self):
            if self.i >= 16:
                return
            s_, kt = self.i // 8, self.i % 8
            k32, v32 = self.k32[self.i % 2], self.v32[self.i % 2]
            k16, v16, kst = self.k16, self.v16, self.kst
            self.i += 1
            dma_load(k32, k32.ap, cache_k.ap()[s_, kt * 128:(kt + 1) * 128, :])
            dma_load(v32, v32.ap, cache_v.ap()[s_, kt * 128:(kt + 1) * 128, :])
            E("pool", lambda e: e.tensor_copy(out=k16.ap, in_=k32.ap), R=[k32], W=[k16])
            E("pool", lambda e: e.tensor_copy(out=v16.ap, in_=v32.ap), R=[v32], W=[v16])
            bk = next_bank()
            for h in range(NH):
                E("pe", lambda e, h=h: e.transpose(psum_b[bk][:, h * 128:(h + 1) * 128], k16.ap[:, h * 128:(h + 1) * 128], ident.ap),
                  R=[k16, ident], W=[psum[bk]])
            E("act", lambda e: e.copy(out=kst.ap, in_=psum_b[bk][:, 0:1024].rearrange("p (h k) -> p h k", k=128)), R=[psum[bk]], W=[kst])
            dma_store(kst, kTq[s_].ap()[:, :, kt * 128:(kt + 1) * 128].rearrange("h p k -> p h k"), kst.ap, WM=[kTq_res[s_][0]])
            dma_store(v16, Vq[s_].ap()[kt * 128:(kt + 1) * 128, :], v16.ap, WM=[Vq_res[s_][0]])

    class Stop(Exception):
        pass

    def chk(name):
        if KSTOP == name:
            raise Stop()

    def program():
        try:
            program_()
        except Stop:
            pass

    def program_():
        chk("consts")
        pre_res_k = lambda ci: [kTs_res[2 * ci], kTs_res[2 * ci + 1]]
        pre_res_v = lambda ci: [Vs_res[2 * ci], Vs_res[2 * ci + 1]]
        for pb in range(4):
            load_x(GM, x_pre.ap(), pb * 512)
            norm_transpose(GM, g1c)

            def kdst(t, pt, si, pb=pb):
                k0 = pb * 512 + t * 128
                return [(kTs.ap()[si * 4:(si + 1) * 4, :, k0:k0 + pt].rearrange("h p k -> p h k"), 0, pt, kTs_res[pb])]

            def vdst(t, pt, si, pb=pb):
                k0 = pb * 512 + t * 128
                return [(Vs.ap()[k0:k0 + pt, si * 512:(si + 1) * 512], 0, pt, Vs_res[pb])]
            inproj_kv(GM, None, None, 0, kdst, vdst)
            CONV.advance(40)
            chk("prefix0")
        chk("prefix")
        def tile_info_h(kt, h):
            if kt == 14:
                return 0, True, [(0, 4, EBp, 124, 128)]
            if kt == 15:
                return 0, True, [(0, 4, EBd, 124, 128)]
            return 0, True, []
        for j in range(4):
            gs = [GH, GM] if j == 0 else [GM]
            if j == 0:
                load_x(GH, x_halo.ap(), 0)
            load_x(GM, x_own.ap(), j * 512)
            imp = (lambda n=2: [IMP.step() for _ in range(n)]) if j == 1 else (lambda n=2: None)
            for g in gs:
                norm_transpose(g, g1c)
            imp()
            inproj_conv(gs, [uhalo] * len(gs))
            imp()
            inproj_q(gs)
            imp()

            def kdst(t, pt, si, j=j):
                k0 = 2048 + j * 512 + t * 128
                return [(kTs.ap()[si * 4:(si + 1) * 4, :, k0:k0 + pt].rearrange("h p k -> p h k"), 0, pt, kTs_res[4 + j])]

            def vdst(t, pt, si, j=j):
                k0 = 2048 + j * 512 + t * 128
                return [(Vs.ap()[k0:k0 + pt, si * 512:(si + 1) * 512], 0, pt, Vs_res[4 + j])]
            inproj_kv(GM, k_own.ap(), v_own.ap(), j * 512, kdst, vdst)
            if j == 0:
                attention(GH, GH.c0, GH.c0 + 4, kTs, Vs, pre_res_k, pre_res_v, 2048, tile_info_h)

            def tile_info(kt, h, j=j):
                own0 = 16 + 4 * j
                if kt < 16:
                    sp = [(0, 128, EBp, 0, 128)] if (kt == 15 and j == 0) else []
                    return 0, True, sp
                if kt < own0:
                    sp = [(0, 128, EBp, 0, 128)] if kt == own0 - 1 else []
                    return 0, False, sp
                i = kt - own0
                sp = [(i * 128, (i + 1) * 128, EBd, 0, 128)]
                if i + 1 < 4:
                    sp.append(((i + 1) * 128, (i + 2) * 128, EBp, 0, 128))
                return i * 128, False, sp
            nkeys = 2048 + (j + 1) * 512
            attention(GM, 0, 512, kTs, Vs, pre_res_k, pre_res_v, nkeys, tile_info)
            chk("attn")
            imp()
            proj_merge(gs)
            imp()
            out_proj(gs, [x_halo.ap(), x_own.ap()] if j == 0 else [x_own.ap()], [0, 0] if j == 0 else [j * 512])
            imp()
            for g in gs:
                norm_transpose(g, g2c)
            imp()
            if j == 0:
                dma_load(gfb, gfb.ap, gf_d.ap().partition_broadcast(128))
            ffn_up(gs, [uph] * len(gs), [GM])
            imp()
            ffn_down(GM)
            final_norm(GM, y_own.ap(), j * 512)
            chk("block0")
        store_states(GM, uhalo, uph, cm_own, cf_own)
        while IMP.left():
            IMP.step()
        load_x(GS, x_smp.ap(), 0)
        norm_transpose(GS, g1c)
        inproj_conv([GS], [uhaloq])
        inproj_q([GS])

        def kdst(t, pt, si):
            return [(kTq[s].ap()[si * 4:(si + 1) * 4, :, 1024:1088].rearrange("h p k -> p h k"), s * 64, (s + 1) * 64, kTq_res[s][1])
                    for s in range(2)]

        def vdst(t, pt, si):
            return [(Vq[s].ap()[1024:1088, si * 512:(si + 1) * 512], s * 64, (s + 1) * 64, Vq_res[s][1]) for s in range(2)]
        inproj_kv(GS, k_smp.ap(), v_smp.ap(), 0, kdst, vdst)
        for s in range(2):
            def tile_info(kt, h):
                if kt < 7:
                    return 0, False, []
                if kt == 7:
                    return 0, False, [(0, 64, EBp, 0, 64)]
                return 0, False, [(0, 64, EBd, 0, 64)]
            attention(GS, s * 64, (s + 1) * 64, kTq[s], Vq[s], lambda ci, s=s: [kTq_res[s][ci]], lambda ci, s=s: [Vq_res[s][ci]],
                      1088, tile_info)
        proj_merge([GS])
        out_proj([GS], [x_smp.ap()], [0])
        norm_transpose(GS, g2c)
        ffn_up([GS], [uphq], [GS])
        ffn_down(GS)
        final_norm(GS, y_smp.ap(), 0)
        store_states(GS, uhaloq, uphq, cm_smp, cf_smp)

    Bk = alloc_block_bufs()
    Bk["sg4"] = Bk["sg"] + Bk["mtmp"]
    RM.reset()
    oT = [RM.take(f"oT{h}", [FW], BF16) for h in range(NH)]
    sg4 = [RM.take(f"sg4_{i}", [FW], BF16) for i in range(4)]
    mtmp = [RM.take(f"mtmp{i}", [FW], BF16) for i in range(2)]
    role_old = [b.res for k in ("kf", "kb", "kTst", "ztmp") for b in Bk[k]] + [Bk["ust"].res] + [b.res for b in Bk["ctmp"]]
    alias([b.res for b in oT + sg4 + mtmp], role_old)
    Bk["oT"], Bk["sg4"], Bk["mtmp"] = oT, sg4, mtmp
    GM.xs, GM.xn = Bk["xs"], Bk["xn"]
    GS.xs, GS.xn = Bk["xs"][0:1], Bk["xn"][0:1]
    GH.xs, GH.xn = [gfb], [Bk["xnh"]]
    ld_ctr = [0]

    CONV = Conv()
    IMP = CacheImport()
    alias([b.res for b in CONV.a], [b.res for b in IMP.bufs])
    P.dry = True
    SLB = Slabs(None)
    program()
    IMP.i = 0
    sched = SLB.rec
    P.dry = False
    bank_ctr[0] = 0
    ld_ctr[0] = 0
    cast_rr[0] = 0
    cbufs = setup_consts()
    SLB = Slabs(sched)
    alias([b.res for b in cbufs], [b.res for k in ("xn", "qT", "mergedT", "kTc", "Vc", "r", "o", "actT") for b in Bk[k]]
          + [Bk["junk"].res, Bk["sq"].res] + [b.res for b in Bk["PP"]])
    program()

    def mksems(n):
        return [es.enter_context(nc.semaphore(f"s{i}")) for i in range(n)]
    block = es.enter_context(nc.Block())
    stats = P.finalize(nc, block, mksems)
    es.close()
    return nc, stats


def _rel_bucket(rel):
    nb, max_exact = 16, 8
    ret = np.where(rel > 0, nb, 0)
    n = np.abs(rel)
    large = max_exact + (np.log(np.maximum(n, 1).astype(np.float32) / max_exact)
                         / np.float32(np.log(128 / max_exact)) * (nb - max_exact)).astype(np.int32)
    large = np.minimum(large, nb - 1)
    return ret + np.where(n < max_exact, n, large)


_CACHE = {}


def kernel(**inputs):
    f32 = lambda a: np.ascontiguousarray(np.asarray(a, dtype=np.float32))
    inp = {k: f32(v) for k, v in inputs.items()}
    if "nc" not in _CACHE:
        _CACHE["nc"] = build_program()
    nc, stats = _CACHE["nc"]
    kk = np.arange(128)[:, None]
    qq = np.arange(128)[None, :]
    bkt_d = _rel_bucket(kk - qq)
    bkt_p = _rel_bucket(kk - 128 - qq)
    rb = inp["rel_bias"]
    bias_diag = np.ascontiguousarray(np.transpose(rb[bkt_d], (2, 0, 1)))
    bias_prev = np.ascontiguousarray(np.transpose(rb[bkt_p], (2, 0, 1)))
    mask_diag = ((kk // 64) <= (qq // 64)).astype(np.float32)
    shared = {
        "bias_diag": bias_diag, "bias_prev": bias_prev, "mask_diag": mask_diag, "rel_bias": rb,
        "w_in": inp["w_in"][0], "w_proj_a": inp["w_proj_a"][0], "w_proj_b": inp["w_proj_b"][0], "w_o": inp["w_o"][0],
        "w_up": inp["w_up"][0], "w_down": inp["w_down"][0],
        "norm1_g": inp["norm1_g"][0], "norm2_g": inp["norm2_g"][0], "final_g": inp["final_g"],
        "lambda_q1": inp["lambda_q1"][0], "lambda_k1": inp["lambda_k1"][0], "lambda_q2": inp["lambda_q2"][0],
        "lambda_k2": inp["lambda_k2"][0], "subln_g": inp["subln_g"][0], "conv_w": inp["conv_w"][0],
        "ffn_conv_w": inp["ffn_conv_w"][0],
    }
    in_maps = []
    for c in range(NCORES):
        b, half = c // 2, c % 2
        m = dict(shared)
        m["x_own"] = np.ascontiguousarray(inp["x_prompt"][b, half * 2048:(half + 1) * 2048])
        m["x_pre"] = np.ascontiguousarray(inp["x_prompt"][b, 0:2048]) if half == 1 else np.zeros((2048, D), np.float32)
        m["x_smp"] = np.ascontiguousarray(inp["x_sample"][2 * c:2 * c + 2].reshape(128, D))
        m["x_halo"] = (np.ascontiguousarray(inp["x_prompt"][b, 2044:2048]) if half == 1 else np.zeros((4, D), np.float32))
        m["premask"] = np.full((128, 1), 1.0 if half == 1 else 0.0, np.float32)
        m["cache_k"] = np.ascontiguousarray(inp["cache_k"][0, 2 * c:2 * c + 2].reshape(2, 1024, 1024))
        m["cache_v"] = np.ascontiguousarray(inp["cache_v"][0, 2 * c:2 * c + 2].reshape(2, 1024, 1024))
        m["st_mix"] = np.ascontiguousarray(inp["state_conv_mix"][0, 2 * c:2 * c + 2])
        m["st_ffn"] = np.ascontiguousarray(inp["state_conv_ffn"][0, 2 * c:2 * c + 2])
        in_maps.append(m)
    ncr = int(os.environ.get("NC_DEBUG", NCORES))
    res = run_bass_kernel_spmd(nc, in_maps[:ncr], core_ids=list(range(ncr)))
    R = list(res.results) + [res.results[0]] * (NCORES - ncr)
    y_prompt = np.zeros((4, 4096, D), np.float32)
    k_prompt = np.zeros((1, 4, 4096, NH, 128), np.float32)
    v_prompt = np.zeros((1, 4, 4096, NH, 128), np.float32)
    cm_prompt = np.zeros((1, 4, 2, 1024), np.float32)
    cf_prompt = np.zeros((1, 4, 2, 2 * DFF), np.float32)
    y_sample = np.zeros((16, 64, D), np.float32)
    k_sample = np.zeros((1, 16, 64, NH, 128), np.float32)
    v_sample = np.zeros((1, 16, 64, NH, 128), np.float32)
    cm_sample = np.zeros((1, 16, 2, 1024), np.float32)
    cf_sample = np.zeros((1, 16, 2, 2 * DFF), np.float32)
    for c in range(NCORES):
        b, half = c // 2, c % 2
        r = R[c]
        sl = slice(half * 2048, (half + 1) * 2048)
        y_prompt[b, sl] = r["y_own"]
        k_prompt[0, b, sl] = r["k_own"].reshape(2048, NH, 128)
        v_prompt[0, b, sl] = r["v_own"].reshape(2048, NH, 128)
        if half == 1:
            cm_prompt[0, b] = r["cm_own"][0]
            cf_prompt[0, b] = r["cf_own"][0]
        y_sample[2 * c:2 * c + 2] = r["y_smp"].reshape(2, 64, D)
        k_sample[0, 2 * c:2 * c + 2] = r["k_smp"].reshape(2, 64, NH, 128)
        v_sample[0, 2 * c:2 * c + 2] = r["v_smp"].reshape(2, 64, NH, 128)
        cm_sample[0, 2 * c:2 * c + 2] = r["cm_smp"]
        cf_sample[0, 2 * c:2 * c + 2] = r["cf_smp"]
    return (y_prompt, y_sample, k_prompt, v_prompt, cm_prompt, cf_prompt,
            k_sample, v_sample, cm_sample, cf_sample)
```

```python
import numpy as np
from contextlib import ExitStack
import concourse.bass as bass
import concourse.mybir as mybir
from concourse.bass_utils import run_bass_kernel_spmd

F32 = mybir.dt.float32
BF16 = mybir.dt.bfloat16
AF = mybir.ActivationFunctionType
ALU = mybir.AluOpType
AX = mybir.AxisListType

D = 2048
NH = 8
DFF = 5632
EPS = 1e-6
NCORES = 8
COMPUTE = ("pe", "act", "dve", "pool")
SAME_ENGINE_SYNC = True
SEM_MAX = 30000
NEGM = -30000.0
import os
KSTOP = os.environ.get("KSTOP", "")
SKIP = os.environ.get("SKIP", "")
NOCONV = os.environ.get("NOCONV", "") == "1"


class Res:
    __slots__ = ("name", "w", "r", "al", "chan", "excl")

    def __init__(self, name=""):
        self.name = name
        self.w = []
        self.r = {}
        self.al = []
        self.excl = False
        self.chan = None


class Chan:
    def __init__(self, prog, name):
        self.name = name
        self.idx = len(prog.chans)
        self.count = 0
        prog.chans.append(self)


def alias(ga, gb):
    for a in ga:
        for b in gb:
            if a is not b:
                a.al.append(b)
                b.al.append(a)


class Prog:
    def __init__(self):
        self.recs = {e: [] for e in ("pe", "act", "dve", "pool", "sp")}
        self.chans = []
        self.waited = {e: {} for e in self.recs}
        self.dry = False

    def chan_of(self, res):
        if res.chan is None:
            res.chan = Chan(self, res.name)
        return res.chan

    def emit(self, eng, fn, R=(), W=(), WM=(), chan=None):
        if self.dry:
            return None
        deps = []
        for r in R:
            deps.extend(r.w)
            if r.excl:
                deps.extend(ev for k_, ev in r.r.items() if k_ != eng)
        for w in W:
            deps.extend(w.w)
            deps.extend(w.r.values())
            for a in w.al:
                deps.extend(a.w)
                deps.extend(a.r.values())
        for w in WM:
            deps.extend(w.r.values())
        if chan is not None and chan.count > 0:
            deps.append(("d", chan.idx, chan.count))
        idx = len(self.recs[eng])
        best = {}
        for ev in deps:
            if ev[0] == "c":
                if ev[1] == eng and (eng == "pe" or not SAME_ENGINE_SYNC):
                    continue
                k = ("c", ev[1])
            else:
                k = ("d", ev[1])
            if ev[2] > best.get(k, -1):
                best[k] = ev[2]
        waits = []
        wd = self.waited[eng]
        for k, v in best.items():
            if wd.get(k, -1) >= v:
                continue
            wd[k] = v
            waits.append((k, v))
        if chan is not None:
            chan.count += 16
            ev = ("d", chan.idx, chan.count)
        else:
            ev = ("c", eng, idx)
        self.recs[eng].append((fn, waits, chan))
        key = ev[1] if ev[0] == "c" else ("d", ev[1])
        for r in R:
            old = r.r.get(key)
            if old is None or old[2] < ev[2]:
                r.r[key] = ev
        for w in W:
            w.w = [ev]
            w.r = {}
        for w in WM:
            w.w = w.w + [ev]
        return ev

    def finalize(self, nc, block, mksems):
        miles = {e: set() for e in COMPUTE}
        for e, recs in self.recs.items():
            for fn, waits, chan in recs:
                for k, v in waits:
                    if k[0] == "c":
                        miles[k[1]].add(v)
        rank, nsem = {}, {}
        for e in COMPUTE:
            s = sorted(miles[e])
            rank[e] = {idx: i for i, idx in enumerate(s)}
            nsem[e] = max(1, (len(s) + SEM_MAX - 1) // SEM_MAX)
        total = sum(nsem.values()) + len(self.chans)
        sems = mksems(total)
        pos = 0
        esem = {}
        for e in COMPUTE:
            esem[e] = sems[pos:pos + nsem[e]]
            pos += nsem[e]
        csem = sems[pos:pos + len(self.chans)]

        def run(e):
            def body(eng):
                myrank = rank.get(e, {})
                for idx, (fn, waits, chan) in enumerate(self.recs[e]):
                    for k, v in waits:
                        if k[0] == "c":
                            r = rank[k[1]][v]
                            eng.wait_ge(esem[k[1]][r // SEM_MAX], (r % SEM_MAX) + 1)
                        else:
                            eng.wait_ge(csem[k[1]], v)
                    ins = fn(eng)
                    if chan is not None:
                        ins.then_inc(csem[chan.idx], 16)
                    elif idx in myrank:
                        r = myrank[idx]
                        ins.then_inc(esem[e][r // SEM_MAX], 1)
                if e == "sp":
                    for c in self.chans:
                        if c.count > 0:
                            eng.wait_ge(csem[c.idx], c.count)
            return body

        block.sync(run("sp"))
        block.tensor(run("pe"))
        block.scalar(run("act"))
        block.vector(run("dve"))
        block.gpsimd(run("pool"))
        return ({e: len(r) for e, r in self.recs.items()}, {e: len(miles[e]) for e in COMPUTE},
                total)


class Buf:
    def __init__(self, ap, name):
        self.ap = ap
        self.res = Res(name)

    def __getitem__(self, k):
        return self.ap[k]


WSPECS = {
    "w_in": (D, 10240), "w_proj_a": (1024, D), "w_proj_b": (1024, D),
    "w_o": (D, D), "w_up": (D, 2 * DFF), "w_down": (DFF, D),
}
SL_Q, SL_K, SL_V, SL_B, SL_C, SL_X, SL_GA, SL_GB = (0, 1), (2, 3), (4, 5), (6, 7), (8, 9), (10, 11), (12, 13, 14, 15), (16, 17, 18, 19)


class Group:
    def __init__(self, name, N, pts, nseg, L, c0=0, u0=0, scol=0):
        self.name, self.N, self.pts, self.nseg, self.L = name, N, pts, nseg, L
        self.nt = len(pts)
        self.c0, self.u0, self.scol = c0, u0, scol
        self.xs = self.xn = None


def build_program():
    nc = bass.Bass("TRN2", target_bir_lowering=False)
    es = ExitStack()
    P = Prog()

    def din(name, shape, dt=F32):
        return nc.dram_tensor(name, list(shape), dt, kind="ExternalInput")

    def dout(name, shape, dt=F32):
        return nc.dram_tensor(name, list(shape), dt, kind="ExternalOutput")

    def dscr(name, shape, dt=BF16):
        return nc.dram_tensor(name, list(shape), dt, kind="Internal")

    x_own = din("x_own", [2048, D])
    x_pre = din("x_pre", [2048, D])
    x_smp = din("x_smp", [128, D])
    x_halo = din("x_halo", [4, D])
    premask_d = din("premask", [128, 1])
    cache_k = din("cache_k", [2, 1024, 1024])
    cache_v = din("cache_v", [2, 1024, 1024])
    st_mix = din("st_mix", [2, 2, 1024])
    st_ffn = din("st_ffn", [2, 2, 2 * DFF])
    bd_d = din("bias_diag", [NH, 128, 128])
    bp_d = din("bias_prev", [NH, 128, 128])
    mk_d = din("mask_diag", [128, 128])
    relb_d = din("rel_bias", [32, NH])
    wd = {n: din(n, s) for n, s in WSPECS.items()}
    g1_d, g2_d, gf_d = din("norm1_g", [D]), din("norm2_g", [D]), din("final_g", [D])
    lam_d = [din(n, [64]) for n in ("lambda_q1", "lambda_k1", "lambda_q2", "lambda_k2")]
    subg_d = din("subln_g", [128])
    cw_d = din("conv_w", [3, 1024])
    fw_d = din("ffn_conv_w", [3, 2 * DFF])

    y_own = dout("y_own", [2048, D])
    y_smp = dout("y_smp", [128, D])
    k_own = dout("k_own", [2048, 1024])
    v_own = dout("v_own", [2048, 1024])
    cm_own = dout("cm_own", [1, 2, 1024])
    cf_own = dout("cf_own", [1, 2, 2 * DFF])
    k_smp = dout("k_smp", [128, 1024])
    v_smp = dout("v_smp", [128, 1024])
    cm_smp = dout("cm_smp", [2, 2, 1024])
    cf_smp = dout("cf_smp", [2, 2, 2 * DFF])

    wb = {n: dscr(n + "_bf", [N // 512, 128, K // 128, 512]) for n, (K, N) in WSPECS.items()}
    wb_res = {n: [[Res(f"{n}_s{s}_k{kc}") for kc in range(K // 128)] for s in range(N // 512)]
              for n, (K, N) in WSPECS.items()}
    kTs = dscr("kTs", [NH, 128, 4096])
    Vs = dscr("Vs", [4096, 1024])
    kTs_res = [Res(f"kTs_b{i}") for i in range(8)]
    Vs_res = [Res(f"Vs_b{i}") for i in range(8)]
    kTq = [dscr(f"kTq{s}", [NH, 128, 1152]) for s in range(2)]
    Vq = [dscr(f"Vq{s}", [1152, 1024]) for s in range(2)]
    kTq_res = [[Res(f"kTq{s}_{i}") for i in range(3)] for s in range(2)]
    Vq_res = [[Res(f"Vq{s}_{i}") for i in range(3)] for s in range(2)]

    def region(name, kb):
        return es.enter_context(nc.sbuf_tensor(name, [128, int(kb * 512)], BF16))

    class Carver:
        def __init__(self, reg, name):
            self.reg, self.name, self.off = reg, name, 0
            self.bufs = []

        def reset(self):
            self.off = 0

        def take(self, name, free_shape, dt):
            esz = 4 if dt == F32 else 2
            n = int(np.prod(free_shape))
            self.off = (self.off + 7) // 8 * 8
            if dt == F32:
                v = self.reg.bitcast(F32)[:, self.off // 4: self.off // 4 + n]
            else:
                v = self.reg[:, self.off // 2: self.off // 2 + n]
            self.off += n * esz
            assert self.off <= self.reg.shape[1] * 2, (self.name, name, self.off)
            if len(free_shape) == 2:
                v = v.rearrange("p (a b) -> p a b", a=free_shape[0])
            elif len(free_shape) == 3:
                v = v.rearrange("p (a b c) -> p a b c", a=free_shape[0], b=free_shape[1])
            b = Buf(v, name)
            self.bufs.append(b)
            return b

    RC = Carver(region("RC", 21), "RC")
    RX = Carver(region("RX", 32.5), "RX")
    RW = Carver(region("RW", 48), "RW")
    RN = Carver(region("RN", 16.25), "RN")
    RA = Carver(region("RA", 44.5), "RA")
    RM = Carver(region("RM", 20), "RM")

    pp = [es.enter_context(nc.psum_tensor(f"pp{i}", [128, 1024], F32)) for i in range(4)]
    psum = [Buf(pp[i // 2][:, (i % 2) * 512:(i % 2 + 1) * 512], f"ps{i}") for i in range(8)]
    psum_b = [pp[i // 2].bitcast(BF16)[:, (i % 2) * 1024:(i % 2 + 1) * 1024] for i in range(8)]
    for p in psum:
        p.res.excl = True
    bank_ctr = [0]

    def next_bank():
        b = bank_ctr[0] % 8
        bank_ctr[0] += 1
        return b

    def rs(xs):
        out = []
        for x in xs:
            if x is None:
                continue
            out.append(x.res if isinstance(x, Buf) else x)
        return out

    def E(eng, fn, R=(), W=(), WM=()):
        return P.emit(eng, fn, rs(R), rs(W), rs(WM))

    def dma_load(dst, dst_ap, src_ap, R=(), **kw):
        if kw.get("allow_slow_non_contiguous") and "slow" in SKIP:
            return E("pool", lambda e: e.memset(dst_ap, 0.5), W=[dst])
        ch = P.chan_of(dst.res)
        return P.emit("sp", lambda e: e.dma_start(out=dst_ap, in_=src_ap, **kw), rs(R), [dst.res], (), ch)

    def dma_store(src, dst_ap, src_ap, W=(), WM=(), R=(), **kw):
        if kw.get("allow_slow_non_contiguous") and "slow" in SKIP:
            return None
        ch = P.chan_of(src.res)
        return P.emit("sp", lambda e: e.dma_start(out=dst_ap, in_=src_ap, **kw), rs([src] + list(R)), rs(W), rs(WM), ch)

    cast_rr = [0]

    def cast_copy(out_ap, in_ap, R, W, engs=("act", "dve")):
        eng = engs[cast_rr[0] % len(engs)]
        cast_rr[0] += 1
        if eng == "act":
            E("act", lambda e: e.copy(out=out_ap, in_=in_ap), R, W)
        elif eng == "dve":
            E("dve", lambda e: e.tensor_copy(out=out_ap, in_=in_ap), R, W)
        else:
            E("pool", lambda e: e.tensor_copy(out=out_ap, in_=in_ap), R, W)

    ident = RC.take("ident", [128], BF16)
    ones = RC.take("ones", [128], BF16)
    iot = RC.take("iota", [128], F32)
    g1c = RC.take("g1c", [16], F32)
    g2c = RC.take("g2c", [16], F32)
    gfb = RC.take("gfb", [D], F32)
    cwc = RC.take("cwc", [3, 8], F32)
    fwc = RC.take("fwc", [3, 88], F32)
    identf = RC.take("identf", [128], F32)
    cst = RC.take("cst", [128], F32)
    cst2 = RC.take("cst2", [128], F32)
    chv = RC.take("chv", [NH], F32)
    nchv = RC.take("nchv", [NH], F32)
    cpre = RC.take("cpre", [NH], F32)
    pmk = RC.take("pmk", [1], F32)
    onespre = RC.take("onespre", [128], BF16)
    selA = RC.take("selA", [128], F32)
    selB = RC.take("selB", [128], F32)
    lamv = RC.take("lamv", [4, 64], F32)
    lamt = RC.take("lamt", [8], F32)
    nlam = RC.take("nlam", [1], F32)
    gsub = RC.take("gsub", [1], F32)
    EBd = RC.take("EBd", [NH, 128], BF16)
    EBp = RC.take("EBp", [NH, 128], BF16)
    uph = RC.take("uph", [88, 2], F32)
    uphq = RC.take("uphq", [88, 2, 2], F32)
    uhalo = RC.take("uhalo", [8, 2], F32)
    uhaloq = RC.take("uhaloq", [8, 2, 2], F32)
    ssq = RC.take("ssq", [10], F32)
    rstd = RC.take("rstd", [10], F32)

    def load_cols(dst, dst_ap, src_rows, n):
        dma_load(cst, cst.ap[0:n, :], src_rows)
        bk = next_bank()
        E("pe", lambda e: e.transpose(psum[bk].ap[:, 0:n], cst.ap[0:n, :], identf.ap[0:n, 0:n]), R=[cst, identf], W=[psum[bk]])
        E("dve", lambda e: e.tensor_copy(out=dst_ap, in_=psum[bk].ap[:, 0:n]), R=[psum[bk]], W=[dst])

    def store_cols(src, src_ap, dst_rows, n):
        E("dve", lambda e: e.tensor_copy(out=cst2.ap[:, 0:n], in_=src_ap), R=[src], W=[cst2])
        bk = next_bank()
        E("pe", lambda e: e.transpose(psum[bk].ap[0:n, 0:128], cst2.ap[:, 0:n], identf.ap), R=[cst2, identf], W=[psum[bk]])
        E("act", lambda e: e.copy(out=cst.ap[0:n, :], in_=psum[bk].ap[0:n, 0:128]), R=[psum[bk]], W=[cst])
        dma_store(cst, dst_rows, cst.ap[0:n, :])

    def setup_consts():
        E("pool", lambda e: e.iota(iot.ap, pattern=[[1, 128]], base=0, channel_multiplier=-1,
                                   allow_small_or_imprecise_dtypes=True), W=[iot])
        E("dve", lambda e: e.tensor_scalar(out=ident.ap, in0=iot.ap, scalar1=0.0, scalar2=None, op0=ALU.is_equal),
          R=[iot], W=[ident])
        E("dve", lambda e: e.tensor_scalar(out=identf.ap, in0=iot.ap, scalar1=0.0, scalar2=None, op0=ALU.is_equal),
          R=[iot], W=[identf])
        E("pool", lambda e: e.memset(ones.ap, 1.0), W=[ones])
        E("pool", lambda e: e.memset(selA.ap, 0.0), W=[selA])
        E("pool", lambda e: e.memset(selA.ap[0:1, :], 1.0), W=[selA])
        E("pool", lambda e: e.memset(selB.ap, 0.0), W=[selB])
        E("pool", lambda e: e.memset(selB.ap[32:33, :], 1.0), W=[selB])
        E("pool", lambda e: e.memset(uph.ap, 0.0), W=[uph])
        E("pool", lambda e: e.memset(uhalo.ap, 0.0), W=[uhalo])
        load_cols(g1c, g1c.ap, g1_d.ap().rearrange("(kc p) -> kc p", p=128), 16)
        load_cols(g2c, g2c.ap, g2_d.ap().rearrange("(kc p) -> kc p", p=128), 16)
        dma_load(gfb, gfb.ap, gf_d.ap().partition_broadcast(128))
        for jj in range(3):
            load_cols(cwc, cwc.ap[:, jj, :], cw_d.ap()[jj, :].rearrange("(c p) -> c p", p=128), 8)
            load_cols(fwc, fwc.ap[:, jj, :], fw_d.ap()[jj, :].rearrange("(c p) -> c p", p=128), 88)
        dma_load(chv, chv.ap, relb_d.ap()[15, :].partition_broadcast(128))
        dma_load(pmk, pmk.ap, premask_d.ap())
        for i in range(4):
            dma_load(lamv, lamv.ap[:, i, :], lam_d[i].ap().partition_broadcast(128))
        dma_load(gsub, gsub.ap, subg_d.ap().rearrange("(p o) -> p o", o=1))
        for s in range(2):
            for jj in range(2):
                load_cols(uhaloq, uhaloq.ap[:, :, s, jj], st_mix.ap()[s, jj, :].rearrange("(c p) -> c p", p=128), 8)
                load_cols(uphq, uphq.ap[:, :, s, jj], st_ffn.ap()[s, jj, :].rearrange("(c p) -> c p", p=128), 88)
        E("dve", lambda e: e.tensor_scalar(out=nchv.ap, in0=chv.ap, scalar1=-1.0, scalar2=None, op0=ALU.mult),
          R=[chv], W=[nchv])
        E("dve", lambda e: e.tensor_scalar(out=onespre.ap, in0=ones.ap, scalar1=pmk.ap[:, 0:1], scalar2=None, op0=ALU.mult),
          R=[ones, pmk], W=[onespre])
        E("dve", lambda e: e.tensor_scalar(out=gsub.ap, in0=gsub.ap, scalar1=0.8, scalar2=None, op0=ALU.mult),
          R=[gsub], W=[gsub])
        E("dve", lambda e: e.tensor_tensor(out=lamv.ap[:, 0, :], in0=lamv.ap[:, 0, :], in1=lamv.ap[:, 1, :], op=ALU.mult),
          R=[lamv], W=[lamv])
        E("dve", lambda e: e.tensor_tensor(out=lamv.ap[:, 2, :], in0=lamv.ap[:, 2, :], in1=lamv.ap[:, 3, :], op=ALU.mult),
          R=[lamv], W=[lamv])
        E("dve", lambda e: e.tensor_reduce(out=lamt.ap[:, 0:1], in_=lamv.ap[:, 0, :], axis=AX.X, op=ALU.add), R=[lamv], W=[lamt])
        E("dve", lambda e: e.tensor_reduce(out=lamt.ap[:, 1:2], in_=lamv.ap[:, 2, :], axis=AX.X, op=ALU.add), R=[lamv], W=[lamt])
        E("act", lambda e: e.activation(out=lamt.ap[:, 2:4], in_=lamt.ap[:, 0:2], func=AF.Exp), R=[lamt], W=[lamt])
        E("dve", lambda e: e.tensor_tensor(out=nlam.ap, in0=lamt.ap[:, 3:4], in1=lamt.ap[:, 2:3], op=ALU.subtract), R=[lamt], W=[nlam])
        E("dve", lambda e: e.tensor_scalar(out=nlam.ap, in0=nlam.ap, scalar1=-0.2, scalar2=None, op0=ALU.add), R=[nlam], W=[nlam])
        RA.reset()
        bst = RA.take("bst", [NH, 128], F32)
        bst2 = RA.take("bst2", [NH, 128], F32)
        mst = RA.take("mst", [128], F32)
        dma_load(bst, bst.ap, bd_d.ap().rearrange("h k q -> k h q"))
        dma_load(bst2, bst2.ap, bp_d.ap().rearrange("h k q -> k h q"))
        dma_load(mst, mst.ap, mk_d.ap())
        for h in range(NH):
            E("act", lambda e, h=h: e.activation(out=bst.ap[:, h, :], in_=bst.ap[:, h, :], func=AF.Exp,
                                                 bias=nchv.ap[:, h:h + 1], scale=1.0), R=[bst, nchv], W=[bst])
            E("dve", lambda e, h=h: e.tensor_tensor(out=EBd.ap[:, h, :], in0=bst.ap[:, h, :], in1=mst.ap, op=ALU.mult),
              R=[bst, mst], W=[EBd])
            E("act", lambda e, h=h: e.activation(out=EBp.ap[:, h, :], in_=bst2.ap[:, h, :], func=AF.Exp,
                                                 bias=nchv.ap[:, h:h + 1], scale=1.0), R=[bst2, nchv], W=[EBp])
        return [bst, bst2, mst]

    RV = Carver(region("RV", 24), "RV")

    class Conv:
        def __init__(self):
            self.a = [RV.take(f"cv32_{i}", [4, 512], F32) for i in range(3)]
            self.b = []
            self.i = 0

        def advance(self, k):
            pass

    class Slabs:
        def __init__(self, sched):
            self.sched = sched
            self.rec = []
            self.i = 0
            self.loaded = 0
            self.converted = set()
            self.pass_id = 0
            RW.reset()
            self.slots = [RW.take(f"slab{i}", [16, 512], BF16) for i in range(3)]

        def _load(self, j):
            name, s, k0, nk = self.sched[j]
            slot = self.slots[j % 3]
            key = (name, s, k0)
            if key in self.converted:
                dma_load(slot, slot.ap[:, 0:nk, :], wb[name].ap()[s, :, k0:k0 + nk, :],
                         R=[wb_res[name][s][k] for k in range(k0, k0 + nk)])
                return
            store = not (self.pass_id == 1 and j % 2 == 1)
            if store:
                self.converted.add(key)
            kk = 0
            while kk < nk:
                n = min(4, nk - kk)
                a_ = CONV.a[CONV.i % 3]
                CONV.i += 1
                dma_load(a_, a_.ap[:, 0:n, :],
                         wd[name].ap()[(k0 + kk) * 128:(k0 + kk + n) * 128, s * 512:(s + 1) * 512].rearrange("(k p) c -> p k c", p=128))
                eng = ("act", "dve")[cast_rr[0] % 2]
                cast_rr[0] += 1
                oap, iap = slot.ap[:, kk:kk + n, :], a_.ap[:, 0:n, :]
                if eng == "act":
                    P.emit("act", lambda e, oap=oap, iap=iap: e.copy(out=oap, in_=iap), [a_.res], [slot.res])
                else:
                    P.emit("dve", lambda e, oap=oap, iap=iap: e.tensor_copy(out=oap, in_=iap), [a_.res], [slot.res])
                kk += n
            if store:
                dma_store(slot, wb[name].ap()[s, :, k0:k0 + nk, :], slot.ap[:, 0:nk, :], W=[wb_res[name][s][k] for k in range(k0, k0 + nk)])

        def get(self, name, s, k0, nk):
            d = (name, s, k0, nk)
            if self.sched is None:
                self.rec.append(d)
                return self.slots[0]
            assert self.sched[self.i] == d, (self.i, self.sched[self.i], d)
            while self.loaded < min(len(self.sched), self.i + 3):
                self._load(self.loaded)
                self.loaded += 1
            slot = self.slots[self.i % 3]
            self.i += 1
            return slot

    FW = 516

    def alloc_block_bufs():
        B = {}
        RX.reset()
        B["xs"] = [RX.take(f"xs{t}", [D], F32) for t in range(4)]
        RX.reset()
        B["bT"] = [RX.take(f"bT{j}", [FW], BF16) for j in range(8)]
        B["cT"] = [RX.take(f"cT{j}", [FW], BF16) for j in range(8)]
        B["uT"] = [RX.take(f"uT{j}", [520], BF16) for j in range(8)]
        B["obT"] = [RX.take(f"obT{j}", [FW], BF16) for j in range(8)]
        alias([b.res for b in B["xs"]], [b.res for k in ("bT", "cT", "uT", "obT") for b in B[k]])
        RN.reset()
        B["nT"] = [RN.take(f"nT{kc}", [FW], BF16) for kc in range(16)]
        RA.reset()
        B["xn"] = [RA.take(f"xn{t}", [D], BF16) for t in range(4)]
        B["junk"] = RA.take("junk", [D], BF16)
        B["xnh"] = RA.take("xnh", [D], BF16)
        roleA1 = [b.res for b in B["xn"]] + [B["junk"].res, B["xnh"].res]
        RA.reset()
        B["qT"] = [RA.take(f"qT{h}", [FW], BF16) for h in range(NH)]
        B["mergedT"] = [RA.take(f"mg{m}", [FW], BF16) for m in range(16)]
        off_m = RA.off
        RA.off = 8 * FW * 2
        B["kTc"] = [RA.take(f"kTc{i}", [1024], BF16) for i in range(3)]
        B["Vc"] = [RA.take(f"Vc{i}", [8, 128], BF16) for i in range(3)]
        B["PP"] = [RA.take(f"PP{i}", [2, 512], BF16) for i in range(2)]
        B["r"] = [RA.take(f"r{m}", [512], F32) for m in range(2)]
        B["o"] = [RA.take(f"o{m}", [512], F32) for m in range(2)]
        B["sq"] = RA.take("sq", [512], BF16)
        B["rz"] = RA.take("rz", [512], F32)
        att = [B["rz"].res] + [b.res for k in ("kTc", "Vc", "r", "o") for b in B[k]] + [b.res for b in B["PP"]] + [B["sq"].res]
        roleA2 = [b.res for b in B["qT"]] + [b.res for b in B["mergedT"]] + att
        alias(att, [b.res for b in B["mergedT"]])
        RA.reset()
        B["actT"] = [RA.take(f"act{c}", [FW], BF16) for c in range(44)]
        roleA3 = [b.res for b in B["actT"]]
        alias(roleA1, roleA2)
        alias(roleA1, roleA3)
        alias(roleA2, roleA3)
        RM.reset()
        B["kf"] = [RM.take(f"kf{i}", [512], F32) for i in range(2)]
        B["kb"] = [RM.take(f"kb{i}", [512], BF16) for i in range(2)]
        B["kTst"] = [RM.take(f"kTst{i}", [4, 128], BF16) for i in range(2)]
        B["ztmp"] = [RM.take(f"z{i}", [512], F32) for i in range(2)]
        B["ust"] = RM.take("ust", [8, 2, 2], F32)
        roleM1 = [b.res for k in ("kf", "kb", "kTst", "ztmp") for b in B[k]] + [B["ust"].res]
        RM.reset()
        B["oT"] = [RM.take(f"oT{h}", [512], BF16) for h in range(NH)]
        B["sg"] = [RM.take(f"sg{i}", [512], BF16) for i in range(2)]
        B["mtmp"] = [RM.take(f"mtmp{i}", [512], BF16) for i in range(2)]
        roleM2 = [b.res for k in ("oT", "sg", "mtmp") for b in B[k]]
        RM.reset()
        B["ctmp"] = [RM.take(f"ct{i}", [512], F32) for i in range(4)]
        roleM3 = [b.res for b in B["ctmp"]]
        alias(roleM1, roleM2)
        alias(roleM1, roleM3)
        alias(roleM2, roleM3)
        return B

    def rms_stats(g, src, t, col):
        pt = g.pts[t]
        E("act", lambda e: e.activation(out=Bk["junk"].ap[0:pt, :], in_=src.ap[0:pt, :], func=AF.Square,
                                        accum_out=ssq.ap[0:pt, col:col + 1]), R=[src], W=[Bk["junk"], ssq])
        E("dve", lambda e: e.tensor_scalar(out=rstd.ap[0:pt, col:col + 1], in0=ssq.ap[0:pt, col:col + 1],
                                           scalar1=1.0 / D, scalar2=EPS, op0=ALU.mult, op1=ALU.add), R=[ssq], W=[rstd])
        E("act", lambda e: e.activation(out=rstd.ap[0:pt, col:col + 1], in_=rstd.ap[0:pt, col:col + 1], func=AF.Sqrt),
          R=[rstd], W=[rstd])
        E("dve", lambda e: e.reciprocal(out=rstd.ap[0:pt, col:col + 1], in_=rstd.ap[0:pt, col:col + 1]), R=[rstd], W=[rstd])

    def norm_transpose(g, gcol):
        c0, N = g.c0, g.N
        for t in range(g.nt):
            pt = g.pts[t]
            xs, xn = g.xs[t], g.xn[t]
            rms_stats(g, xs, t, g.scol + t)
            E("dve", lambda e, xs=xs, xn=xn, t=t, pt=pt: e.tensor_scalar(out=xn.ap[0:pt, :], in0=xs.ap[0:pt, :],
                                                                        scalar1=rstd.ap[0:pt, g.scol + t:g.scol + t + 1], scalar2=None, op0=ALU.mult),
              R=[xs, rstd], W=[xn])
        for kc in range(16):
            bk = next_bank()
            for t in range(g.nt):
                pt = g.pts[t]
                xn = g.xn[t]
                E("pe", lambda e, bk=bk, t=t, pt=pt, kc=kc, xn=xn: e.transpose(psum_b[bk][:, t * 128:t * 128 + pt],
                                                                               xn.ap[0:pt, kc * 128:(kc + 1) * 128],
                                                                               ident.ap[0:pt, 0:pt]),
                  R=[xn, ident], W=[psum[bk]])
            if kc % 2 == 0:
                E("dve", lambda e, bk=bk, kc=kc: e.tensor_scalar(out=Bk["nT"][kc].ap[:, c0:c0 + N], in0=psum_b[bk][:, 0:N],
                                                                 scalar1=gcol.ap[:, kc:kc + 1], scalar2=None, op0=ALU.mult),
                  R=[psum[bk], gcol], W=[Bk["nT"][kc]])
            else:
                E("act", lambda e, bk=bk, kc=kc: e.activation(out=Bk["nT"][kc].ap[:, c0:c0 + N], in_=psum_b[bk][:, 0:N],
                                                              func=AF.Copy, scale=gcol.ap[:, kc:kc + 1]),
                  R=[psum[bk], gcol], W=[Bk["nT"][kc]])

    def formA(g, slot, nk, j, rhs):
        bk = next_bank()
        c0, N = g.c0, g.N
        for kc in range(nk):
            E("pe", lambda e, bk=bk, kc=kc: e.matmul(psum[bk].ap[:, 0:N], lhsT=slot.ap[:, kc, j * 128:(j + 1) * 128],
                                                     rhs=rhs[kc].ap[:, c0:c0 + N], start=(kc == 0), stop=(kc == nk - 1)),
              R=[slot, rhs[kc]], W=[psum[bk]])
        return bk

    def formB(g, slot, k0, nk, lhs, t, bk, first, last):
        pt = g.pts[t]
        c0 = g.c0
        for kc in range(nk):
            E("pe", lambda e, kc=kc: e.matmul(psum[bk].ap[0:pt, :], lhsT=lhs[k0 + kc].ap[:, c0 + t * 128:c0 + t * 128 + pt],
                                              rhs=slot.ap[:, kc, :], start=(first and kc == 0), stop=(last and kc == nk - 1)),
              R=[slot, lhs[k0 + kc]], W=[psum[bk]])

    def inproj_conv(gs, halos):
        for si, s in enumerate(SL_B):
            slot = SLB.get("w_in", s, 0, 16)
            for j in range(4):
                c = si * 4 + j
                for g in gs:
                    bk = formA(g, slot, 16, j, Bk["nT"])
                    E("act", lambda e, bk=bk, c=c, g=g: e.copy(out=Bk["bT"][c].ap[:, g.c0:g.c0 + g.N], in_=psum[bk].ap[:, 0:g.N]),
                      R=[psum[bk]], W=[Bk["bT"][c]])
        for si, s in enumerate(SL_C):
            slot = SLB.get("w_in", s, 0, 16)
            for j in range(4):
                c = si * 4 + j
                for g in gs:
                    bk = formA(g, slot, 16, j, Bk["nT"])
                    E("act", lambda e, bk=bk, c=c, g=g: e.copy(out=Bk["cT"][c].ap[:, g.c0:g.c0 + g.N], in_=psum[bk].ap[:, 0:g.N]),
                      R=[psum[bk]], W=[Bk["cT"][c]])
        for si, s in enumerate(SL_X):
            slot = SLB.get("w_in", s, 0, 16)
            for j in range(4):
                c = si * 4 + j
                for g, u_halo_src in zip(gs, halos):
                    nseg, L, N, c0 = g.nseg, g.L, g.N, g.c0
                    W2 = L + 2
                    bk = formA(g, slot, 16, j, Bk["nT"])
                    u3 = Bk["uT"][c].ap[:, g.u0:g.u0 + nseg * W2].rearrange("p (s w) -> p s w", w=W2)
                    ps3 = psum[bk].ap[:, 0:N].rearrange("p (s w) -> p s w", w=L)
                    c3 = Bk["cT"][c].ap[:, c0:c0 + N].rearrange("p (s w) -> p s w", w=L)
                    hsrc = u_halo_src.ap[:, c] if nseg > 1 else u_halo_src.ap[:, c, :].rearrange("p (s w) -> p s w", s=1)
                    E("act", lambda e, u3=u3, hsrc=hsrc: e.copy(out=u3[:, :, 0:2], in_=hsrc), R=[u_halo_src], W=[Bk["uT"][c]])
                    E("dve", lambda e, u3=u3, ps3=ps3, c3=c3, W2=W2: e.tensor_tensor(out=u3[:, :, 2:W2], in0=ps3, in1=c3, op=ALU.mult),
                      R=[psum[bk], Bk["cT"][c]], W=[Bk["uT"][c]])
                    E("act", lambda e, u3=u3, hsrc=hsrc, L=L, W2=W2: e.copy(out=hsrc, in_=u3[:, :, L:W2]), R=[Bk["uT"][c]], W=[u_halo_src])
                    z = Bk["ztmp"][c % 2]
                    z3 = z.ap[:, 0:N].rearrange("p (s w) -> p s w", w=L)
                    E("dve", lambda e, z3=z3, u3=u3, c=c, L=L: e.tensor_scalar(out=z3, in0=u3[:, :, 0:L], scalar1=cwc.ap[:, 0, c:c + 1],
                                                                             scalar2=None, op0=ALU.mult), R=[Bk["uT"][c], cwc], W=[z])
                    E("dve", lambda e, z3=z3, u3=u3, c=c, L=L: e.scalar_tensor_tensor(out=z3, in0=u3[:, :, 1:L + 1], scalar=cwc.ap[:, 1, c:c + 1],
                                                                                    in1=z3, op0=ALU.mult, op1=ALU.add),
                      R=[Bk["uT"][c], cwc, z], W=[z])
                    E("dve", lambda e, z3=z3, u3=u3, c=c, L=L: e.scalar_tensor_tensor(out=z3, in0=u3[:, :, 2:L + 2], scalar=cwc.ap[:, 2, c:c + 1],
                                                                                    in1=z3, op0=ALU.mult, op1=ALU.add),
                      R=[Bk["uT"][c], cwc, z], W=[z])
                    E("pool", lambda e, z=z, c=c, N=N, c0=c0: e.tensor_tensor(out=Bk["obT"][c].ap[:, c0:c0 + N], in0=z.ap[:, 0:N],
                                                                            in1=Bk["bT"][c].ap[:, c0:c0 + N], op=ALU.mult),
                      R=[z, Bk["bT"][c]], W=[Bk["obT"][c]])

    def inproj_q(gs):
        for si, s in enumerate(SL_Q):
            slot = SLB.get("w_in", s, 0, 16)
            for j in range(4):
                h = si * 4 + j
                for g in gs:
                    bk = formA(g, slot, 16, j, Bk["nT"])
                    E("act", lambda e, bk=bk, h=h, g=g: e.copy(out=Bk["qT"][h].ap[:, g.c0:g.c0 + g.N], in_=psum[bk].ap[:, 0:g.N]),
                      R=[psum[bk]], W=[Bk["qT"][h]])

    def inproj_kv(g, kout, vout, row0, kdst, vdst):
        for si, s in enumerate(SL_K):
            slot = SLB.get("w_in", s, 0, 16)
            for t in range(g.nt):
                pt = g.pts[t]
                bk = next_bank()
                formB(g, slot, 0, 16, Bk["nT"], t, bk, True, True)
                i = (si * g.nt + t) % 2
                kf, kb, kst = Bk["kf"][i], Bk["kb"][i], Bk["kTst"][i]
                if kout is not None:
                    E("act", lambda e, bk=bk, kf=kf, pt=pt: e.copy(out=kf.ap[0:pt, :], in_=psum[bk].ap[0:pt, :]), R=[psum[bk]], W=[kf])
                    dma_store(kf, kout[row0 + t * 128: row0 + t * 128 + pt, si * 512:(si + 1) * 512], kf.ap[0:pt, :])
                E("dve", lambda e, bk=bk, kb=kb, pt=pt: e.tensor_copy(out=kb.ap[0:pt, :], in_=psum[bk].ap[0:pt, :]), R=[psum[bk]], W=[kb])
                bk2 = next_bank()
                for hh in range(4):
                    E("pe", lambda e, bk2=bk2, hh=hh, kb=kb, pt=pt: e.transpose(psum_b[bk2][:, hh * 128:hh * 128 + pt],
                                                                             kb.ap[0:pt, hh * 128:(hh + 1) * 128], ident.ap[0:pt, 0:pt]),
                      R=[kb, ident], W=[psum[bk2]])
                E("act", lambda e, bk2=bk2, kst=kst, pt=pt: e.copy(out=kst.ap[:, :, 0:pt],
                                                                  in_=psum_b[bk2][:, 0:512].rearrange("p (h k) -> p h k", k=128)[:, :, 0:pt]),
                  R=[psum[bk2]], W=[kst])
                for (dap, a_, b_, res) in kdst(t, pt, si):
                    dma_store(kst, dap, kst.ap[:, :, a_:b_], WM=[res])
        for si, s in enumerate(SL_V):
            slot = SLB.get("w_in", s, 0, 16)
            for t in range(g.nt):
                pt = g.pts[t]
                bk = next_bank()
                formB(g, slot, 0, 16, Bk["nT"], t, bk, True, True)
                i = (si * g.nt + t) % 2
                vf, vb = Bk["kf"][i], Bk["kb"][i]
                if vout is not None:
                    E("act", lambda e, bk=bk, vf=vf, pt=pt: e.copy(out=vf.ap[0:pt, :], in_=psum[bk].ap[0:pt, :]), R=[psum[bk]], W=[vf])
                    dma_store(vf, vout[row0 + t * 128: row0 + t * 128 + pt, si * 512:(si + 1) * 512], vf.ap[0:pt, :])
                E("dve", lambda e, bk=bk, vb=vb, pt=pt: e.tensor_copy(out=vb.ap[0:pt, :], in_=psum[bk].ap[0:pt, :]), R=[psum[bk]], W=[vb])
                for (dap, a_, b_, res) in vdst(t, pt, si):
                    dma_store(vb, dap, vb.ap[a_:b_, :], WM=[res])

    def attention(g, q0, q1, kT_d, V_d, kres, vres, nkeys, tile_info):
        nq = q1 - q0
        nkt = (nkeys + 127) // 128
        nch = (nkeys + 1023) // 1024
        O = [psum[4], psum[5]]
        ZB = psum[6]
        EP = psum[7]
        r, o, sq, rz = Bk["r"], Bk["o"], Bk["sq"], Bk["rz"]
        pending = []

        def make_epilogue(h):
            def st0():
                E("dve", lambda e: e.tensor_copy(out=o[0].ap[:, 0:nq], in_=O[0].ap[:, 0:nq]), R=[O[0]], W=[o[0]])
                E("dve", lambda e: e.tensor_copy(out=o[1].ap[:, 0:nq], in_=O[1].ap[:, 0:nq]), R=[O[1]], W=[o[1]])
                E("dve", lambda e: e.tensor_scalar(out=rz.ap[0:64, 0:nq], in0=ZB.ap[0:64, 0:nq], scalar1=1e-30, scalar2=None, op0=ALU.max),
                  R=[ZB], W=[rz])
                E("dve", lambda e: e.reciprocal(out=rz.ap[0:64, 0:nq], in_=rz.ap[0:64, 0:nq]), R=[rz], W=[rz])

            def st1():
                E("pe", lambda e: e.matmul(EP.ap[:, 0:nq], lhsT=selA.ap[0:64, :], rhs=rz.ap[0:64, 0:nq], start=True, stop=True),
                  R=[selA, rz], W=[EP])
                E("dve", lambda e: e.tensor_tensor(out=o[0].ap[:, 0:nq], in0=o[0].ap[:, 0:nq], in1=EP.ap[:, 0:nq], op=ALU.mult),
                  R=[o[0], EP], W=[o[0]])

            def st2():
                E("pe", lambda e: e.matmul(EP.ap[:, 0:nq], lhsT=selB.ap[0:64, :], rhs=rz.ap[0:64, 0:nq], start=True, stop=True),
                  R=[selB, rz], W=[EP])
                E("dve", lambda e: e.tensor_tensor(out=o[1].ap[:, 0:nq], in0=o[1].ap[:, 0:nq], in1=EP.ap[:, 0:nq], op=ALU.mult),
                  R=[o[1], EP], W=[o[1]])
                E("dve", lambda e: e.scalar_tensor_tensor(out=o[0].ap[:, 0:nq], in0=o[1].ap[:, 0:nq], scalar=nlam.ap[:, 0:1], in1=o[0].ap[:, 0:nq],
                                                          op0=ALU.mult, op1=ALU.add), R=[o[0], o[1], nlam], W=[o[0]])
                E("dve", lambda e: e.tensor_tensor(out=sq.ap[:, 0:nq], in0=o[0].ap[:, 0:nq], in1=o[0].ap[:, 0:nq], op=ALU.mult), R=[o[0]], W=[sq])

            def st3():
                E("pe", lambda e: e.matmul(EP.ap[:, 0:nq], lhsT=ones.ap, rhs=sq.ap[:, 0:nq], start=True, stop=True), R=[ones, sq], W=[EP])
                E("dve", lambda e: e.tensor_scalar(out=r[0].ap[:, 0:nq], in0=EP.ap[:, 0:nq], scalar1=1.0 / 128, scalar2=EPS, op0=ALU.mult, op1=ALU.add),
                  R=[EP], W=[r[0]])
                E("act", lambda e: e.activation(out=r[0].ap[:, 0:nq], in_=r[0].ap[:, 0:nq], func=AF.Ln), R=[r[0]], W=[r[0]])
                E("act", lambda e: e.activation(out=r[0].ap[:, 0:nq], in_=r[0].ap[:, 0:nq], func=AF.Exp, scale=-0.5), R=[r[0]], W=[r[0]])
                E("dve", lambda e: e.scalar_tensor_tensor(out=Bk["oT"][h].ap[:, q0:q1], in0=o[0].ap[:, 0:nq], scalar=gsub.ap[:, 0:1],
                                                          in1=r[0].ap[:, 0:nq], op0=ALU.mult, op1=ALU.mult),
                  R=[o[0], gsub, r[0]], W=[Bk["oT"][h]])
            return [st0, st1, st2, st3]

        def load_chunk(ci, h):
            k0 = ci * 1024
            n = min(1024, nkeys - k0)
            kc_, vc_ = Bk["kTc"][ld_ctr[0] % 3], Bk["Vc"][ld_ctr[0] % 3]
            ld_ctr[0] += 1
            dma_load(kc_, kc_.ap[:, 0:n], kT_d.ap()[h, :, k0:k0 + n], R=kres(ci))
            nfull = n // 128
            if nfull > 0:
                dma_load(vc_, vc_.ap[:, 0:nfull, :],
                         V_d.ap()[k0:k0 + nfull * 128, h * 128:(h + 1) * 128].rearrange("(kt p) d -> p kt d", p=128),
                         R=vres(ci))
            rem = n - nfull * 128
            if rem > 0:
                dma_load(vc_, vc_.ap[0:rem, nfull, :], V_d.ap()[k0 + nfull * 128:k0 + n, h * 128:(h + 1) * 128], R=vres(ci))
            return kc_, vc_

        first_chunk = load_chunk(0, 0)
        for h in range(NH):
            chunks = {0: first_chunk}
            first_chunk = None
            pend = None
            for kt in range(nkt + 1):
                cur = None
                if kt < nkt:
                    ci = kt // 8
                    if kt % 8 == 0:
                        if ci + 1 < nch:
                            chunks[ci + 1] = load_chunk(ci + 1, h)
                        elif h + 1 < NH:
                            first_chunk = load_chunk(0, h + 1)
                    kc_, vc_ = chunks[ci]
                    kl = kt % 8
                    kp = min(128, nkeys - kt * 128)
                    qa, is_pre, specials = tile_info(kt, h)
                    pi = kt % 2
                    sb0, sb1 = psum[2 * pi], psum[2 * pi + 1]
                    for m, sb in enumerate((sb0, sb1)):
                        E("pe", lambda e, sb=sb, m=m, kc_=kc_, kl=kl, kp=kp, qa=qa, h=h: e.matmul(
                            sb.ap[0:kp, qa:nq], lhsT=kc_.ap[64 * m:64 * m + 64, kl * 128:kl * 128 + kp],
                            rhs=Bk["qT"][h].ap[64 * m:64 * m + 64, q0 + qa:q1], start=True, stop=True),
                          R=[kc_, Bk["qT"][h]], W=[sb])
                    PPt = Bk["PP"][pi]
                    sv = pp[pi][0:kp, :].rearrange("p (m w) -> p m w", m=2)[:, :, qa:nq]
                    E("act", lambda e, sv=sv, PPt=PPt, kp=kp, qa=qa: e.activation(
                        out=PPt.ap[0:kp, :, qa:nq], in_=sv, func=AF.Exp, scale=0.125), R=[sb0, sb1], W=[PPt])
                    for (ca, cb, EB, ea, eb) in specials:
                        for m in range(2):
                            E("dve", lambda e, PPt=PPt, m=m, kp=kp, ca=ca, cb=cb, EB=EB, ea=ea, eb=eb, h=h: e.tensor_tensor(
                                out=PPt.ap[0:kp, m, ca:cb], in0=PPt.ap[0:kp, m, ca:cb], in1=EB.ap[0:kp, h, ea:eb], op=ALU.mult),
                              R=[PPt, EB], W=[PPt])
                    cur = (kt, kp, qa, vc_, kl, PPt, onespre if is_pre else ones)
                if pend is not None:
                    pkt, pkp, pqa, pvc, pkl, pP, pones = pend
                    for m in range(2):
                        E("pe", lambda e, m=m, pkp=pkp, pqa=pqa, pvc=pvc, pkl=pkl, pP=pP, pkt=pkt: e.matmul(
                            O[m].ap[:, pqa:nq], lhsT=pvc.ap[0:pkp, pkl, :], rhs=pP.ap[0:pkp, m, pqa:nq],
                            start=(pkt == 0), stop=(pkt == nkt - 1)), R=[pvc, pP], W=[O[m]])
                    for m in range(2):
                        E("pe", lambda e, m=m, pkp=pkp, pqa=pqa, pP=pP, pkt=pkt, pones=pones: e.matmul(
                            ZB.ap[32 * m:32 * m + 32, pqa:nq], lhsT=pones.ap[0:pkp, 0:32], rhs=pP.ap[0:pkp, m, pqa:nq],
                            start=(pkt == 0), stop=(pkt == nkt - 1)), R=[pones, pP], W=[ZB])
                    if pending and pkt >= 1:
                        pending.pop(0)()
                pend = cur
            while pending:
                pending.pop(0)()
            steps = make_epilogue(h)
            steps[0]()
            pending = steps[1:]
        while pending:
            pending.pop(0)()

    def proj_merge(gs):
        for (wname, gsl, src, first) in (("w_proj_a", SL_GA, "oT", True), ("w_proj_b", SL_GB, "obT", False)):
            for si in range(4):
                gslot = SLB.get("w_in", gsl[si], 0, 16)
                for j in range(4):
                    sg = Bk["sg4"][j]
                    for g in gs:
                        bk = formA(g, gslot, 16, j, Bk["nT"])
                        E("act", lambda e, bk=bk, sg=sg, g=g: e.activation(out=sg.ap[:, g.c0:g.c0 + g.N], in_=psum[bk].ap[:, 0:g.N], func=AF.Sigmoid),
                          R=[psum[bk]], W=[sg])
                pslot = SLB.get(wname, si, 0, 8)
                for j in range(4):
                    m = si * 4 + j
                    sg = Bk["sg4"][j]
                    mg = Bk["mergedT"][m]
                    for g in gs:
                        c0, N = g.c0, g.N
                        bk = formA(g, pslot, 8, j, Bk[src])
                        if first:
                            E("dve", lambda e, bk=bk, mg=mg, sg=sg, c0=c0, N=N: e.tensor_tensor(out=mg.ap[:, c0:c0 + N], in0=psum[bk].ap[:, 0:N],
                                                                                              in1=sg.ap[:, c0:c0 + N], op=ALU.mult),
                              R=[psum[bk], sg], W=[mg])
                        else:
                            tmp = Bk["mtmp"][m % 2]
                            E("dve", lambda e, bk=bk, tmp=tmp, sg=sg, c0=c0, N=N: e.tensor_tensor(out=tmp.ap[:, 0:N], in0=psum[bk].ap[:, 0:N],
                                                                                                in1=sg.ap[:, c0:c0 + N], op=ALU.mult),
                              R=[psum[bk], sg], W=[tmp])
                            E("pool", lambda e, mg=mg, tmp=tmp, c0=c0, N=N: e.tensor_tensor(out=mg.ap[:, c0:c0 + N], in0=mg.ap[:, c0:c0 + N],
                                                                                          in1=tmp.ap[:, 0:N], op=ALU.add),
                              R=[mg, tmp], W=[mg])

    def out_proj(gs, xsrcs, row0s):
        for g, xsrc, row0 in zip(gs, xsrcs, row0s):
            for t in range(g.nt):
                pt = g.pts[t]
                xs = g.xs[t]
                dma_load(xs, xs.ap[0:pt, :], xsrc[row0 + t * 128: row0 + t * 128 + pt, :])
        for n in range(4):
            slot = SLB.get("w_o", n, 0, 16)
            for g in gs:
                for t in range(g.nt):
                    pt = g.pts[t]
                    bk = next_bank()
                    formB(g, slot, 0, 16, Bk["mergedT"], t, bk, True, True)
                    xs = g.xs[t]
                    E("dve", lambda e, bk=bk, xs=xs, pt=pt, n=n: e.tensor_tensor(out=xs.ap[0:pt, n * 512:(n + 1) * 512], in0=psum[bk].ap[0:pt, :],
                                                                             in1=xs.ap[0:pt, n * 512:(n + 1) * 512], op=ALU.add),
                      R=[psum[bk], xs], W=[xs])

    def ffn_up(gs, halos, act_for):
        for s in range(22):
            slot = SLB.get("w_up", s, 0, 16)
            for j in range(4):
                c = s * 4 + j
                for gi, (g, halo) in enumerate(zip(gs, halos)):
                    nseg, L, N, c0 = g.nseg, g.L, g.N, g.c0
                    bk = formA(g, slot, 16, j, Bk["nT"])
                    p3 = psum[bk].ap[:, 0:N].rearrange("p (s w) -> p s w", w=L)
                    h3 = halo.ap[:, c] if nseg > 1 else halo.ap[:, c, :].rearrange("p (s w) -> p s w", s=1)
                    if g not in act_for:
                        E("act", lambda e, h3=h3, p3=p3, L=L: e.copy(out=h3, in_=p3[:, :, L - 2:L]), R=[psum[bk]], W=[halo])
                        continue
                    tmp = Bk["ctmp"][c % 4]
                    t3 = tmp.ap[:, 0:N].rearrange("p (s w) -> p s w", w=L)
                    E("act", lambda e, t3=t3, p3=p3, c=c: e.activation(out=t3, in_=p3, func=AF.Copy, scale=fwc.ap[:, 2, c:c + 1]),
                      R=[psum[bk], fwc], W=[tmp])
                    E("dve", lambda e, t3=t3, h3=h3, c=c: e.scalar_tensor_tensor(out=t3[:, :, 0:2], in0=h3, scalar=fwc.ap[:, 0, c:c + 1],
                                                                              in1=t3[:, :, 0:2], op0=ALU.mult, op1=ALU.add),
                      R=[halo, fwc, tmp], W=[tmp])
                    E("dve", lambda e, t3=t3, h3=h3, c=c: e.scalar_tensor_tensor(out=t3[:, :, 0:1], in0=h3[:, :, 1:2], scalar=fwc.ap[:, 1, c:c + 1],
                                                                              in1=t3[:, :, 0:1], op0=ALU.mult, op1=ALU.add),
                      R=[halo, fwc, tmp], W=[tmp])
                    E("act", lambda e, h3=h3, p3=p3, L=L: e.copy(out=h3, in_=p3[:, :, L - 2:L]), R=[psum[bk]], W=[halo])
                    E("dve", lambda e, t3=t3, p3=p3, c=c, L=L: e.scalar_tensor_tensor(out=t3[:, :, 1:L], in0=p3[:, :, 0:L - 1], scalar=fwc.ap[:, 1, c:c + 1],
                                                                                   in1=t3[:, :, 1:L], op0=ALU.mult, op1=ALU.add),
                      R=[psum[bk], fwc, tmp], W=[tmp])
                    E("dve", lambda e, t3=t3, p3=p3, c=c, L=L: e.scalar_tensor_tensor(out=t3[:, :, 2:L], in0=p3[:, :, 0:L - 2], scalar=fwc.ap[:, 0, c:c + 1],
                                                                                   in1=t3[:, :, 2:L], op0=ALU.mult, op1=ALU.add),
                      R=[psum[bk], fwc, tmp], W=[tmp])
                    if c < 44:
                        E("act", lambda e, tmp=tmp, c=c, N=N, c0=c0: e.activation(out=Bk["actT"][c].ap[:, c0:c0 + N], in_=tmp.ap[:, 0:N], func=AF.Silu),
                          R=[tmp], W=[Bk["actT"][c]])
                    else:
                        a_ = Bk["actT"][c - 44]
                        E("pool", lambda e, tmp=tmp, a_=a_, N=N, c0=c0: e.tensor_tensor(out=a_.ap[:, c0:c0 + N], in0=a_.ap[:, c0:c0 + N],
                                                                                      in1=tmp.ap[:, 0:N], op=ALU.mult),
                          R=[tmp, a_], W=[a_])

    def ffn_down(g):
        for n in range(4):
            banks = [(n % 2) * 4 + t for t in range(g.nt)]
            for sub in range(4):
                slot = SLB.get("w_down", n, sub * 11, 11)
                for t in range(g.nt):
                    formB(g, slot, sub * 11, 11, Bk["actT"], t, banks[t], sub == 0, sub == 3)
            for t in range(g.nt):
                pt = g.pts[t]
                xs, bk = g.xs[t], banks[t]
                E("dve", lambda e, bk=bk, xs=xs, pt=pt, n=n: e.tensor_tensor(out=xs.ap[0:pt, n * 512:(n + 1) * 512], in0=psum[bk].ap[0:pt, :],
                                                                         in1=xs.ap[0:pt, n * 512:(n + 1) * 512], op=ALU.add),
                  R=[psum[bk], xs], W=[xs])

    def final_norm(g, yout, row0):
        for t in range(g.nt):
            pt = g.pts[t]
            xs = g.xs[t]
            rms_stats(g, xs, t, 4 + t)
            E("dve", lambda e, xs=xs, pt=pt, t=t: e.scalar_tensor_tensor(out=xs.ap[0:pt, :], in0=xs.ap[0:pt, :], scalar=rstd.ap[0:pt, 4 + t:5 + t],
                                                                        in1=gfb.ap[0:pt, :], op0=ALU.mult, op1=ALU.mult),
              R=[xs, rstd, gfb], W=[xs])
            dma_store(xs, yout[row0 + t * 128: row0 + t * 128 + pt, :], xs.ap[0:pt, :])

    def store_states(g, uh, fh, cm_out, cf_out):
        for s in range(g.nseg):
            for jj in range(2):
                src = uh.ap[:, :, s, jj] if g.nseg > 1 else uh.ap[:, :, jj]
                store_cols(uh, src, cm_out.ap()[s, jj, :].rearrange("(c p) -> c p", p=128), 8)
                src = fh.ap[:, :, s, jj] if g.nseg > 1 else fh.ap[:, :, jj]
                store_cols(fh, src, cf_out.ap()[s, jj, :].rearrange("(c p) -> c p", p=128), 88)

    GM = Group("main", 512, [128] * 4, 1, 512)
    GS = Group("smp", 128, [128], 2, 64)
    GH = Group("halo", 4, [4], 1, 4, c0=512, u0=514, scol=8)

    def load_x(g, src, row0):
        for t in range(g.nt):
            pt = g.pts[t]
            xs = g.xs[t]
            dma_load(xs, xs.ap[0:pt, :], src[row0 + t * 128: row0 + t * 128 + pt, :])

    class CacheImport:
        def __init__(self):
            RV.reset()
            self.k32 = [RV.take(f"ik32_{i}", [1024], F32) for i in range(2)]
            self.v32 = [RV.take(f"iv32_{i}", [1024], F32) for i in range(2)]
            self.k16 = RV.take("ik16", [1024], BF16)
            self.v16 = RV.take("iv16", [1024], BF16)
            self.kst = RV.take("ikst", [NH, 128], BF16)
            self.bufs = self.k32 + self.v32 + [self.k16, self.v16, self.kst]
            self.i = 0

        def left(self):
            return self.i < 16

        def step(self):
            if self.i >= 16:
                return
            s_, kt = self.i // 8, self.i % 8
            k32, v32 = self.k32[self.i % 2], self.v32[self.i % 2]
            k16, v16, kst = self.k16, self.v16, self.kst
            self.i += 1
            dma_load(k32, k32.ap, cache_k.ap()[s_, kt * 128:(kt + 1) * 128, :])
            dma_load(v32, v32.ap, cache_v.ap()[s_, kt * 128:(kt + 1) * 128, :])
            E("pool", lambda e: e.tensor_copy(out=k16.ap, in_=k32.ap), R=[k32], W=[k16])
            E("pool", lambda e: e.tensor_copy(out=v16.ap, in_=v32.ap), R=[v32], W=[v16])
            bk = next_bank()
            for h in range(NH):
                E("pe", lambda e, h=h: e.transpose(psum_b[bk][:, h * 128:(h + 1) * 128], k16.ap[:, h * 128:(h + 1) * 128], ident.ap),
                  R=[k16, ident], W=[psum[bk]])
            E("act", lambda e: e.copy(out=kst.ap, in_=psum_b[bk][:, 0:1024].rearrange("p (h k) -> p h k", k=128)), R=[psum[bk]], W=[kst])
            dma_store(kst, kTq[s_].ap()[:, :, kt * 128:(kt + 1) * 128].rearrange("h p k -> p h k"), kst.ap, WM=[kTq_res[s_][0]])
            dma_store(v16, Vq[s_].ap()[kt * 128:(kt + 1) * 128, :], v16.ap, WM=[Vq_res[s_][0]])

    class Stop(Exception):
        pass

    def chk(name):
        if KSTOP == name:
            raise Stop()

    def program():
        try:
            program_()
        except Stop:
            pass

    def program_():
        chk("consts")
        pre_res_k = lambda ci: [kTs_res[2 * ci], kTs_res[2 * ci + 1]]
        pre_res_v = lambda ci: [Vs_res[2 * ci], Vs_res[2 * ci + 1]]
        for pb in range(4):
            load_x(GM, x_pre.ap(), pb * 512)
            norm_transpose(GM, g1c)

            def kdst(t, pt, si, pb=pb):
                k0 = pb * 512 + t * 128
                return [(kTs.ap()[si * 4:(si + 1) * 4, :, k0:k0 + pt].rearrange("h p k -> p h k"), 0, pt, kTs_res[pb])]

            def vdst(t, pt, si, pb=pb):
                k0 = pb * 512 + t * 128
                return [(Vs.ap()[k0:k0 + pt, si * 512:(si + 1) * 512], 0, pt, Vs_res[pb])]
            inproj_kv(GM, None, None, 0, kdst, vdst)
            CONV.advance(40)
            chk("prefix0")
        chk("prefix")
        def tile_info_h(kt, h):
            if kt == 14:
                return 0, True, [(0, 4, EBp, 124, 128)]
            if kt == 15:
                return 0, True, [(0, 4, EBd, 124, 128)]
            return 0, True, []
        for j in range(4):
            SLB.pass_id = j + 1
            gs = [GH, GM] if j == 0 else [GM]
            if j == 0:
                load_x(GH, x_halo.ap(), 0)
            load_x(GM, x_own.ap(), j * 512)
            imp = (lambda n=2: [IMP.step() for _ in range(n)]) if j == 1 else (lambda n=2: None)
            for g in gs:
                norm_transpose(g, g1c)
            imp()
            inproj_conv(gs, [uhalo] * len(gs))
            imp()
            inproj_q(gs)
            imp()

            def kdst(t, pt, si, j=j):
                k0 = 2048 + j * 512 + t * 128
                return [(kTs.ap()[si * 4:(si + 1) * 4, :, k0:k0 + pt].rearrange("h p k -> p h k"), 0, pt, kTs_res[4 + j])]

            def vdst(t, pt, si, j=j):
                k0 = 2048 + j * 512 + t * 128
                return [(Vs.ap()[k0:k0 + pt, si * 512:(si + 1) * 512], 0, pt, Vs_res[4 + j])]
            inproj_kv(GM, k_own.ap(), v_own.ap(), j * 512, kdst, vdst)
            if j == 0:
                attention(GH, GH.c0, GH.c0 + 4, kTs, Vs, pre_res_k, pre_res_v, 2048, tile_info_h)

            def tile_info(kt, h, j=j):
                own0 = 16 + 4 * j
                if kt < 16:
                    sp = [(0, 128, EBp, 0, 128)] if (kt == 15 and j == 0) else []
                    return 0, True, sp
                if kt < own0:
                    sp = [(0, 128, EBp, 0, 128)] if kt == own0 - 1 else []
                    return 0, False, sp
                i = kt - own0
                sp = [(i * 128, (i + 1) * 128, EBd, 0, 128)]
                if i + 1 < 4:
                    sp.append(((i + 1) * 128, (i + 2) * 128, EBp, 0, 128))
                return i * 128, False, sp
            nkeys = 2048 + (j + 1) * 512
            attention(GM, 0, 512, kTs, Vs, pre_res_k, pre_res_v, nkeys, tile_info)
            chk("attn")
            imp()
            proj_merge(gs)
            imp()
            out_proj(gs, [x_halo.ap(), x_own.ap()] if j == 0 else [x_own.ap()], [0, 0] if j == 0 else [j * 512])
            imp()
            for g in gs:
                norm_transpose(g, g2c)
            imp()
            if j == 0:
                dma_load(gfb, gfb.ap, gf_d.ap().partition_broadcast(128))
            ffn_up(gs, [uph] * len(gs), [GM])
            imp()
            ffn_down(GM)
            final_norm(GM, y_own.ap(), j * 512)
            chk("block0")
        store_states(GM, uhalo, uph, cm_own, cf_own)
        while IMP.left():
            IMP.step()
        load_x(GS, x_smp.ap(), 0)
        norm_transpose(GS, g1c)
        inproj_conv([GS], [uhaloq])
        inproj_q([GS])

        def kdst(t, pt, si):
            return [(kTq[s].ap()[si * 4:(si + 1) * 4, :, 1024:1088].rearrange("h p k -> p h k"), s * 64, (s + 1) * 64, kTq_res[s][1])
                    for s in range(2)]

        def vdst(t, pt, si):
            return [(Vq[s].ap()[1024:1088, si * 512:(si + 1) * 512], s * 64, (s + 1) * 64, Vq_res[s][1]) for s in range(2)]
        inproj_kv(GS, k_smp.ap(), v_smp.ap(), 0, kdst, vdst)
        for s in range(2):
            def tile_info(kt, h):
                if kt < 7:
                    return 0, False, []
                if kt == 7:
                    return 0, False, [(0, 64, EBp, 0, 64)]
                return 0, False, [(0, 64, EBd, 0, 64)]
            attention(GS, s * 64, (s + 1) * 64, kTq[s], Vq[s], lambda ci, s=s: [kTq_res[s][ci]], lambda ci, s=s: [Vq_res[s][ci]],
                      1088, tile_info)
        proj_merge([GS])
        out_proj([GS], [x_smp.ap()], [0])
        norm_transpose(GS, g2c)
        ffn_up([GS], [uphq], [GS])
        ffn_down(GS)
        final_norm(GS, y_smp.ap(), 0)
        store_states(GS, uhaloq, uphq, cm_smp, cf_smp)

    Bk = alloc_block_bufs()
    Bk["sg4"] = Bk["sg"] + Bk["mtmp"]
    RM.reset()
    oT = [RM.take(f"oT{h}", [FW], BF16) for h in range(NH)]
    sg4 = [RM.take(f"sg4_{i}", [FW], BF16) for i in range(4)]
    mtmp = [RM.take(f"mtmp{i}", [FW], BF16) for i in range(2)]
    role_old = [b.res for k in ("kf", "kb", "kTst", "ztmp") for b in Bk[k]] + [Bk["ust"].res] + [b.res for b in Bk["ctmp"]]
    alias([b.res for b in oT + sg4 + mtmp], role_old)
    Bk["oT"], Bk["sg4"], Bk["mtmp"] = oT, sg4, mtmp
    GM.xs, GM.xn = Bk["xs"], Bk["xn"]
    GS.xs, GS.xn = Bk["xs"][0:1], Bk["xn"][0:1]
    GH.xs, GH.xn = [gfb], [Bk["xnh"]]
    ld_ctr = [0]

    CONV = Conv()
    IMP = CacheImport()
    alias([b.res for b in CONV.a], [b.res for b in IMP.bufs])
    P.dry = True
    SLB = Slabs(None)
    program()
    IMP.i = 0
    sched = SLB.rec
    P.dry = False
    bank_ctr[0] = 0
    ld_ctr[0] = 0
    cast_rr[0] = 0
    cbufs = setup_consts()
    SLB = Slabs(sched)
    alias([b.res for b in cbufs], [b.res for k in ("xn", "qT", "mergedT", "kTc", "Vc", "r", "o", "actT") for b in Bk[k]]
          + [Bk["junk"].res, Bk["sq"].res] + [b.res for b in Bk["PP"]])
    program()

    def mksems(n):
        return [es.enter_context(nc.semaphore(f"s{i}")) for i in range(n)]
    block = es.enter_context(nc.Block())
    stats = P.finalize(nc, block, mksems)
    es.close()
    return nc, stats


def _rel_bucket(rel):
    nb, max_exact = 16, 8
    ret = np.where(rel > 0, nb, 0)
    n = np.abs(rel)
    large = max_exact + (np.log(np.maximum(n, 1).astype(np.float32) / max_exact)
                         / np.float32(np.log(128 / max_exact)) * (nb - max_exact)).astype(np.int32)
    large = np.minimum(large, nb - 1)
    return ret + np.where(n < max_exact, n, large)


_CACHE = {}


def kernel(**inputs):
    f32 = lambda a: np.ascontiguousarray(np.asarray(a, dtype=np.float32))
    inp = {k: f32(v) for k, v in inputs.items()}
    if "nc" not in _CACHE:
        _CACHE["nc"] = build_program()
    nc, stats = _CACHE["nc"]
    kk = np.arange(128)[:, None]
    qq = np.arange(128)[None, :]
    bkt_d = _rel_bucket(kk - qq)
    bkt_p = _rel_bucket(kk - 128 - qq)
    rb = inp["rel_bias"]
    bias_diag = np.ascontiguousarray(np.transpose(rb[bkt_d], (2, 0, 1)))
    bias_prev = np.ascontiguousarray(np.transpose(rb[bkt_p], (2, 0, 1)))
    mask_diag = ((kk // 64) <= (qq // 64)).astype(np.float32)
    shared = {
        "bias_diag": bias_diag, "bias_prev": bias_prev, "mask_diag": mask_diag, "rel_bias": rb,
        "w_in": inp["w_in"][0], "w_proj_a": inp["w_proj_a"][0], "w_proj_b": inp["w_proj_b"][0], "w_o": inp["w_o"][0],
        "w_up": inp["w_up"][0], "w_down": inp["w_down"][0],
        "norm1_g": inp["norm1_g"][0], "norm2_g": inp["norm2_g"][0], "final_g": inp["final_g"],
        "lambda_q1": inp["lambda_q1"][0], "lambda_k1": inp["lambda_k1"][0], "lambda_q2": inp["lambda_q2"][0],
        "lambda_k2": inp["lambda_k2"][0], "subln_g": inp["subln_g"][0], "conv_w": inp["conv_w"][0],
        "ffn_conv_w": inp["ffn_conv_w"][0],
    }
    in_maps = []
    for c in range(NCORES):
        b, half = c // 2, c % 2
        m = dict(shared)
        m["x_own"] = np.ascontiguousarray(inp["x_prompt"][b, half * 2048:(half + 1) * 2048])
        m["x_pre"] = np.ascontiguousarray(inp["x_prompt"][b, 0:2048]) if half == 1 else np.zeros((2048, D), np.float32)
        m["x_smp"] = np.ascontiguousarray(inp["x_sample"][2 * c:2 * c + 2].reshape(128, D))
        m["x_halo"] = (np.ascontiguousarray(inp["x_prompt"][b, 2044:2048]) if half == 1 else np.zeros((4, D), np.float32))
        m["premask"] = np.full((128, 1), 1.0 if half == 1 else 0.0, np.float32)
        m["cache_k"] = np.ascontiguousarray(inp["cache_k"][0, 2 * c:2 * c + 2].reshape(2, 1024, 1024))
        m["cache_v"] = np.ascontiguousarray(inp["cache_v"][0, 2 * c:2 * c + 2].reshape(2, 1024, 1024))
        m["st_mix"] = np.ascontiguousarray(inp["state_conv_mix"][0, 2 * c:2 * c + 2])
        m["st_ffn"] = np.ascontiguousarray(inp["state_conv_ffn"][0, 2 * c:2 * c + 2])
        in_maps.append(m)
    ncr = int(os.environ.get("NC_DEBUG", NCORES))
    res = run_bass_kernel_spmd(nc, in_maps[:ncr], core_ids=list(range(ncr)))
    R = list(res.results) + [res.results[0]] * (NCORES - ncr)
    y_prompt = np.zeros((4, 4096, D), np.float32)
    k_prompt = np.zeros((1, 4, 4096, NH, 128), np.float32)
    v_prompt = np.zeros((1, 4, 4096, NH, 128), np.float32)
    cm_prompt = np.zeros((1, 4, 2, 1024), np.float32)
    cf_prompt = np.zeros((1, 4, 2, 2 * DFF), np.float32)
    y_sample = np.zeros((16, 64, D), np.float32)
    k_sample = np.zeros((1, 16, 64, NH, 128), np.float32)
    v_sample = np.zeros((1, 16, 64, NH, 128), np.float32)
    cm_sample = np.zeros((1, 16, 2, 1024), np.float32)
    cf_sample = np.zeros((1, 16, 2, 2 * DFF), np.float32)
    for c in range(NCORES):
        b, half = c // 2, c % 2
        r = R[c]
        sl = slice(half * 2048, (half + 1) * 2048)
        y_prompt[b, sl] = r["y_own"]
        k_prompt[0, b, sl] = r["k_own"].reshape(2048, NH, 128)
        v_prompt[0, b, sl] = r["v_own"].reshape(2048, NH, 128)
        if half == 1:
            cm_prompt[0, b] = r["cm_own"][0]
            cf_prompt[0, b] = r["cf_own"][0]
        y_sample[2 * c:2 * c + 2] = r["y_smp"].reshape(2, 64, D)
        k_sample[0, 2 * c:2 * c + 2] = r["k_smp"].reshape(2, 64, NH, 128)
        v_sample[0, 2 * c:2 * c + 2] = r["v_smp"].reshape(2, 64, NH, 128)
        cm_sample[0, 2 * c:2 * c + 2] = r["cm_smp"]
        cf_sample[0, 2 * c:2 * c + 2] = r["cf_smp"]
    return (y_prompt, y_sample, k_prompt, v_prompt, cm_prompt, cf_prompt,
            k_sample, v_sample, cm_sample, cf_sample)
```

```python
import numpy as np
from contextlib import ExitStack
import concourse.bass as bass
import concourse.mybir as mybir
from concourse.bass_utils import run_bass_kernel_spmd

F32 = mybir.dt.float32
BF16 = mybir.dt.bfloat16
AF = mybir.ActivationFunctionType
ALU = mybir.AluOpType
AX = mybir.AxisListType

D = 2048
NH = 8
DFF = 5632
EPS = 1e-6
NCORES = 8
COMPUTE = ("pe", "act", "dve", "pool")
SAME_ENGINE_SYNC = True
SEM_MAX = 30000
NEGM = -30000.0
import os
KSTOP = os.environ.get("KSTOP", "")
SKIP = os.environ.get("SKIP", "")
NOCONV = os.environ.get("NOCONV", "") == "1"


class Res:
    __slots__ = ("name", "w", "r", "al", "chan", "excl")

    def __init__(self, name=""):
        self.name = name
        self.w = []
        self.r = {}
        self.al = []
        self.excl = False
        self.chan = None


class Chan:
    def __init__(self, prog, name):
        self.name = name
        self.idx = len(prog.chans)
        self.count = 0
        prog.chans.append(self)


def alias(ga, gb):
    for a in ga:
        for b in gb:
            if a is not b:
                a.al.append(b)
                b.al.append(a)


class Prog:
    def __init__(self):
        self.recs = {e: [] for e in ("pe", "act", "dve", "pool", "sp")}
        self.chans = []
        self.waited = {e: {} for e in self.recs}
        self.dry = False

    def chan_of(self, res):
        if res.chan is None:
            res.chan = Chan(self, res.name)
        return res.chan

    def emit(self, eng, fn, R=(), W=(), WM=(), chan=None):
        if self.dry:
            return None
        deps = []
        for r in R:
            deps.extend(r.w)
            if r.excl:
                deps.extend(ev for k_, ev in r.r.items() if k_ != eng)
        for w in W:
            deps.extend(w.w)
            deps.extend(w.r.values())
            for a in w.al:
                deps.extend(a.w)
                deps.extend(a.r.values())
        for w in WM:
            deps.extend(w.r.values())
        if chan is not None and chan.count > 0:
            deps.append(("d", chan.idx, chan.count))
        idx = len(self.recs[eng])
        best = {}
        for ev in deps:
            if ev[0] == "c":
                if ev[1] == eng and (eng == "pe" or not SAME_ENGINE_SYNC):
                    continue
                k = ("c", ev[1])
            else:
                k = ("d", ev[1])
            if ev[2] > best.get(k, -1):
                best[k] = ev[2]
        waits = []
        wd = self.waited[eng]
        for k, v in best.items():
            if wd.get(k, -1) >= v:
                continue
            wd[k] = v
            waits.append((k, v))
        if chan is not None:
            chan.count += 16
            ev = ("d", chan.idx, chan.count)
        else:
            ev = ("c", eng, idx)
        self.recs[eng].append((fn, waits, chan))
        key = ev[1] if ev[0] == "c" else ("d", ev[1])
        for r in R:
            old = r.r.get(key)
            if old is None or old[2] < ev[2]:
                r.r[key] = ev
        for w in W:
            w.w = [ev]
            w.r = {}
        for w in WM:
            w.w = w.w + [ev]
        return ev

    def finalize(self, nc, block, mksems):
        miles = {e: set() for e in COMPUTE}
        for e, recs in self.recs.items():
            for fn, waits, chan in recs:
                for k, v in waits:
                    if k[0] == "c":
                        miles[k[1]].add(v)
        rank, nsem = {}, {}
        for e in COMPUTE:
            s = sorted(miles[e])
            rank[e] = {idx: i for i, idx in enumerate(s)}
            nsem[e] = max(1, (len(s) + SEM_MAX - 1) // SEM_MAX)
        total = sum(nsem.values()) + len(self.chans)
        sems = mksems(total)
        pos = 0
        esem = {}
        for e in COMPUTE:
            esem[e] = sems[pos:pos + nsem[e]]
            pos += nsem[e]
        csem = sems[pos:pos + len(self.chans)]

        def run(e):
            def body(eng):
                myrank = rank.get(e, {})
                for idx, (fn, waits, chan) in enumerate(self.recs[e]):
                    for k, v in waits:
                        if k[0] == "c":
                            r = rank[k[1]][v]
                            eng.wait_ge(esem[k[1]][r // SEM_MAX], (r % SEM_MAX) + 1)
                        else:
                            eng.wait_ge(csem[k[1]], v)
                    ins = fn(eng)
                    if chan is not None:
                        ins.then_inc(csem[chan.idx], 16)
                    elif idx in myrank:
                        r = myrank[idx]
                        ins.then_inc(esem[e][r // SEM_MAX], 1)
                if e == "sp":
                    for c in self.chans:
                        if c.count > 0:
                            eng.wait_ge(csem[c.idx], c.count)
            return body

        block.sync(run("sp"))
        block.tensor(run("pe"))
        block.scalar(run("act"))
        block.vector(run("dve"))
        block.gpsimd(run("pool"))
        return ({e: len(r) for e, r in self.recs.items()}, {e: len(miles[e]) for e in COMPUTE},
                total)


class Buf:
    def __init__(self, ap, name):
        self.ap = ap
        self.res = Res(name)

    def __getitem__(self, k):
        return self.ap[k]


WSPECS = {
    "w_in": (D, 10240), "w_proj_a": (1024, D), "w_proj_b": (1024, D),
    "w_o": (D, D), "w_up": (D, 2 * DFF), "w_down": (DFF, D),
}
SL_Q, SL_K, SL_V, SL_B, SL_C, SL_X, SL_GA, SL_GB = (0, 1), (2, 3), (4, 5), (6, 7), (8, 9), (10, 11), (12, 13, 14, 15), (16, 17, 18, 19)


class Group:
    def __init__(self, name, N, pts, nseg, L, c0=0, u0=0, scol=0):
        self.name, self.N, self.pts, self.nseg, self.L = name, N, pts, nseg, L
        self.nt = len(pts)
        self.c0, self.u0, self.scol = c0, u0, scol
        self.xs = self.xn = None


def build_program():
    nc = bass.Bass("TRN2", target_bir_lowering=False)
    es = ExitStack()
    P = Prog()

    def din(name, shape, dt=F32):
        return nc.dram_tensor(name, list(shape), dt, kind="ExternalInput")

    def dout(name, shape, dt=F32):
        return nc.dram_tensor(name, list(shape), dt, kind="ExternalOutput")

    def dscr(name, shape, dt=BF16):
        return nc.dram_tensor(name, list(shape), dt, kind="Internal")

    x_own = din("x_own", [2048, D])
    x_pre = din("x_pre", [2048, D])
    x_smp = din("x_smp", [128, D])
    x_halo = din("x_halo", [4, D])
    premask_d = din("premask", [128, 1])
    cache_k = din("cache_k", [2, 1024, 1024])
    cache_v = din("cache_v", [2, 1024, 1024])
    st_mix = din("st_mix", [2, 2, 1024])
    st_ffn = din("st_ffn", [2, 2, 2 * DFF])
    bd_d = din("bias_diag", [NH, 128, 128])
    bp_d = din("bias_prev", [NH, 128, 128])
    mk_d = din("mask_diag", [128, 128])
    relb_d = din("rel_bias", [32, NH])
    wd = {n: din(n, s) for n, s in WSPECS.items()}
    g1_d, g2_d, gf_d = din("norm1_g", [D]), din("norm2_g", [D]), din("final_g", [D])
    lam_d = [din(n, [64]) for n in ("lambda_q1", "lambda_k1", "lambda_q2", "lambda_k2")]
    subg_d = din("subln_g", [128])
    cw_d = din("conv_w", [3, 1024])
    fw_d = din("ffn_conv_w", [3, 2 * DFF])

    y_own = dout("y_own", [2048, D])
    y_smp = dout("y_smp", [128, D])
    k_own = dout("k_own", [2048, 1024])
    v_own = dout("v_own", [2048, 1024])
    cm_own = dout("cm_own", [1, 2, 1024])
    cf_own = dout("cf_own", [1, 2, 2 * DFF])
    k_smp = dout("k_smp", [128, 1024])
    v_smp = dout("v_smp", [128, 1024])
    cm_smp = dout("cm_smp", [2, 2, 1024])
    cf_smp = dout("cf_smp", [2, 2, 2 * DFF])

    wb = {n: dscr(n + "_bf", [N // 512, 128, K // 128, 512]) for n, (K, N) in WSPECS.items()}
    wb_res = {n: [[Res(f"{n}_s{s}_k{kc}") for kc in range(K // 128)] for s in range(N // 512)]
              for n, (K, N) in WSPECS.items()}
    kTs = dscr("kTs", [NH, 128, 4096])
    Vs = dscr("Vs", [4096, 1024])
    kTs_res = [Res(f"kTs_b{i}") for i in range(8)]
    Vs_res = [Res(f"Vs_b{i}") for i in range(8)]
    kTq = [dscr(f"kTq{s}", [NH, 128, 1152]) for s in range(2)]
    Vq = [dscr(f"Vq{s}", [1152, 1024]) for s in range(2)]
    kTq_res = [[Res(f"kTq{s}_{i}") for i in range(3)] for s in range(2)]
    Vq_res = [[Res(f"Vq{s}_{i}") for i in range(3)] for s in range(2)]

    def region(name, kb):
        return es.enter_context(nc.sbuf_tensor(name, [128, int(kb * 512)], BF16))

    class Carver:
        def __init__(self, reg, name):
            self.reg, self.name, self.off = reg, name, 0
            self.bufs = []

        def reset(self):
            self.off = 0

        def take(self, name, free_shape, dt):
            esz = 4 if dt == F32 else 2
            n = int(np.prod(free_shape))
            self.off = (self.off + 7) // 8 * 8
            if dt == F32:
                v = self.reg.bitcast(F32)[:, self.off // 4: self.off // 4 + n]
            else:
                v = self.reg[:, self.off // 2: self.off // 2 + n]
            self.off += n * esz
            assert self.off <= self.reg.shape[1] * 2, (self.name, name, self.off)
            if len(free_shape) == 2:
                v = v.rearrange("p (a b) -> p a b", a=free_shape[0])
            elif len(free_shape) == 3:
                v = v.rearrange("p (a b c) -> p a b c", a=free_shape[0], b=free_shape[1])
            b = Buf(v, name)
            self.bufs.append(b)
            return b

    RC = Carver(region("RC", 21), "RC")
    RX = Carver(region("RX", 32.5), "RX")
    RW = Carver(region("RW", 48), "RW")
    RN = Carver(region("RN", 16.25), "RN")
    RA = Carver(region("RA", 44.5), "RA")
    RM = Carver(region("RM", 20), "RM")

    pp = [es.enter_context(nc.psum_tensor(f"pp{i}", [128, 1024], F32)) for i in range(4)]
    psum = [Buf(pp[i // 2][:, (i % 2) * 512:(i % 2 + 1) * 512], f"ps{i}") for i in range(8)]
    psum_b = [pp[i // 2].bitcast(BF16)[:, (i % 2) * 1024:(i % 2 + 1) * 1024] for i in range(8)]
    for p in psum:
        p.res.excl = True
    bank_ctr = [0]

    def next_bank():
        b = bank_ctr[0] % 8
        bank_ctr[0] += 1
        return b

    def rs(xs):
        out = []
        for x in xs:
            if x is None:
                continue
            out.append(x.res if isinstance(x, Buf) else x)
        return out

    def E(eng, fn, R=(), W=(), WM=()):
        return P.emit(eng, fn, rs(R), rs(W), rs(WM))

    def dma_load(dst, dst_ap, src_ap, R=(), **kw):
        if kw.get("allow_slow_non_contiguous") and "slow" in SKIP:
            return E("pool", lambda e: e.memset(dst_ap, 0.5), W=[dst])
        ch = P.chan_of(dst.res)
        return P.emit("sp", lambda e: e.dma_start(out=dst_ap, in_=src_ap, **kw), rs(R), [dst.res], (), ch)

    def dma_store(src, dst_ap, src_ap, W=(), WM=(), R=(), **kw):
        if kw.get("allow_slow_non_contiguous") and "slow" in SKIP:
            return None
        ch = P.chan_of(src.res)
        return P.emit("sp", lambda e: e.dma_start(out=dst_ap, in_=src_ap, **kw), rs([src] + list(R)), rs(W), rs(WM), ch)

    cast_rr = [0]

    def cast_copy(out_ap, in_ap, R, W, engs=("act", "dve")):
        eng = engs[cast_rr[0] % len(engs)]
        cast_rr[0] += 1
        if eng == "act":
            E("act", lambda e: e.copy(out=out_ap, in_=in_ap), R, W)
        elif eng == "dve":
            E("dve", lambda e: e.tensor_copy(out=out_ap, in_=in_ap), R, W)
        else:
            E("pool", lambda e: e.tensor_copy(out=out_ap, in_=in_ap), R, W)

    ident = RC.take("ident", [128], BF16)
    ones = RC.take("ones", [128], BF16)
    iot = RC.take("iota", [128], F32)
    g1c = RC.take("g1c", [16], F32)
    g2c = RC.take("g2c", [16], F32)
    gfb = RC.take("gfb", [D], F32)
    cwc = RC.take("cwc", [3, 8], F32)
    fwc = RC.take("fwc", [3, 88], F32)
    identf = RC.take("identf", [128], F32)
    cst = RC.take("cst", [128], F32)
    cst2 = RC.take("cst2", [128], F32)
    chv = RC.take("chv", [NH], F32)
    nchv = RC.take("nchv", [NH], F32)
    cpre = RC.take("cpre", [NH], F32)
    pmk = RC.take("pmk", [1], F32)
    onespre = RC.take("onespre", [128], BF16)
    selA = RC.take("selA", [128], F32)
    selB = RC.take("selB", [128], F32)
    lamv = RC.take("lamv", [4, 64], F32)
    lamt = RC.take("lamt", [8], F32)
    nlam = RC.take("nlam", [1], F32)
    gsub = RC.take("gsub", [1], F32)
    EBd = RC.take("EBd", [NH, 128], BF16)
    EBp = RC.take("EBp", [NH, 128], BF16)
    uph = RC.take("uph", [88, 2], F32)
    uphq = RC.take("uphq", [88, 2, 2], F32)
    uhalo = RC.take("uhalo", [8, 2], F32)
    uhaloq = RC.take("uhaloq", [8, 2, 2], F32)
    ssq = RC.take("ssq", [10], F32)
    rstd = RC.take("rstd", [10], F32)

    def load_cols(dst, dst_ap, src_rows, n):
        dma_load(cst, cst.ap[0:n, :], src_rows)
        bk = next_bank()
        E("pe", lambda e: e.transpose(psum[bk].ap[:, 0:n], cst.ap[0:n, :], identf.ap[0:n, 0:n]), R=[cst, identf], W=[psum[bk]])
        E("dve", lambda e: e.tensor_copy(out=dst_ap, in_=psum[bk].ap[:, 0:n]), R=[psum[bk]], W=[dst])

    def store_cols(src, src_ap, dst_rows, n):
        E("dve", lambda e: e.tensor_copy(out=cst2.ap[:, 0:n], in_=src_ap), R=[src], W=[cst2])
        bk = next_bank()
        E("pe", lambda e: e.transpose(psum[bk].ap[0:n, 0:128], cst2.ap[:, 0:n], identf.ap), R=[cst2, identf], W=[psum[bk]])
        E("act", lambda e: e.copy(out=cst.ap[0:n, :], in_=psum[bk].ap[0:n, 0:128]), R=[psum[bk]], W=[cst])
        dma_store(cst, dst_rows, cst.ap[0:n, :])

    def setup_consts():
        E("pool", lambda e: e.iota(iot.ap, pattern=[[1, 128]], base=0, channel_multiplier=-1,
                                   allow_small_or_imprecise_dtypes=True), W=[iot])
        E("dve", lambda e: e.tensor_scalar(out=ident.ap, in0=iot.ap, scalar1=0.0, scalar2=None, op0=ALU.is_equal),
          R=[iot], W=[ident])
        E("dve", lambda e: e.tensor_scalar(out=identf.ap, in0=iot.ap, scalar1=0.0, scalar2=None, op0=ALU.is_equal),
          R=[iot], W=[identf])
        E("pool", lambda e: e.memset(ones.ap, 1.0), W=[ones])
        E("pool", lambda e: e.memset(selA.ap, 0.0), W=[selA])
        E("pool", lambda e: e.memset(selA.ap[0:1, :], 1.0), W=[selA])
        E("pool", lambda e: e.memset(selB.ap, 0.0), W=[selB])
        E("pool", lambda e: e.memset(selB.ap[32:33, :], 1.0), W=[selB])
        E("pool", lambda e: e.memset(uph.ap, 0.0), W=[uph])
        E("pool", lambda e: e.memset(uhalo.ap, 0.0), W=[uhalo])
        load_cols(g1c, g1c.ap, g1_d.ap().rearrange("(kc p) -> kc p", p=128), 16)
        load_cols(g2c, g2c.ap, g2_d.ap().rearrange("(kc p) -> kc p", p=128), 16)
        dma_load(gfb, gfb.ap, gf_d.ap().partition_broadcast(128))
        for jj in range(3):
            load_cols(cwc, cwc.ap[:, jj, :], cw_d.ap()[jj, :].rearrange("(c p) -> c p", p=128), 8)
            load_cols(fwc, fwc.ap[:, jj, :], fw_d.ap()[jj, :].rearrange("(c p) -> c p", p=128), 88)
        dma_load(chv, chv.ap, relb_d.ap()[15, :].partition_broadcast(128))
        dma_load(pmk, pmk.ap, premask_d.ap())
        for i in range(4):
            dma_load(lamv, lamv.ap[:, i, :], lam_d[i].ap().partition_broadcast(128))
        dma_load(gsub, gsub.ap, subg_d.ap().rearrange("(p o) -> p o", o=1))
        for s in range(2):
            for jj in range(2):
                load_cols(uhaloq, uhaloq.ap[:, :, s, jj], st_mix.ap()[s, jj, :].rearrange("(c p) -> c p", p=128), 8)
                load_cols(uphq, uphq.ap[:, :, s, jj], st_ffn.ap()[s, jj, :].rearrange("(c p) -> c p", p=128), 88)
        E("dve", lambda e: e.tensor_scalar(out=nchv.ap, in0=chv.ap, scalar1=-1.0, scalar2=None, op0=ALU.mult),
          R=[chv], W=[nchv])
        E("dve", lambda e: e.tensor_scalar(out=onespre.ap, in0=ones.ap, scalar1=pmk.ap[:, 0:1], scalar2=None, op0=ALU.mult),
          R=[ones, pmk], W=[onespre])
        E("dve", lambda e: e.tensor_scalar(out=gsub.ap, in0=gsub.ap, scalar1=0.8, scalar2=None, op0=ALU.mult),
          R=[gsub], W=[gsub])
        E("dve", lambda e: e.tensor_tensor(out=lamv.ap[:, 0, :], in0=lamv.ap[:, 0, :], in1=lamv.ap[:, 1, :], op=ALU.mult),
          R=[lamv], W=[lamv])
        E("dve", lambda e: e.tensor_tensor(out=lamv.ap[:, 2, :], in0=lamv.ap[:, 2, :], in1=lamv.ap[:, 3, :], op=ALU.mult),
          R=[lamv], W=[lamv])
        E("dve", lambda e: e.tensor_reduce(out=lamt.ap[:, 0:1], in_=lamv.ap[:, 0, :], axis=AX.X, op=ALU.add), R=[lamv], W=[lamt])
        E("dve", lambda e: e.tensor_reduce(out=lamt.ap[:, 1:2], in_=lamv.ap[:, 2, :], axis=AX.X, op=ALU.add), R=[lamv], W=[lamt])
        E("act", lambda e: e.activation(out=lamt.ap[:, 2:4], in_=lamt.ap[:, 0:2], func=AF.Exp), R=[lamt], W=[lamt])
        E("dve", lambda e: e.tensor_tensor(out=nlam.ap, in0=lamt.ap[:, 3:4], in1=lamt.ap[:, 2:3], op=ALU.subtract), R=[lamt], W=[nlam])
        E("dve", lambda e: e.tensor_scalar(out=nlam.ap, in0=nlam.ap, scalar1=-0.2, scalar2=None, op0=ALU.add), R=[nlam], W=[nlam])
        RA.reset()
        bst = RA.take("bst", [NH, 128], F32)
        bst2 = RA.take("bst2", [NH, 128], F32)
        mst = RA.take("mst", [128], F32)
        dma_load(bst, bst.ap, bd_d.ap().rearrange("h k q -> k h q"))
        dma_load(bst2, bst2.ap, bp_d.ap().rearrange("h k q -> k h q"))
        dma_load(mst, mst.ap, mk_d.ap())
        for h in range(NH):
            E("act", lambda e, h=h: e.activation(out=bst.ap[:, h, :], in_=bst.ap[:, h, :], func=AF.Exp,
                                                 bias=nchv.ap[:, h:h + 1], scale=1.0), R=[bst, nchv], W=[bst])
            E("dve", lambda e, h=h: e.tensor_tensor(out=EBd.ap[:, h, :], in0=bst.ap[:, h, :], in1=mst.ap, op=ALU.mult),
              R=[bst, mst], W=[EBd])
            E("act", lambda e, h=h: e.activation(out=EBp.ap[:, h, :], in_=bst2.ap[:, h, :], func=AF.Exp,
                                                 bias=nchv.ap[:, h:h + 1], scale=1.0), R=[bst2, nchv], W=[EBp])
        return [bst, bst2, mst]

    RV = Carver(region("RV", 24), "RV")

    class Conv:
        def __init__(self):
            self.a = [RV.take(f"cv32_{i}", [4, 512], F32) for i in range(3)]
            self.b = []
            self.i = 0

        def advance(self, k):
            pass

    class Slabs:
        def __init__(self, sched):
            self.sched = sched
            self.rec = []
            self.i = 0
            self.loaded = 0
            self.converted = set()
            self.pass_id = 0
            RW.reset()
            self.slots = [RW.take(f"slab{i}", [16, 512], BF16) for i in range(3)]

        def _load(self, j):
            name, s, k0, nk = self.sched[j]
            slot = self.slots[j % 3]
            key = (name, s, k0)
            if key in self.converted:
                dma_load(slot, slot.ap[:, 0:nk, :], wb[name].ap()[s, :, k0:k0 + nk, :],
                         R=[wb_res[name][s][k] for k in range(k0, k0 + nk)])
                return
            store = not (self.pass_id == 1 and j % 2 == 1)
            if store:
                self.converted.add(key)
            kk = 0
            while kk < nk:
                n = min(4, nk - kk)
                a_ = CONV.a[CONV.i % 3]
                CONV.i += 1
                dma_load(a_, a_.ap[:, 0:n, :],
                         wd[name].ap()[(k0 + kk) * 128:(k0 + kk + n) * 128, s * 512:(s + 1) * 512].rearrange("(k p) c -> p k c", p=128))
                eng = ("act", "dve")[cast_rr[0] % 2]
                cast_rr[0] += 1
                oap, iap = slot.ap[:, kk:kk + n, :], a_.ap[:, 0:n, :]
                if eng == "act":
                    P.emit("act", lambda e, oap=oap, iap=iap: e.copy(out=oap, in_=iap), [a_.res], [slot.res])
                else:
                    P.emit("dve", lambda e, oap=oap, iap=iap: e.tensor_copy(out=oap, in_=iap), [a_.res], [slot.res])
                kk += n
            if store:
                dma_store(slot, wb[name].ap()[s, :, k0:k0 + nk, :], slot.ap[:, 0:nk, :], W=[wb_res[name][s][k] for k in range(k0, k0 + nk)])

        def get(self, name, s, k0, nk):
            d = (name, s, k0, nk)
            if self.sched is None:
                self.rec.append(d)
                return self.slots[0]
            assert self.sched[self.i] == d, (self.i, self.sched[self.i], d)
            while self.loaded < min(len(self.sched), self.i + 3):
                self._load(self.loaded)
                self.loaded += 1
            slot = self.slots[self.i % 3]
            self.i += 1
            return slot

    FW = 516

    def alloc_block_bufs():
        B = {}
        RX.reset()
        B["xs"] = [RX.take(f"xs{t}", [D], F32) for t in range(4)]
        RX.reset()
        B["bT"] = [RX.take(f"bT{j}", [FW], BF16) for j in range(8)]
        B["cT"] = [RX.take(f"cT{j}", [FW], BF16) for j in range(8)]
        B["uT"] = [RX.take(f"uT{j}", [520], BF16) for j in range(8)]
        B["obT"] = [RX.take(f"obT{j}", [FW], BF16) for j in range(8)]
        alias([b.res for b in B["xs"]], [b.res for k in ("bT", "cT", "uT", "obT") for b in B[k]])
        RN.reset()
        B["nT"] = [RN.take(f"nT{kc}", [FW], BF16) for kc in range(16)]
        RA.reset()
        B["xn"] = [RA.take(f"xn{t}", [D], BF16) for t in range(4)]
        B["junk"] = RA.take("junk", [D], BF16)
        B["xnh"] = RA.take("xnh", [D], BF16)
        roleA1 = [b.res for b in B["xn"]] + [B["junk"].res, B["xnh"].res]
        RA.reset()
        B["qT"] = [RA.take(f"qT{h}", [FW], BF16) for h in range(NH)]
        B["mergedT"] = [RA.take(f"mg{m}", [FW], BF16) for m in range(16)]
        off_m = RA.off
        RA.off = 8 * FW * 2
        B["kTc"] = [RA.take(f"kTc{i}", [1024], BF16) for i in range(3)]
        B["Vc"] = [RA.take(f"Vc{i}", [8, 128], BF16) for i in range(3)]
        B["PP"] = [RA.take(f"PP{i}", [2, 512], BF16) for i in range(2)]
        B["r"] = [RA.take(f"r{m}", [512], F32) for m in range(2)]
        B["o"] = [RA.take(f"o{m}", [512], F32) for m in range(2)]
        B["sq"] = RA.take("sq", [512], BF16)
        B["rz"] = RA.take("rz", [512], F32)
        att = [B["rz"].res] + [b.res for k in ("kTc", "Vc", "r", "o") for b in B[k]] + [b.res for b in B["PP"]] + [B["sq"].res]
        roleA2 = [b.res for b in B["qT"]] + [b.res for b in B["mergedT"]] + att
        alias(att, [b.res for b in B["mergedT"]])
        RA.reset()
        B["actT"] = [RA.take(f"act{c}", [FW], BF16) for c in range(44)]
        roleA3 = [b.res for b in B["actT"]]
        alias(roleA1, roleA2)
        alias(roleA1, roleA3)
        alias(roleA2, roleA3)
        RM.reset()
        B["kf"] = [RM.take(f"kf{i}", [512], F32) for i in range(2)]
        B["kb"] = [RM.take(f"kb{i}", [512], BF16) for i in range(2)]
        B["kTst"] = [RM.take(f"kTst{i}", [4, 128], BF16) for i in range(2)]
        B["ztmp"] = [RM.take(f"z{i}", [512], F32) for i in range(2)]
        B["ust"] = RM.take("ust", [8, 2, 2], F32)
        roleM1 = [b.res for k in ("kf", "kb", "kTst", "ztmp") for b in B[k]] + [B["ust"].res]
        RM.reset()
        B["oT"] = [RM.take(f"oT{h}", [512], BF16) for h in range(NH)]
        B["sg"] = [RM.take(f"sg{i}", [512], BF16) for i in range(2)]
        B["mtmp"] = [RM.take(f"mtmp{i}", [512], BF16) for i in range(2)]
        roleM2 = [b.res for k in ("oT", "sg", "mtmp") for b in B[k]]
        RM.reset()
        B["ctmp"] = [RM.take(f"ct{i}", [512], F32) for i in range(4)]
        roleM3 = [b.res for b in B["ctmp"]]
        alias(roleM1, roleM2)
        alias(roleM1, roleM3)
        alias(roleM2, roleM3)
        return B

    def rms_stats(g, src, t, col):
        pt = g.pts[t]
        E("act", lambda e: e.activation(out=Bk["junk"].ap[0:pt, :], in_=src.ap[0:pt, :], func=AF.Square,
                                        accum_out=ssq.ap[0:pt, col:col + 1]), R=[src], W=[Bk["junk"], ssq])
        E("dve", lambda e: e.tensor_scalar(out=rstd.ap[0:pt, col:col + 1], in0=ssq.ap[0:pt, col:col + 1],
                                           scalar1=1.0 / D, scalar2=EPS, op0=ALU.mult, op1=ALU.add), R=[ssq], W=[rstd])
        E("act", lambda e: e.activation(out=rstd.ap[0:pt, col:col + 1], in_=rstd.ap[0:pt, col:col + 1], func=AF.Sqrt),
          R=[rstd], W=[rstd])
        E("dve", lambda e: e.reciprocal(out=rstd.ap[0:pt, col:col + 1], in_=rstd.ap[0:pt, col:col + 1]), R=[rstd], W=[rstd])

    def norm_transpose(g, gcol):
        c0, N = g.c0, g.N
        for t in range(g.nt):
            pt = g.pts[t]
            xs, xn = g.xs[t], g.xn[t]
            rms_stats(g, xs, t, g.scol + t)
            E("dve", lambda e, xs=xs, xn=xn, t=t, pt=pt: e.tensor_scalar(out=xn.ap[0:pt, :], in0=xs.ap[0:pt, :],
                                                                        scalar1=rstd.ap[0:pt, g.scol + t:g.scol + t + 1], scalar2=None, op0=ALU.mult),
              R=[xs, rstd], W=[xn])
        for kc in range(16):
            bk = next_bank()
            for t in range(g.nt):
                pt = g.pts[t]
                xn = g.xn[t]
                E("pe", lambda e, bk=bk, t=t, pt=pt, kc=kc, xn=xn: e.transpose(psum_b[bk][:, t * 128:t * 128 + pt],
                                                                               xn.ap[0:pt, kc * 128:(kc + 1) * 128],
                                                                               ident.ap[0:pt, 0:pt]),
                  R=[xn, ident], W=[psum[bk]])
            if kc % 2 == 0:
                E("dve", lambda e, bk=bk, kc=kc: e.tensor_scalar(out=Bk["nT"][kc].ap[:, c0:c0 + N], in0=psum_b[bk][:, 0:N],
                                                                 scalar1=gcol.ap[:, kc:kc + 1], scalar2=None, op0=ALU.mult),
                  R=[psum[bk], gcol], W=[Bk["nT"][kc]])
            else:
                E("act", lambda e, bk=bk, kc=kc: e.activation(out=Bk["nT"][kc].ap[:, c0:c0 + N], in_=psum_b[bk][:, 0:N],
                                                              func=AF.Copy, scale=gcol.ap[:, kc:kc + 1]),
                  R=[psum[bk], gcol], W=[Bk["nT"][kc]])

    def formA(g, slot, nk, j, rhs):
        bk = next_bank()
        c0, N = g.c0, g.N
        for kc in range(nk):
            E("pe", lambda e, bk=bk, kc=kc: e.matmul(psum[bk].ap[:, 0:N], lhsT=slot.ap[:, kc, j * 128:(j + 1) * 128],
                                                     rhs=rhs[kc].ap[:, c0:c0 + N], start=(kc == 0), stop=(kc == nk - 1)),
              R=[slot, rhs[kc]], W=[psum[bk]])
        return bk

    def formB(g, slot, k0, nk, lhs, t, bk, first, last):
        pt = g.pts[t]
        c0 = g.c0
        for kc in range(nk):
            E("pe", lambda e, kc=kc: e.matmul(psum[bk].ap[0:pt, :], lhsT=lhs[k0 + kc].ap[:, c0 + t * 128:c0 + t * 128 + pt],
                                              rhs=slot.ap[:, kc, :], start=(first and kc == 0), stop=(last and kc == nk - 1)),
              R=[slot, lhs[k0 + kc]], W=[psum[bk]])

    def inproj_conv(gs, halos):
        for si, s in enumerate(SL_B):
            slot = SLB.get("w_in", s, 0, 16)
            for j in range(4):
                c = si * 4 + j
                for g in gs:
                    bk = formA(g, slot, 16, j, Bk["nT"])
                    E("act", lambda e, bk=bk, c=c, g=g: e.copy(out=Bk["bT"][c].ap[:, g.c0:g.c0 + g.N], in_=psum[bk].ap[:, 0:g.N]),
                      R=[psum[bk]], W=[Bk["bT"][c]])
        for si, s in enumerate(SL_C):
            slot = SLB.get("w_in", s, 0, 16)
            for j in range(4):
                c = si * 4 + j
                for g in gs:
                    bk = formA(g, slot, 16, j, Bk["nT"])
                    E("act", lambda e, bk=bk, c=c, g=g: e.copy(out=Bk["cT"][c].ap[:, g.c0:g.c0 + g.N], in_=psum[bk].ap[:, 0:g.N]),
                      R=[psum[bk]], W=[Bk["cT"][c]])
        for si, s in enumerate(SL_X):
            slot = SLB.get("w_in", s, 0, 16)
            for j in range(4):
                c = si * 4 + j
                for g, u_halo_src in zip(gs, halos):
                    nseg, L, N, c0 = g.nseg, g.L, g.N, g.c0
                    W2 = L + 2
                    bk = formA(g, slot, 16, j, Bk["nT"])
                    u3 = Bk["uT"][c].ap[:, g.u0:g.u0 + nseg * W2].rearrange("p (s w) -> p s w", w=W2)
                    ps3 = psum[bk].ap[:, 0:N].rearrange("p (s w) -> p s w", w=L)
                    c3 = Bk["cT"][c].ap[:, c0:c0 + N].rearrange("p (s w) -> p s w", w=L)
                    hsrc = u_halo_src.ap[:, c] if nseg > 1 else u_halo_src.ap[:, c, :].rearrange("p (s w) -> p s w", s=1)
                    E("act", lambda e, u3=u3, hsrc=hsrc: e.copy(out=u3[:, :, 0:2], in_=hsrc), R=[u_halo_src], W=[Bk["uT"][c]])
                    E("dve", lambda e, u3=u3, ps3=ps3, c3=c3, W2=W2: e.tensor_tensor(out=u3[:, :, 2:W2], in0=ps3, in1=c3, op=ALU.mult),
                      R=[psum[bk], Bk["cT"][c]], W=[Bk["uT"][c]])
                    E("act", lambda e, u3=u3, hsrc=hsrc, L=L, W2=W2: e.copy(out=hsrc, in_=u3[:, :, L:W2]), R=[Bk["uT"][c]], W=[u_halo_src])
                    z = Bk["ztmp"][c % 2]
                    z3 = z.ap[:, 0:N].rearrange("p (s w) -> p s w", w=L)
                    E("dve", lambda e, z3=z3, u3=u3, c=c, L=L: e.tensor_scalar(out=z3, in0=u3[:, :, 0:L], scalar1=cwc.ap[:, 0, c:c + 1],
                                                                             scalar2=None, op0=ALU.mult), R=[Bk["uT"][c], cwc], W=[z])
                    E("dve", lambda e, z3=z3, u3=u3, c=c, L=L: e.scalar_tensor_tensor(out=z3, in0=u3[:, :, 1:L + 1], scalar=cwc.ap[:, 1, c:c + 1],
                                                                                    in1=z3, op0=ALU.mult, op1=ALU.add),
                      R=[Bk["uT"][c], cwc, z], W=[z])
                    E("dve", lambda e, z3=z3, u3=u3, c=c, L=L: e.scalar_tensor_tensor(out=z3, in0=u3[:, :, 2:L + 2], scalar=cwc.ap[:, 2, c:c + 1],
                                                                                    in1=z3, op0=ALU.mult, op1=ALU.add),
                      R=[Bk["uT"][c], cwc, z], W=[z])
                    E("pool", lambda e, z=z, c=c, N=N, c0=c0: e.tensor_tensor(out=Bk["obT"][c].ap[:, c0:c0 + N], in0=z.ap[:, 0:N],
                                                                            in1=Bk["bT"][c].ap[:, c0:c0 + N], op=ALU.mult),
                      R=[z, Bk["bT"][c]], W=[Bk["obT"][c]])

    def inproj_q(gs):
        for si, s in enumerate(SL_Q):
            slot = SLB.get("w_in", s, 0, 16)
            for j in range(4):
                h = si * 4 + j
                for g in gs:
                    bk = formA(g, slot, 16, j, Bk["nT"])
                    E("act", lambda e, bk=bk, h=h, g=g: e.copy(out=Bk["qT"][h].ap[:, g.c0:g.c0 + g.N], in_=psum[bk].ap[:, 0:g.N]),
                      R=[psum[bk]], W=[Bk["qT"][h]])

    def inproj_kv(g, kout, vout, row0, kdst, vdst):
        for si, s in enumerate(SL_K):
            slot = SLB.get("w_in", s, 0, 16)
            for t in range(g.nt):
                pt = g.pts[t]
                bk = next_bank()
                formB(g, slot, 0, 16, Bk["nT"], t, bk, True, True)
                i = (si * g.nt + t) % 2
                kf, kb, kst = Bk["kf"][i], Bk["kb"][i], Bk["kTst"][i]
                if kout is not None:
                    E("act", lambda e, bk=bk, kf=kf, pt=pt: e.copy(out=kf.ap[0:pt, :], in_=psum[bk].ap[0:pt, :]), R=[psum[bk]], W=[kf])
                    dma_store(kf, kout[row0 + t * 128: row0 + t * 128 + pt, si * 512:(si + 1) * 512], kf.ap[0:pt, :])
                E("dve", lambda e, bk=bk, kb=kb, pt=pt: e.tensor_copy(out=kb.ap[0:pt, :], in_=psum[bk].ap[0:pt, :]), R=[psum[bk]], W=[kb])
                bk2 = next_bank()
                for hh in range(4):
                    E("pe", lambda e, bk2=bk2, hh=hh, kb=kb, pt=pt: e.transpose(psum_b[bk2][:, hh * 128:hh * 128 + pt],
                                                                             kb.ap[0:pt, hh * 128:(hh + 1) * 128], ident.ap[0:pt, 0:pt]),
                      R=[kb, ident], W=[psum[bk2]])
                E("act", lambda e, bk2=bk2, kst=kst, pt=pt: e.copy(out=kst.ap[:, :, 0:pt],
                                                                  in_=psum_b[bk2][:, 0:512].rearrange("p (h k) -> p h k", k=128)[:, :, 0:pt]),
                  R=[psum[bk2]], W=[kst])
                for (dap, a_, b_, res) in kdst(t, pt, si):
                    dma_store(kst, dap, kst.ap[:, :, a_:b_], WM=[res])
        for si, s in enumerate(SL_V):
            slot = SLB.get("w_in", s, 0, 16)
            for t in range(g.nt):
                pt = g.pts[t]
                bk = next_bank()
                formB(g, slot, 0, 16, Bk["nT"], t, bk, True, True)
                i = (si * g.nt + t) % 2
                vf, vb = Bk["kf"][i], Bk["kb"][i]
                if vout is not None:
                    E("act", lambda e, bk=bk, vf=vf, pt=pt: e.copy(out=vf.ap[0:pt, :], in_=psum[bk].ap[0:pt, :]), R=[psum[bk]], W=[vf])
                    dma_store(vf, vout[row0 + t * 128: row0 + t * 128 + pt, si * 512:(si + 1) * 512], vf.ap[0:pt, :])
                E("dve", lambda e, bk=bk, vb=vb, pt=pt: e.tensor_copy(out=vb.ap[0:pt, :], in_=psum[bk].ap[0:pt, :]), R=[psum[bk]], W=[vb])
                for (dap, a_, b_, res) in vdst(t, pt, si):
                    dma_store(vb, dap, vb.ap[a_:b_, :], WM=[res])

    def attention(g, q0, q1, kT_d, V_d, kres, vres, nkeys, tile_info):
        nq = q1 - q0
        nkt = (nkeys + 127) // 128
        nch = (nkeys + 1023) // 1024
        O = [psum[4], psum[5]]
        ZB = psum[6]
        EP = psum[7]
        r, o, sq, rz = Bk["r"], Bk["o"], Bk["sq"], Bk["rz"]
        pending = []

        def make_epilogue(h):
            def st0():
                E("dve", lambda e: e.tensor_copy(out=o[0].ap[:, 0:nq], in_=O[0].ap[:, 0:nq]), R=[O[0]], W=[o[0]])
                E("dve", lambda e: e.tensor_copy(out=o[1].ap[:, 0:nq], in_=O[1].ap[:, 0:nq]), R=[O[1]], W=[o[1]])
                E("dve", lambda e: e.tensor_scalar(out=rz.ap[0:64, 0:nq], in0=ZB.ap[0:64, 0:nq], scalar1=1e-30, scalar2=None, op0=ALU.max),
                  R=[ZB], W=[rz])
                E("dve", lambda e: e.reciprocal(out=rz.ap[0:64, 0:nq], in_=rz.ap[0:64, 0:nq]), R=[rz], W=[rz])

            def st1():
                E("pe", lambda e: e.matmul(EP.ap[:, 0:nq], lhsT=selA.ap[0:64, :], rhs=rz.ap[0:64, 0:nq], start=True, stop=True),
                  R=[selA, rz], W=[EP])
                E("dve", lambda e: e.tensor_tensor(out=o[0].ap[:, 0:nq], in0=o[0].ap[:, 0:nq], in1=EP.ap[:, 0:nq], op=ALU.mult),
                  R=[o[0], EP], W=[o[0]])

            def st2():
                E("pe", lambda e: e.matmul(EP.ap[:, 0:nq], lhsT=selB.ap[0:64, :], rhs=rz.ap[0:64, 0:nq], start=True, stop=True),
                  R=[selB, rz], W=[EP])
                E("dve", lambda e: e.tensor_tensor(out=o[1].ap[:, 0:nq], in0=o[1].ap[:, 0:nq], in1=EP.ap[:, 0:nq], op=ALU.mult),
                  R=[o[1], EP], W=[o[1]])
                E("dve", lambda e: e.scalar_tensor_tensor(out=o[0].ap[:, 0:nq], in0=o[1].ap[:, 0:nq], scalar=nlam.ap[:, 0:1], in1=o[0].ap[:, 0:nq],
                                                          op0=ALU.mult, op1=ALU.add), R=[o[0], o[1], nlam], W=[o[0]])
                E("dve", lambda e: e.tensor_tensor(out=sq.ap[:, 0:nq], in0=o[0].ap[:, 0:nq], in1=o[0].ap[:, 0:nq], op=ALU.mult), R=[o[0]], W=[sq])

            def st3():
                E("pe", lambda e: e.matmul(EP.ap[:, 0:nq], lhsT=ones.ap, rhs=sq.ap[:, 0:nq], start=True, stop=True), R=[ones, sq], W=[EP])
                E("dve", lambda e: e.tensor_scalar(out=r[0].ap[:, 0:nq], in0=EP.ap[:, 0:nq], scalar1=1.0 / 128, scalar2=EPS, op0=ALU.mult, op1=ALU.add),
                  R=[EP], W=[r[0]])
                E("act", lambda e: e.activation(out=r[0].ap[:, 0:nq], in_=r[0].ap[:, 0:nq], func=AF.Ln), R=[r[0]], W=[r[0]])
                E("act", lambda e: e.activation(out=r[0].ap[:, 0:nq], in_=r[0].ap[:, 0:nq], func=AF.Exp, scale=-0.5), R=[r[0]], W=[r[0]])
                E("dve", lambda e: e.scalar_tensor_tensor(out=Bk["oT"][h].ap[:, q0:q1], in0=o[0].ap[:, 0:nq], scalar=gsub.ap[:, 0:1],
                                                          in1=r[0].ap[:, 0:nq], op0=ALU.mult, op1=ALU.mult),
                  R=[o[0], gsub, r[0]], W=[Bk["oT"][h]])
            return [st0, st1, st2, st3]

        def load_chunk(ci, h):
            k0 = ci * 1024
            n = min(1024, nkeys - k0)
            kc_, vc_ = Bk["kTc"][ld_ctr[0] % 3], Bk["Vc"][ld_ctr[0] % 3]
            ld_ctr[0] += 1
            dma_load(kc_, kc_.ap[:, 0:n], kT_d.ap()[h, :, k0:k0 + n], R=kres(ci))
            nfull = n // 128
            if nfull > 0:
                dma_load(vc_, vc_.ap[:, 0:nfull, :],
                         V_d.ap()[k0:k0 + nfull * 128, h * 128:(h + 1) * 128].rearrange("(kt p) d -> p kt d", p=128),
                         R=vres(ci))
            rem = n - nfull * 128
            if rem > 0:
                dma_load(vc_, vc_.ap[0:rem, nfull, :], V_d.ap()[k0 + nfull * 128:k0 + n, h * 128:(h + 1) * 128], R=vres(ci))
            return kc_, vc_

        first_chunk = load_chunk(0, 0)
        for h in range(NH):
            chunks = {0: first_chunk}
            first_chunk = None
            pend = None
            for kt in range(nkt + 1):
                cur = None
                if kt < nkt:
                    ci = kt // 8
                    if kt % 8 == 0:
                        if ci + 1 < nch:
                            chunks[ci + 1] = load_chunk(ci + 1, h)
                        elif h + 1 < NH:
                            first_chunk = load_chunk(0, h + 1)
                    kc_, vc_ = chunks[ci]
                    kl = kt % 8
                    kp = min(128, nkeys - kt * 128)
                    qa, is_pre, specials = tile_info(kt, h)
                    pi = kt % 2
                    sb0, sb1 = psum[2 * pi], psum[2 * pi + 1]
                    for m, sb in enumerate((sb0, sb1)):
                        E("pe", lambda e, sb=sb, m=m, kc_=kc_, kl=kl, kp=kp, qa=qa, h=h: e.matmul(
                            sb.ap[0:kp, qa:nq], lhsT=kc_.ap[64 * m:64 * m + 64, kl * 128:kl * 128 + kp],
                            rhs=Bk["qT"][h].ap[64 * m:64 * m + 64, q0 + qa:q1], start=True, stop=True),
                          R=[kc_, Bk["qT"][h]], W=[sb])
                    PPt = Bk["PP"][pi]
                    sv = pp[pi][0:kp, :].rearrange("p (m w) -> p m w", m=2)[:, :, qa:nq]
                    E("act", lambda e, sv=sv, PPt=PPt, kp=kp, qa=qa: e.activation(
                        out=PPt.ap[0:kp, :, qa:nq], in_=sv, func=AF.Exp, scale=0.125), R=[sb0, sb1], W=[PPt])
                    for (ca, cb, EB, ea, eb) in specials:
                        for m in range(2):
                            E("dve", lambda e, PPt=PPt, m=m, kp=kp, ca=ca, cb=cb, EB=EB, ea=ea, eb=eb, h=h: e.tensor_tensor(
                                out=PPt.ap[0:kp, m, ca:cb], in0=PPt.ap[0:kp, m, ca:cb], in1=EB.ap[0:kp, h, ea:eb], op=ALU.mult),
                              R=[PPt, EB], W=[PPt])
                    cur = (kt, kp, qa, vc_, kl, PPt, onespre if is_pre else ones)
                if pend is not None:
                    pkt, pkp, pqa, pvc, pkl, pP, pones = pend
                    for m in range(2):
                        E("pe", lambda e, m=m, pkp=pkp, pqa=pqa, pvc=pvc, pkl=pkl, pP=pP, pkt=pkt: e.matmul(
                            O[m].ap[:, pqa:nq], lhsT=pvc.ap[0:pkp, pkl, :], rhs=pP.ap[0:pkp, m, pqa:nq],
                            start=(pkt == 0), stop=(pkt == nkt - 1)), R=[pvc, pP], W=[O[m]])
                    for m in range(2):
                        E("pe", lambda e, m=m, pkp=pkp, pqa=pqa, pP=pP, pkt=pkt, pones=pones: e.matmul(
                            ZB.ap[32 * m:32 * m + 32, pqa:nq], lhsT=pones.ap[0:pkp, 0:32], rhs=pP.ap[0:pkp, m, pqa:nq],
                            start=(pkt == 0), stop=(pkt == nkt - 1)), R=[pones, pP], W=[ZB])
                    if pending and pkt % 3 == 2:
                        pending.pop(0)()
                pend = cur
            while pending:
                pending.pop(0)()
            steps = make_epilogue(h)
            steps[0]()
            pending = steps[1:]
        while pending:
            pending.pop(0)()

    def proj_merge(gs):
        for (wname, gsl, src, first) in (("w_proj_a", SL_GA, "oT", True), ("w_proj_b", SL_GB, "obT", False)):
            for si in range(4):
                gslot = SLB.get("w_in", gsl[si], 0, 16)
                for j in range(4):
                    sg = Bk["sg4"][j]
                    for g in gs:
                        bk = formA(g, gslot, 16, j, Bk["nT"])
                        E("act", lambda e, bk=bk, sg=sg, g=g: e.activation(out=sg.ap[:, g.c0:g.c0 + g.N], in_=psum[bk].ap[:, 0:g.N], func=AF.Sigmoid),
                          R=[psum[bk]], W=[sg])
                pslot = SLB.get(wname, si, 0, 8)
                for j in range(4):
                    m = si * 4 + j
                    sg = Bk["sg4"][j]
                    mg = Bk["mergedT"][m]
                    for g in gs:
                        c0, N = g.c0, g.N
                        bk = formA(g, pslot, 8, j, Bk[src])
                        if first:
                            E("dve", lambda e, bk=bk, mg=mg, sg=sg, c0=c0, N=N: e.tensor_tensor(out=mg.ap[:, c0:c0 + N], in0=psum[bk].ap[:, 0:N],
                                                                                              in1=sg.ap[:, c0:c0 + N], op=ALU.mult),
                              R=[psum[bk], sg], W=[mg])
                        else:
                            tmp = Bk["mtmp"][m % 2]
                            E("dve", lambda e, bk=bk, tmp=tmp, sg=sg, c0=c0, N=N: e.tensor_tensor(out=tmp.ap[:, 0:N], in0=psum[bk].ap[:, 0:N],
                                                                                                in1=sg.ap[:, c0:c0 + N], op=ALU.mult),
                              R=[psum[bk], sg], W=[tmp])
                            E("pool", lambda e, mg=mg, tmp=tmp, c0=c0, N=N: e.tensor_tensor(out=mg.ap[:, c0:c0 + N], in0=mg.ap[:, c0:c0 + N],
                                                                                          in1=tmp.ap[:, 0:N], op=ALU.add),
                              R=[mg, tmp], W=[mg])

    def out_proj(gs, xsrcs, row0s):
        for g, xsrc, row0 in zip(gs, xsrcs, row0s):
            for t in range(g.nt):
                pt = g.pts[t]
                xs = g.xs[t]
                dma_load(xs, xs.ap[0:pt, :], xsrc[row0 + t * 128: row0 + t * 128 + pt, :])
        for n in range(4):
            slot = SLB.get("w_o", n, 0, 16)
            for g in gs:
                for t in range(g.nt):
                    pt = g.pts[t]
                    bk = next_bank()
                    formB(g, slot, 0, 16, Bk["mergedT"], t, bk, True, True)
                    xs = g.xs[t]
                    E("dve", lambda e, bk=bk, xs=xs, pt=pt, n=n: e.tensor_tensor(out=xs.ap[0:pt, n * 512:(n + 1) * 512], in0=psum[bk].ap[0:pt, :],
                                                                             in1=xs.ap[0:pt, n * 512:(n + 1) * 512], op=ALU.add),
                      R=[psum[bk], xs], W=[xs])

    def ffn_up(gs, halos, act_for):
        for s in range(22):
            slot = SLB.get("w_up", s, 0, 16)
            for j in range(4):
                c = s * 4 + j
                for gi, (g, halo) in enumerate(zip(gs, halos)):
                    nseg, L, N, c0 = g.nseg, g.L, g.N, g.c0
                    bk = formA(g, slot, 16, j, Bk["nT"])
                    p3 = psum[bk].ap[:, 0:N].rearrange("p (s w) -> p s w", w=L)
                    h3 = halo.ap[:, c] if nseg > 1 else halo.ap[:, c, :].rearrange("p (s w) -> p s w", s=1)
                    if g not in act_for:
                        E("act", lambda e, h3=h3, p3=p3, L=L: e.copy(out=h3, in_=p3[:, :, L - 2:L]), R=[psum[bk]], W=[halo])
                        continue
                    tmp = Bk["ctmp"][c % 4]
                    t3 = tmp.ap[:, 0:N].rearrange("p (s w) -> p s w", w=L)
                    E("act", lambda e, t3=t3, p3=p3, c=c: e.activation(out=t3, in_=p3, func=AF.Copy, scale=fwc.ap[:, 2, c:c + 1]),
                      R=[psum[bk], fwc], W=[tmp])
                    E("dve", lambda e, t3=t3, h3=h3, c=c: e.scalar_tensor_tensor(out=t3[:, :, 0:2], in0=h3, scalar=fwc.ap[:, 0, c:c + 1],
                                                                              in1=t3[:, :, 0:2], op0=ALU.mult, op1=ALU.add),
                      R=[halo, fwc, tmp], W=[tmp])
                    E("dve", lambda e, t3=t3, h3=h3, c=c: e.scalar_tensor_tensor(out=t3[:, :, 0:1], in0=h3[:, :, 1:2], scalar=fwc.ap[:, 1, c:c + 1],
                                                                              in1=t3[:, :, 0:1], op0=ALU.mult, op1=ALU.add),
                      R=[halo, fwc, tmp], W=[tmp])
                    E("act", lambda e, h3=h3, p3=p3, L=L: e.copy(out=h3, in_=p3[:, :, L - 2:L]), R=[psum[bk]], W=[halo])
                    E("dve", lambda e, t3=t3, p3=p3, c=c, L=L: e.scalar_tensor_tensor(out=t3[:, :, 1:L], in0=p3[:, :, 0:L - 1], scalar=fwc.ap[:, 1, c:c + 1],
                                                                                   in1=t3[:, :, 1:L], op0=ALU.mult, op1=ALU.add),
                      R=[psum[bk], fwc, tmp], W=[tmp])
                    E("dve", lambda e, t3=t3, p3=p3, c=c, L=L: e.scalar_tensor_tensor(out=t3[:, :, 2:L], in0=p3[:, :, 0:L - 2], scalar=fwc.ap[:, 0, c:c + 1],
                                                                                   in1=t3[:, :, 2:L], op0=ALU.mult, op1=ALU.add),
                      R=[psum[bk], fwc, tmp], W=[tmp])
                    if c < 44:
                        E("act", lambda e, tmp=tmp, c=c, N=N, c0=c0: e.activation(out=Bk["actT"][c].ap[:, c0:c0 + N], in_=tmp.ap[:, 0:N], func=AF.Silu),
                          R=[tmp], W=[Bk["actT"][c]])
                    else:
                        a_ = Bk["actT"][c - 44]
                        E("pool", lambda e, tmp=tmp, a_=a_, N=N, c0=c0: e.tensor_tensor(out=a_.ap[:, c0:c0 + N], in0=a_.ap[:, c0:c0 + N],
                                                                                      in1=tmp.ap[:, 0:N], op=ALU.mult),
                          R=[tmp, a_], W=[a_])

    def ffn_down(g):
        for n in range(4):
            banks = [(n % 2) * 4 + t for t in range(g.nt)]
            for sub in range(4):
                slot = SLB.get("w_down", n, sub * 11, 11)
                for t in range(g.nt):
                    formB(g, slot, sub * 11, 11, Bk["actT"], t, banks[t], sub == 0, sub == 3)
            for t in range(g.nt):
                pt = g.pts[t]
                xs, bk = g.xs[t], banks[t]
                E("dve", lambda e, bk=bk, xs=xs, pt=pt, n=n: e.tensor_tensor(out=xs.ap[0:pt, n * 512:(n + 1) * 512], in0=psum[bk].ap[0:pt, :],
                                                                         in1=xs.ap[0:pt, n * 512:(n + 1) * 512], op=ALU.add),
                  R=[psum[bk], xs], W=[xs])

    def final_norm(g, yout, row0):
        for t in range(g.nt):
            pt = g.pts[t]
            xs = g.xs[t]
            rms_stats(g, xs, t, 4 + t)
            E("dve", lambda e, xs=xs, pt=pt, t=t: e.scalar_tensor_tensor(out=xs.ap[0:pt, :], in0=xs.ap[0:pt, :], scalar=rstd.ap[0:pt, 4 + t:5 + t],
                                                                        in1=gfb.ap[0:pt, :], op0=ALU.mult, op1=ALU.mult),
              R=[xs, rstd, gfb], W=[xs])
            dma_store(xs, yout[row0 + t * 128: row0 + t * 128 + pt, :], xs.ap[0:pt, :])

    def store_states(g, uh, fh, cm_out, cf_out):
        for s in range(g.nseg):
            for jj in range(2):
                src = uh.ap[:, :, s, jj] if g.nseg > 1 else uh.ap[:, :, jj]
                store_cols(uh, src, cm_out.ap()[s, jj, :].rearrange("(c p) -> c p", p=128), 8)
                src = fh.ap[:, :, s, jj] if g.nseg > 1 else fh.ap[:, :, jj]
                store_cols(fh, src, cf_out.ap()[s, jj, :].rearrange("(c p) -> c p", p=128), 88)

    GM = Group("main", 512, [128] * 4, 1, 512)
    GS = Group("smp", 128, [128], 2, 64)
    GH = Group("halo", 4, [4], 1, 4, c0=512, u0=514, scol=8)

    def load_x(g, src, row0):
        for t in range(g.nt):
            pt = g.pts[t]
            xs = g.xs[t]
            dma_load(xs, xs.ap[0:pt, :], src[row0 + t * 128: row0 + t * 128 + pt, :])

    class CacheImport:
        def __init__(self):
            RV.reset()
            self.k32 = [RV.take(f"ik32_{i}", [1024], F32) for i in range(2)]
            self.v32 = [RV.take(f"iv32_{i}", [1024], F32) for i in range(2)]
            self.k16 = RV.take("ik16", [1024], BF16)
            self.v16 = RV.take("iv16", [1024], BF16)
            self.kst = RV.take("ikst", [NH, 128], BF16)
            self.bufs = self.k32 + self.v32 + [self.k16, self.v16, self.kst]
            self.i = 0

        def left(self):
            return self.i < 16

        def step(self):
            if self.i >= 16:
                return
            s_, kt = self.i // 8, self.i % 8
            k32, v32 = self.k32[self.i % 2], self.v32[self.i % 2]
            k16, v16, kst = self.k16, self.v16, self.kst
            self.i += 1
            dma_load(k32, k32.ap, cache_k.ap()[s_, kt * 128:(kt + 1) * 128, :])
            dma_load(v32, v32.ap, cache_v.ap()[s_, kt * 128:(kt + 1) * 128, :])
            E("pool", lambda e: e.tensor_copy(out=k16.ap, in_=k32.ap), R=[k32], W=[k16])
            E("pool", lambda e: e.tensor_copy(out=v16.ap, in_=v32.ap), R=[v32], W=[v16])
            bk = next_bank()
            for h in range(NH):
                E("pe", lambda e, h=h: e.transpose(psum_b[bk][:, h * 128:(h + 1) * 128], k16.ap[:, h * 128:(h + 1) * 128], ident.ap),
                  R=[k16, ident], W=[psum[bk]])
            E("act", lambda e: e.copy(out=kst.ap, in_=psum_b[bk][:, 0:1024].rearrange("p (h k) -> p h k", k=128)), R=[psum[bk]], W=[kst])
            dma_store(kst, kTq[s_].ap()[:, :, kt * 128:(kt + 1) * 128].rearrange("h p k -> p h k"), kst.ap, WM=[kTq_res[s_][0]])
            dma_store(v16, Vq[s_].ap()[kt * 128:(kt + 1) * 128, :], v16.ap, WM=[Vq_res[s_][0]])

    class Stop(Exception):
        pass

    def chk(name):
        if KSTOP == name:
            raise Stop()

    def program():
        try:
            program_()
        except Stop:
            pass

    def program_():
        chk("consts")
        pre_res_k = lambda ci: [kTs_res[2 * ci], kTs_res[2 * ci + 1]]
        pre_res_v = lambda ci: [Vs_res[2 * ci], Vs_res[2 * ci + 1]]
        for pb in range(4):
            load_x(GM, x_pre.ap(), pb * 512)
            norm_transpose(GM, g1c)

            def kdst(t, pt, si, pb=pb):
                k0 = pb * 512 + t * 128
                return [(kTs.ap()[si * 4:(si + 1) * 4, :, k0:k0 + pt].rearrange("h p k -> p h k"), 0, pt, kTs_res[pb])]

            def vdst(t, pt, si, pb=pb):
                k0 = pb * 512 + t * 128
                return [(Vs.ap()[k0:k0 + pt, si * 512:(si + 1) * 512], 0, pt, Vs_res[pb])]
            inproj_kv(GM, None, None, 0, kdst, vdst)
            CONV.advance(40)
            chk("prefix0")
        chk("prefix")
        def tile_info_h(kt, h):
            if kt == 14:
                return 0, True, [(0, 4, EBp, 124, 128)]
            if kt == 15:
                return 0, True, [(0, 4, EBd, 124, 128)]
            return 0, True, []
        for j in range(4):
            SLB.pass_id = j + 1
            gs = [GH, GM] if j == 0 else [GM]
            if j == 0:
                load_x(GH, x_halo.ap(), 0)
            load_x(GM, x_own.ap(), j * 512)
            imp = (lambda n=2: [IMP.step() for _ in range(n)]) if j == 1 else (lambda n=2: None)
            for g in gs:
                norm_transpose(g, g1c)
            imp()
            inproj_conv(gs, [uhalo] * len(gs))
            imp()
            inproj_q(gs)
            imp()

            def kdst(t, pt, si, j=j):
                k0 = 2048 + j * 512 + t * 128
                return [(kTs.ap()[si * 4:(si + 1) * 4, :, k0:k0 + pt].rearrange("h p k -> p h k"), 0, pt, kTs_res[4 + j])]

            def vdst(t, pt, si, j=j):
                k0 = 2048 + j * 512 + t * 128
                return [(Vs.ap()[k0:k0 + pt, si * 512:(si + 1) * 512], 0, pt, Vs_res[4 + j])]
            inproj_kv(GM, k_own.ap(), v_own.ap(), j * 512, kdst, vdst)
            if j == 0:
                attention(GH, GH.c0, GH.c0 + 4, kTs, Vs, pre_res_k, pre_res_v, 2048, tile_info_h)

            def tile_info(kt, h, j=j):
                own0 = 16 + 4 * j
                if kt < 16:
                    sp = [(0, 128, EBp, 0, 128)] if (kt == 15 and j == 0) else []
                    return 0, True, sp
                if kt < own0:
                    sp = [(0, 128, EBp, 0, 128)] if kt == own0 - 1 else []
                    return 0, False, sp
                i = kt - own0
                sp = [(i * 128, (i + 1) * 128, EBd, 0, 128)]
                if i + 1 < 4:
                    sp.append(((i + 1) * 128, (i + 2) * 128, EBp, 0, 128))
                return i * 128, False, sp
            nkeys = 2048 + (j + 1) * 512
            attention(GM, 0, 512, kTs, Vs, pre_res_k, pre_res_v, nkeys, tile_info)
            chk("attn")
            imp()
            proj_merge(gs)
            imp()
            out_proj(gs, [x_halo.ap(), x_own.ap()] if j == 0 else [x_own.ap()], [0, 0] if j == 0 else [j * 512])
            imp()
            for g in gs:
                norm_transpose(g, g2c)
            imp()
            if j == 0:
                dma_load(gfb, gfb.ap, gf_d.ap().partition_broadcast(128))
            ffn_up(gs, [uph] * len(gs), [GM])
            imp()
            ffn_down(GM)
            final_norm(GM, y_own.ap(), j * 512)
            chk("block0")
        store_states(GM, uhalo, uph, cm_own, cf_own)
        while IMP.left():
            IMP.step()
        load_x(GS, x_smp.ap(), 0)
        norm_transpose(GS, g1c)
        inproj_conv([GS], [uhaloq])
        inproj_q([GS])

        def kdst(t, pt, si):
            return [(kTq[s].ap()[si * 4:(si + 1) * 4, :, 1024:1088].rearrange("h p k -> p h k"), s * 64, (s + 1) * 64, kTq_res[s][1])
                    for s in range(2)]

        def vdst(t, pt, si):
            return [(Vq[s].ap()[1024:1088, si * 512:(si + 1) * 512], s * 64, (s + 1) * 64, Vq_res[s][1]) for s in range(2)]
        inproj_kv(GS, k_smp.ap(), v_smp.ap(), 0, kdst, vdst)
        for s in range(2):
            def tile_info(kt, h):
                if kt < 7:
                    return 0, False, []
                if kt == 7:
                    return 0, False, [(0, 64, EBp, 0, 64)]
                return 0, False, [(0, 64, EBd, 0, 64)]
            attention(GS, s * 64, (s + 1) * 64, kTq[s], Vq[s], lambda ci, s=s: [kTq_res[s][ci]], lambda ci, s=s: [Vq_res[s][ci]],
                      1088, tile_info)
        proj_merge([GS])
        out_proj([GS], [x_smp.ap()], [0])
        norm_transpose(GS, g2c)
        ffn_up([GS], [uphq], [GS])
        ffn_down(GS)
        final_norm(GS, y_smp.ap(), 0)
        store_states(GS, uhaloq, uphq, cm_smp, cf_smp)

    Bk = alloc_block_bufs()
    Bk["sg4"] = Bk["sg"] + Bk["mtmp"]
    RM.reset()
    oT = [RM.take(f"oT{h}", [FW], BF16) for h in range(NH)]
    sg4 = [RM.take(f"sg4_{i}", [FW], BF16) for i in range(4)]
    mtmp = [RM.take(f"mtmp{i}", [FW], BF16) for i in range(2)]
    role_old = [b.res for k in ("kf", "kb", "kTst", "ztmp") for b in Bk[k]] + [Bk["ust"].res] + [b.res for b in Bk["ctmp"]]
    alias([b.res for b in oT + sg4 + mtmp], role_old)
    Bk["oT"], Bk["sg4"], Bk["mtmp"] = oT, sg4, mtmp
    GM.xs, GM.xn = Bk["xs"], Bk["xn"]
    GS.xs, GS.xn = Bk["xs"][0:1], Bk["xn"][0:1]
    GH.xs, GH.xn = [gfb], [Bk["xnh"]]
    ld_ctr = [0]

    CONV = Conv()
    IMP = CacheImport()
    alias([b.res for b in CONV.a], [b.res for b in IMP.bufs])
    P.dry = True
    SLB = Slabs(None)
    program()
    IMP.i = 0
    sched = SLB.rec
    P.dry = False
    bank_ctr[0] = 0
    ld_ctr[0] = 0
    cast_rr[0] = 0
    cbufs = setup_consts()
    SLB = Slabs(sched)
    alias([b.res for b in cbufs], [b.res for k in ("xn", "qT", "mergedT", "kTc", "Vc", "r", "o", "actT") for b in Bk[k]]
          + [Bk["junk"].res, Bk["sq"].res] + [b.res for b in Bk["PP"]])
    program()

    def mksems(n):
        return [es.enter_context(nc.semaphore(f"s{i}")) for i in range(n)]
    block = es.enter_context(nc.Block())
    stats = P.finalize(nc, block, mksems)
    es.close()
    return nc, stats


def _rel_bucket(rel):
    nb, max_exact = 16, 8
    ret = np.where(rel > 0, nb, 0)
    n = np.abs(rel)
    large = max_exact + (np.log(np.maximum(n, 1).astype(np.float32) / max_exact)
                         / np.float32(np.log(128 / max_exact)) * (nb - max_exact)).astype(np.int32)
    large = np.minimum(large, nb - 1)
    return ret + np.where(n < max_exact, n, large)


_CACHE = {}


def kernel(**inputs):
    f32 = lambda a: np.ascontiguousarray(np.asarray(a, dtype=np.float32))
    inp = {k: f32(v) for k, v in inputs.items()}
    if "nc" not in _CACHE:
        _CACHE["nc"] = build_program()
    nc, stats = _CACHE["nc"]
    kk = np.arange(128)[:, None]
    qq = np.arange(128)[None, :]
    bkt_d = _rel_bucket(kk - qq)
    bkt_p = _rel_bucket(kk - 128 - qq)
    rb = inp["rel_bias"]
    bias_diag = np.ascontiguousarray(np.transpose(rb[bkt_d], (2, 0, 1)))
    bias_prev = np.ascontiguousarray(np.transpose(rb[bkt_p], (2, 0, 1)))
    mask_diag = ((kk // 64) <= (qq // 64)).astype(np.float32)
    shared = {
        "bias_diag": bias_diag, "bias_prev": bias_prev, "mask_diag": mask_diag, "rel_bias": rb,
        "w_in": inp["w_in"][0], "w_proj_a": inp["w_proj_a"][0], "w_proj_b": inp["w_proj_b"][0], "w_o": inp["w_o"][0],
        "w_up": inp["w_up"][0], "w_down": inp["w_down"][0],
        "norm1_g": inp["norm1_g"][0], "norm2_g": inp["norm2_g"][0], "final_g": inp["final_g"],
        "lambda_q1": inp["lambda_q1"][0], "lambda_k1": inp["lambda_k1"][0], "lambda_q2": inp["lambda_q2"][0],
        "lambda_k2": inp["lambda_k2"][0], "subln_g": inp["subln_g"][0], "conv_w": inp["conv_w"][0],
        "ffn_conv_w": inp["ffn_conv_w"][0],
    }
    in_maps = []
    for c in range(NCORES):
        b, half = c // 2, c % 2
        m = dict(shared)
        m["x_own"] = np.ascontiguousarray(inp["x_prompt"][b, half * 2048:(half + 1) * 2048])
        m["x_pre"] = np.ascontiguousarray(inp["x_prompt"][b, 0:2048]) if half == 1 else np.zeros((2048, D), np.float32)
        m["x_smp"] = np.ascontiguousarray(inp["x_sample"][2 * c:2 * c + 2].reshape(128, D))
        m["x_halo"] = (np.ascontiguousarray(inp["x_prompt"][b, 2044:2048]) if half == 1 else np.zeros((4, D), np.float32))
        m["premask"] = np.full((128, 1), 1.0 if half == 1 else 0.0, np.float32)
        m["cache_k"] = np.ascontiguousarray(inp["cache_k"][0, 2 * c:2 * c + 2].reshape(2, 1024, 1024))
        m["cache_v"] = np.ascontiguousarray(inp["cache_v"][0, 2 * c:2 * c + 2].reshape(2, 1024, 1024))
        m["st_mix"] = np.ascontiguousarray(inp["state_conv_mix"][0, 2 * c:2 * c + 2])
        m["st_ffn"] = np.ascontiguousarray(inp["state_conv_ffn"][0, 2 * c:2 * c + 2])
        in_maps.append(m)
    ncr = int(os.environ.get("NC_DEBUG", NCORES))
    res = run_bass_kernel_spmd(nc, in_maps[:ncr], core_ids=list(range(ncr)))
    R = list(res.results) + [res.results[0]] * (NCORES - ncr)
    y_prompt = np.zeros((4, 4096, D), np.float32)
    k_prompt = np.zeros((1, 4, 4096, NH, 128), np.float32)
    v_prompt = np.zeros((1, 4, 4096, NH, 128), np.float32)
    cm_prompt = np.zeros((1, 4, 2, 1024), np.float32)
    cf_prompt = np.zeros((1, 4, 2, 2 * DFF), np.float32)
    y_sample = np.zeros((16, 64, D), np.float32)
    k_sample = np.zeros((1, 16, 64, NH, 128), np.float32)
    v_sample = np.zeros((1, 16, 64, NH, 128), np.float32)
    cm_sample = np.zeros((1, 16, 2, 1024), np.float32)
    cf_sample = np.zeros((1, 16, 2, 2 * DFF), np.float32)
    for c in range(NCORES):
        b, half = c // 2, c % 2
        r = R[c]
        sl = slice(half * 2048, (half + 1) * 2048)
        y_prompt[b, sl] = r["y_own"]
        k_prompt[0, b, sl] = r["k_own"].reshape(2048, NH, 128)
        v_prompt[0, b, sl] = r["v_own"].reshape(2048, NH, 128)
        if half == 1:
            cm_prompt[0, b] = r["cm_own"][0]
            cf_prompt[0, b] = r["cf_own"][0]
        y_sample[2 * c:2 * c + 2] = r["y_smp"].reshape(2, 64, D)
        k_sample[0, 2 * c:2 * c + 2] = r["k_smp"].reshape(2, 64, NH, 128)
        v_sample[0, 2 * c:2 * c + 2] = r["v_smp"].reshape(2, 64, NH, 128)
        cm_sample[0, 2 * c:2 * c + 2] = r["cm_smp"]
        cf_sample[0, 2 * c:2 * c + 2] = r["cf_smp"]
    return (y_prompt, y_sample, k_prompt, v_prompt, cm_prompt, cf_prompt,
            k_sample, v_sample, cm_sample, cf_sample)
```
